# Optimizing a Trainium2 kernel written in Bass

```python
import jax, jax.numpy as jnp
from jax import lax
import numpy as np

D_MODEL = 1024
BATCH = 8
SEQ = 4096
DEPTH = 1

CHUNK = 64
Q_BLOCK = 128
EPS = 1e-6
DN_HEADS = 8
DN_DK = 128
DN_DV = 128
DN_CONV = 4
MLA_HEADS = 8
MLA_NOPE = 128
MLA_ROPE = 64
MLA_DV = 128
MLA_KV_RANK = 256
ROPE_THETA = 10000.0
D_FF = 2816
FFN_CONV = 3

DN_QK = DN_HEADS * DN_DK
DN_V = DN_HEADS * DN_DV
MLA_QD = MLA_HEADS * (MLA_NOPE + MLA_ROPE)
MLA_VD = MLA_HEADS * MLA_DV
IN_SPLITS = (DN_QK, DN_QK, DN_V, DN_V, DN_HEADS, DN_HEADS, MLA_QD, MLA_KV_RANK, MLA_ROPE, D_MODEL, D_MODEL)
D_IN = DN_QK * 2 + DN_V * 2 + DN_HEADS * 2 + MLA_QD + MLA_KV_RANK + MLA_ROPE + 2 * D_MODEL

kernel_name = "hybrid_gdn_mla_convffn_adaln_block"


def rms_norm(x, w):
    xf = x.astype(jnp.float32)
    y = xf * lax.rsqrt(jnp.mean(xf * xf, axis=-1, keepdims=True) + EPS)
    return (y * w.astype(jnp.float32)).astype(x.dtype)


def l2_norm(x):
    xf = x.astype(jnp.float32)
    return xf * lax.rsqrt(jnp.sum(xf * xf, axis=-1, keepdims=True) + EPS)


def causal_dwconv(x, w, b=None):
    k = w.shape[0]
    y = lax.conv_general_dilated(x, w[:, None, :].astype(x.dtype), window_strides=(1,),
                                 padding=((k - 1, 0),),
                                 dimension_numbers=('NWC', 'WIO', 'NWC'),
                                 feature_group_count=x.shape[-1])
    if b is not None:
        y = y + b
    return y


def apply_rope(x, pos):
    half = x.shape[-1] // 2
    inv = ROPE_THETA ** (-jnp.arange(half, dtype=jnp.float32) / half)
    ang = pos.astype(jnp.float32)[..., None] * inv
    cos = jnp.cos(ang)[:, :, None, :]
    sin = jnp.sin(ang)[:, :, None, :]
    xf = x.astype(jnp.float32)
    x1, x2 = xf[..., :half], xf[..., half:]
    return jnp.concatenate([x1 * cos - x2 * sin, x2 * cos + x1 * sin], axis=-1).astype(x.dtype)


def gated_delta_rule(q, k, v, g, beta):
    out_dtype = v.dtype
    B, S, H, dk = q.shape
    dv = v.shape[-1]
    n = S // CHUNK
    f32 = jnp.float32

    def chunks(t):
        return t.astype(f32).reshape(B, n, CHUNK, H, t.shape[-1]).transpose(0, 3, 1, 2, 4)

    def chunks_s(t):
        return t.astype(f32).reshape(B, n, CHUNK, H).transpose(0, 3, 1, 2)

    qc = chunks(q) * (dk ** -0.5)
    kc = chunks(k)
    vc = chunks(v)
    bc = chunks_s(beta)
    gc = jnp.cumsum(chunks_s(g), axis=-1)
    tri = jnp.tril(jnp.ones((CHUNK, CHUNK), dtype=bool))
    strict = jnp.tril(jnp.ones((CHUNK, CHUNK), dtype=bool), -1)
    decay = jnp.exp(jnp.where(tri, gc[..., :, None] - gc[..., None, :], -jnp.inf))
    k_beta = kc * bc[..., None]
    v_beta = vc * bc[..., None]
    a_mat = jnp.where(strict, jnp.einsum('bhncd,bhnjd->bhncj', k_beta, kc) * decay, 0.0)
    eye = jnp.eye(CHUNK, dtype=f32)
    rhs = jnp.concatenate([v_beta, k_beta * jnp.exp(gc)[..., None]], axis=-1)
    sol = lax.linalg.triangular_solve(eye + a_mat, rhs, left_side=True, lower=True, unit_diagonal=True)
    u, w = sol[..., :dv], sol[..., dv:]
    intra = jnp.where(tri, jnp.einsum('bhncd,bhnjd->bhncj', qc, kc) * decay, 0.0)
    q_e = qc * jnp.exp(gc)[..., None]
    k_dec = kc * jnp.exp(gc[..., -1:] - gc)[..., None]
    chunk_dec = jnp.exp(gc[..., -1])

    def to_n(t):
        return jnp.moveaxis(t, 2, 0)

    xs = (to_n(q_e), to_n(intra), to_n(u), to_n(w), to_n(k_dec), jnp.moveaxis(chunk_dec, 2, 0))

    def step(state, inp):
        qe_i, att_i, u_i, w_i, kd_i, dec_i = inp
        v_new = u_i - jnp.einsum('bhcd,bhde->bhce', w_i, state)
        o_i = jnp.einsum('bhcd,bhde->bhce', qe_i, state) + jnp.einsum('bhcj,bhje->bhce', att_i, v_new)
        state = state * dec_i[..., None, None] + jnp.einsum('bhcd,bhce->bhde', kd_i, v_new)
        return state, o_i

    s0 = jnp.zeros((B, H, dk, dv), f32)
    _, o = lax.scan(step, s0, xs)
    o = o.transpose(1, 0, 3, 2, 4).reshape(B, S, H, dv)
    return o.astype(out_dtype)


def mla_attention(q, c_kv, k_rope, pos, kv_norm_w, w_uk, w_uv, q_norm_w, k_norm_w):
    B, S, _ = q.shape
    q = q.reshape(B, S, MLA_HEADS, MLA_NOPE + MLA_ROPE)
    ckv = rms_norm(c_kv, kv_norm_w)
    k_nope = (ckv @ w_uk).reshape(B, S, MLA_HEADS, MLA_NOPE)
    v = (ckv @ w_uv).reshape(B, S, MLA_HEADS, MLA_DV)
    k = jnp.concatenate([k_nope, jnp.broadcast_to(k_rope[:, :, None, :], (B, S, MLA_HEADS, MLA_ROPE))], axis=-1)
    q = rms_norm(q, q_norm_w)
    k = rms_norm(k, k_norm_w)
    q = jnp.concatenate([q[..., :MLA_NOPE], apply_rope(q[..., MLA_NOPE:], pos)], axis=-1)
    k = jnp.concatenate([k[..., :MLA_NOPE], apply_rope(k[..., MLA_NOPE:], pos)], axis=-1)
    scale = (MLA_NOPE + MLA_ROPE) ** -0.5
    chunk_id = jnp.arange(S) // CHUNK
    outs = []
    for s0 in range(0, S, Q_BLOCK):
        s1 = s0 + Q_BLOCK
        kb, vb = k[:, :s1], v[:, :s1]
        sc = jnp.einsum('bqhd,bkhd->bhqk', q[:, s0:s1], kb, preferred_element_type=jnp.float32) * scale
        mask = chunk_id[s0:s1, None] >= chunk_id[None, :s1]
        sc = jnp.where(mask, sc, -jnp.inf)
        p = jax.nn.softmax(sc, axis=-1).astype(v.dtype)
        outs.append(jnp.einsum('bhqk,bkhd->bqhd', p, vb))
    return jnp.concatenate(outs, axis=1)


def setup_inputs(seed: int = 0) -> dict:
    key = jax.random.key(seed)
    ks = jax.random.split(key, 32)
    L = DEPTH
    f32 = jnp.float32

    def w(k, shape, fan_in, mult=1.0):
        return jax.random.normal(k, shape, f32) * (mult * fan_in ** -0.5)

    def gain(k, shape):
        return 1.0 + 0.02 * jax.random.normal(k, shape, f32)

    x = jax.random.normal(ks[0], (BATCH, SEQ, D_MODEL), f32)
    c = jax.random.normal(ks[1], (BATCH, D_MODEL), f32)
    offset = jax.random.randint(ks[2], (BATCH, 1), 0, 8192, dtype=jnp.int32)
    positions = (offset + jnp.arange(SEQ, dtype=jnp.int32)[None, :]).astype(jnp.int32)
    a_init = jax.random.uniform(ks[3], (L, DN_HEADS), f32, 1.0, 16.0)
    dt = jnp.exp(jax.random.uniform(ks[4], (L, DN_HEADS), f32, np.log(1e-3), np.log(1e-1)))
    dt_bias = dt + jnp.log(-jnp.expm1(-dt))
    return {
        "x": x,
        "c": c,
        "positions": positions,
        "w_ada": w(ks[5], (L, D_MODEL, 6 * D_MODEL), D_MODEL, 0.5),
        "b_ada": 0.02 * jax.random.normal(ks[6], (L, 6 * D_MODEL), f32),
        "norm1_w": gain(ks[7], (L, D_MODEL)),
        "w_in": w(ks[8], (L, D_MODEL, D_IN), D_MODEL),
        "dn_conv_w": w(ks[9], (L, DN_CONV, 2 * DN_QK + DN_V), DN_CONV),
        "dn_a_log": jnp.log(a_init),
        "dn_dt_bias": dt_bias,
        "dn_norm_w": gain(ks[10], (L, DN_DV)),
        "mla_kv_norm_w": gain(ks[11], (L, MLA_KV_RANK)),
        "mla_w_uk": w(ks[12], (L, MLA_KV_RANK, MLA_HEADS * MLA_NOPE), MLA_KV_RANK),
        "mla_w_uv": w(ks[13], (L, MLA_KV_RANK, MLA_VD), MLA_KV_RANK),
        "mla_q_norm_w": gain(ks[14], (L, MLA_NOPE + MLA_ROPE)),
        "mla_k_norm_w": gain(ks[15], (L, MLA_NOPE + MLA_ROPE)),
        "w_out_dn": w(ks[16], (L, DN_V, D_MODEL), DN_V),
        "w_out_mla": w(ks[17], (L, MLA_VD, D_MODEL), MLA_VD),
        "w_o": w(ks[18], (L, D_MODEL, D_MODEL), D_MODEL),
        "norm2_w": gain(ks[19], (L, D_MODEL)),
        "ffn_w_up": w(ks[20], (L, D_MODEL, 2 * D_FF), D_MODEL),
        "ffn_conv_w": w(ks[21], (L, FFN_CONV, D_FF), FFN_CONV),
        "ffn_conv_b": 0.02 * jax.random.normal(ks[22], (L, D_FF), f32),
        "ffn_w_down": w(ks[23], (L, D_FF, D_MODEL), D_FF),
    }


def reference(x, c, positions, w_ada, b_ada, norm1_w, w_in, dn_conv_w, dn_a_log, dn_dt_bias,
              dn_norm_w, mla_kv_norm_w, mla_w_uk, mla_w_uv, mla_q_norm_w, mla_k_norm_w,
              w_out_dn, w_out_mla, w_o, norm2_w, ffn_w_up, ffn_conv_w, ffn_conv_b, ffn_w_down):
    B, S, D = x.shape
    split_at = [int(o) for o in np.cumsum(IN_SPLITS)[:-1]]
    for l in range(DEPTH):
        mod = jax.nn.silu(c) @ w_ada[l] + b_ada[l]
        shift1, scale1, gate1, shift2, scale2, gate2 = jnp.split(mod[:, None, :], 6, axis=-1)

        h = rms_norm(x, norm1_w[l]) * (1 + scale1) + shift1
        proj = h @ w_in[l]
        (dn_q, dn_k, dn_v, dn_z, dn_alpha, dn_beta,
         mla_q, mla_ckv, mla_kr, gate_dn, gate_mla) = jnp.split(proj, split_at, axis=-1)

        qkv = jax.nn.silu(causal_dwconv(jnp.concatenate([dn_q, dn_k, dn_v], axis=-1), dn_conv_w[l]))
        q_a = l2_norm(qkv[..., :DN_QK].reshape(B, S, DN_HEADS, DN_DK))
        k_a = l2_norm(qkv[..., DN_QK:2 * DN_QK].reshape(B, S, DN_HEADS, DN_DK))
        v_a = qkv[..., 2 * DN_QK:].reshape(B, S, DN_HEADS, DN_DV)
        g_a = -jnp.exp(dn_a_log[l].astype(jnp.float32)) * jax.nn.softplus(
            dn_alpha.astype(jnp.float32) + dn_dt_bias[l].astype(jnp.float32))
        beta_a = jax.nn.sigmoid(dn_beta.astype(jnp.float32))
        o_a = gated_delta_rule(q_a, k_a, v_a, g_a, beta_a)
        o_a = rms_norm(o_a, dn_norm_w[l]) * jax.nn.silu(dn_z.reshape(B, S, DN_HEADS, DN_DV))
        y_a = o_a.reshape(B, S, DN_V) @ w_out_dn[l]

        o_b = mla_attention(mla_q, mla_ckv, mla_kr, positions, mla_kv_norm_w[l], mla_w_uk[l],
                            mla_w_uv[l], mla_q_norm_w[l], mla_k_norm_w[l])
        y_b = o_b.reshape(B, S, MLA_VD) @ w_out_mla[l]

        mix = jax.nn.sigmoid(gate_dn) * y_a + jax.nn.sigmoid(gate_mla) * y_b
        x = x + gate1 * (mix @ w_o[l])

        h2 = rms_norm(x, norm2_w[l]) * (1 + scale2) + shift2
        up = h2 @ ffn_w_up[l]
        a_path, v_path = up[..., :D_FF], up[..., D_FF:]
        a_path = jax.nn.gelu(causal_dwconv(a_path, ffn_conv_w[l], ffn_conv_b[l]), approximate=False)
        x = x + gate2 * ((a_path * v_path) @ ffn_w_down[l])
    return x
```

```python
import numpy as np
from contextlib import ExitStack
import concourse.bass as bass
import concourse.mybir as mybir
from concourse.bass_utils import run_bass_kernel_spmd

F32 = mybir.dt.float32
BF16 = mybir.dt.bfloat16
I32 = mybir.dt.int32
AF = mybir.ActivationFunctionType
ALU = mybir.AluOpType

D = 1024
SEQ = 4096
NH = 8
D_IN = 8016
D_FF = 2816
EPS = 1e-6
C_Q, C_K, C_V, C_Z, C_AL, C_BE, C_MQ, C_CKV, C_KR, C_GD, C_GM = 0, 1024, 2048, 3072, 4096, 4104, 4112, 5648, 5904, 5968, 6992

ENGS = ("pe", "act", "dve", "pool", "sp")
ENGMAP = {"pe": "tensor", "act": "scalar", "dve": "vector", "pool": "gpsimd", "sp": "sync"}


class Buf:
    __slots__ = ("name", "last_w", "readers", "psum")

    def __init__(self, name="", psum=False):
        self.name = name
        self.psum = psum
        self.last_w = None
        self.readers = []


class Op:
    __slots__ = ("eng", "fn", "idx", "eidx", "deps", "signal", "sigcount", "chan", "chan_count", "is_dma")


class Sched:
    def __init__(self, nc, stack):
        self.nc = nc
        self.stack = stack
        self.esem = {e: stack.enter_context(nc.semaphore("s_" + e)) for e in ENGS}
        self.csem = {}
        self.sig_base = {e: 0 for e in ENGS}
        self.chan_counts = {}
        self.bufs = []
        self.rr_serial = True
        self._reset()
        self.nphase = 0

    def _reset(self):
        self.ops = []
        self.eng_ops = {e: [] for e in ENGS}
        for b in self.bufs:
            b.last_w = None
            b.readers = []
        self.bufs = []

    def buf(self, name="", psum=False):
        b = Buf(name, psum)
        self.bufs.append(b)
        return b

    def _add(self, eng, fn, reads, writes, chan=None):
        op = Op()
        op.eng = eng
        op.fn = fn
        op.idx = len(self.ops)
        op.eidx = len(self.eng_ops[eng])
        op.deps = set()
        op.signal = False
        op.sigcount = 0
        op.chan = chan
        op.is_dma = chan is not None
        op.chan_count = 0
        if chan is not None:
            if chan not in self.csem:
                self.csem[chan] = self.stack.enter_context(self.nc.semaphore("c_" + str(chan)))
            self.chan_counts[chan] = self.chan_counts.get(chan, 0) + 1
            op.chan_count = self.chan_counts[chan]
        for b in reads:
            if b.last_w is not None:
                op.deps.add(b.last_w)
            if b.psum:
                for r in b.readers:
                    if self.ops[r].eng != eng:
                        op.deps.add(r)
        for b in writes:
            if b.last_w is not None:
                op.deps.add(b.last_w)
            for r in b.readers:
                op.deps.add(r)
        for b in reads:
            b.readers.append(op.idx)
        for b in writes:
            b.last_w = op.idx
            b.readers = []
        op.deps.discard(op.idx)
        self.ops.append(op)
        self.eng_ops[eng].append(op)
        return op

    def op(self, eng, fn, reads=(), writes=()):
        return self._add(eng, fn, list(reads), list(writes), None)

    def dma(self, eng, chan, fn, reads=(), writes=()):
        return self._add(eng, fn, list(reads), list(writes), chan)

    def flush(self, final=False):
        nc = self.nc
        ops = self.ops
        if final:
            fin = self._add("sp", None, [], [])
            lastper = {}
            for op in ops:
                if op.is_dma:
                    lastper[op.chan] = op.idx
            fin.deps = set(lastper.values())
        for op in ops:
            keep = set()
            for d in op.deps:
                p = ops[d]
                if p.is_dma or op.is_dma:
                    keep.add(d)
                    continue
                if p.eng == op.eng:
                    if p.eng == "pe":
                        continue
                    if p.eng != "pool" and op.eidx - p.eidx > 3:
                        continue
                keep.add(d)
            op.deps = keep
            for d in keep:
                if not ops[d].is_dma:
                    ops[d].signal = True
        for e in ENGS:
            for op in reversed(self.eng_ops[e]):
                if not op.is_dma and op.fn is not None:
                    op.signal = True
                    break
        for e in ENGS:
            c = self.sig_base[e]
            for op in self.eng_ops[e]:
                if op.signal:
                    c += 1
                op.sigcount = c
        bar_e = dict(self.sig_base)
        bar_c = {}
        for ch, cnt in self.chan_counts.items():
            n_this = sum(1 for op in ops if op.is_dma and op.chan == ch)
            bar_c[ch] = 16 * (cnt - n_this)
        with nc.Block() as block:
            for e in ENGS:
                eops = self.eng_ops[e]
                if not eops:
                    continue

                def body(eng, eops=eops, e=e):
                    waited = {}
                    if self.nphase > 0:
                        for e2 in ENGS:
                            if e2 != e and bar_e[e2] > 0:
                                eng.wait_ge(self.esem[e2], bar_e[e2])
                                waited[("e", e2)] = bar_e[e2]
                        for ch, v in bar_c.items():
                            if v > 0:
                                eng.wait_ge(self.csem[ch], v)
                                waited[("c", ch)] = v
                    for op in eops:
                        need = {}
                        for d in op.deps:
                            p = ops[d]
                            if p.is_dma:
                                key = ("c", p.chan)
                                val = 16 * p.chan_count
                            else:
                                key = ("e", p.eng)
                                val = p.sigcount
                            if need.get(key, 0) < val:
                                need[key] = val
                        for key, val in need.items():
                            if waited.get(key, 0) >= val:
                                continue
                            waited[key] = val
                            sem = self.csem[key[1]] if key[0] == "c" else self.esem[key[1]]
                            eng.wait_ge(sem, val)
                        if op.fn is None:
                            continue
                        inst = op.fn(eng)
                        if op.is_dma:
                            inst.then_inc(self.csem[op.chan], 16)
                        elif op.signal:
                            inst.then_inc(self.esem[e], 1)

                getattr(block, ENGMAP[e])(body)
        for e in ENGS:
            if self.eng_ops[e]:
                self.sig_base[e] = self.eng_ops[e][-1].sigcount
        self.nphase += 1
        self._reset()


class Rot:
    def __init__(self, S, items, psum=False):
        self.items = [(t, S.buf(psum=psum)) for t in items]
        self.i = 0

    def next(self):
        it = self.items[self.i % len(self.items)]
        self.i += 1
        return it


class Ctx:
    pass


TWO_PI = 2.0 * np.pi
CW1 = 6.28125
CW2 = float(np.float32(TWO_PI - CW1))
MAGIC = 12582912.0


ALL_PHASES = ("norm1", "mla_s", "mla_h", "dn_p", "dn_r", "outp", "ffn")


def build(NB=8, debug=(), phases=ALL_PHASES):
    C = Ctx()
    S_ = NB * 512
    C.NB, C.S_, C.NT = NB, S_, NB * 4
    nc = bass.Bass("TRN2", target_bir_lowering=False)
    C.nc = nc

    def din(name, shape, dt=F32):
        return nc.dram_tensor(name, list(shape), dt, kind="ExternalInput").ap()

    def dscr(name, shape, dt):
        return nc.dram_tensor(name, list(shape), dt, kind="Internal").ap()

    C.x = din("x", [S_, D])
    C.cT = din("cT", [128, 8])
    C.pos = din("pos", [1, S_], I32)
    C.w_ada = din("w_ada", [D, 6 * D])
    C.b_ada_pk = din("b_ada_pk", [128, 48])
    C.b_ada = din("b_ada", [1, 6 * D])
    C.norm1_pk = din("norm1_pk", [128, 8])
    C.norm2_pk = din("norm2_pk", [128, 8])
    C.consts = din("consts", [128, 6, 128])
    C.w_in = din("w_in", [D, D_IN])
    C.w_in_rot = din("w_in_rot", [D, 576])
    C.mlapk = din("mlapk", [128, 8])
    C.w_uk = din("w_uk", [256, 1024])
    C.w_uv = din("w_uv", [256, 1024])
    C.conv_pk = din("conv_pk", [128, 24, 4])
    C.ab_row = din("ab_row", [1, 16])
    C.dnw_pk = din("dnw_pk", [128, 1])
    C.w_out_dn = din("w_out_dn", [1024, 1024])
    C.w_out_mla = din("w_out_mla", [1024, 1024])
    C.w_o = din("w_o", [1024, 1024])
    C.w_up = din("w_up", [1024, 2 * D_FF])
    C.w_down = din("w_down", [D_FF, 1024])
    C.fconv_pk = din("fconv_pk", [128, 22, 3])
    C.fconvb_pk = din("fconvb_pk", [128, 22])
    C.out = nc.dram_tensor("out", [S_, D], F32, kind="ExternalOutput").ap()
    dbg = {}
    for name, shape, dt in debug:
        dbg[name] = nc.dram_tensor(name, list(shape), dt, kind="ExternalOutput").ap()
    C.dbg = dbg
    C.hT_d = dbg["hT"] if "hT" in dbg else dscr("hT_d", [8, 128, S_], BF16)
    C.oTb_d = dbg["oTb"] if "oTb" in dbg else dscr("oTb_d", [8, 128, S_], BF16)
    C.oTa_d = dbg["oTa"] if "oTa" in dbg else dscr("oTa_d", [8, 128, S_], BF16)
    C.qk_d = dbg["qk"] if "qk" in dbg else dscr("qk_d", [16, 128, S_], BF16)
    C.v_d = dbg["v"] if "v" in dbg else dscr("v_d", [8, 128, S_], BF16)
    C.sz_d = dbg["sz"] if "sz" in dbg else dscr("sz_d", [8, 128, S_], BF16)

    with ExitStack() as gst:
        S = Sched(nc, gst)
        C.S = S

        cnt = [0]

        def sb(name, shape, dt=F32, st=gst):
            cnt[0] += 1
            return st.enter_context(nc.sbuf_tensor("sb%d_%s" % (cnt[0], name), list(shape), dt))

        def ps(name, shape, dt=F32, st=gst):
            cnt[0] += 1
            return st.enter_context(nc.psum_tensor("ps%d_%s" % (cnt[0], name), list(shape), dt))

        C.sb, C.ps = sb, ps
        C.cst = sb("cst", [128, 6, 128])
        C.idtb = sb("idtb", [128, 128], BF16)
        C.idt = C.cst[:, 0, :]
        C.ones = C.cst[:, 1, :]
        C.dsum = C.cst[:, 2, :]
        C.modpk = sb("modpk", [128, 4, 8])
        C.s1 = sb("s1", [128, 8])
        C.s2 = sb("s2", [128, 8])
        C.gateB = sb("gateB", [128, 2, D])

        phase0(C)
        if "norm1" in phases:
            norm_phase(C, C.x, C.hT_d, C.s1, 0)
        if "mla_s" in phases:
            with ExitStack() as mst:
                C.ckvnT = sb("ckvnT", [128, 2, S_], BF16, mst)
                C.krr2 = sb("krr2", [128, S_], BF16, mst)
                C.cs = sb("cs", [128, S_], F32, mst)
                C.sskr = sb("sskr", [128, C.NT], F32, mst)
                C.mpk = sb("mpk", [128, 8], F32, mst)
                C.wqn = sb("wqrr", [128, 4], F32, mst)
                mla_shared_phase(C)
                if "mla_h" in phases:
                    mla_heads_phase(C)
        if "dn_p" in phases:
            with ExitStack() as mst:
                C.gates = sb("gates", [128, C.NT, 16], F32, mst)
                dn_proj_phase(C)
                if "dn_r" in phases:
                    dn_rule_phase(C)
        if "outp" in phases:
            outproj_phase(C)
        if "ffn" in phases:
            norm_phase(C, C.out, C.hT_d, C.s2, 2)
            ffn_phase(C)
        S.flush(final=True)
    return nc


def phase0(C):
    nc, S, sb, ps = C.nc, C.S, C.sb, C.ps
    modpk, s1, s2, gateB, dbg = C.modpk, C.s1, C.s2, C.gateB, C.dbg
    with ExitStack() as st:
        ct = sb("ct", [128, 8], F32, st)
        sc = sb("sc", [128, 8], BF16, st)
        scB = sb("scB", [128, 8, 128], BF16, st)
        bpk = sb("bpk", [128, 48], F32, st)
        n1 = sb("n1", [128, 8], F32, st)
        n2 = sb("n2", [128, 8], F32, st)
        brow = sb("brow", [128, 2, D], F32, st)
        wb = [sb("wadab%d" % i, [128, 8, 1024], BF16, st) for i in range(2)]
        pm = ps("pm", [128, 512], F32, st)
        pg = [ps("pg%d" % i, [128, 512], F32, st) for i in range(2)]
        b_id, b_ct, b_sc, b_scB, b_bpk, b_n, b_brow, b_mod, b_s, b_gate = [S.buf() for _ in range(10)]
        b_pm = S.buf(psum=True)
        b_wb = [S.buf(), S.buf()]
        b_pg = [S.buf(psum=True), S.buf(psum=True)]
        S.dma("sp", "ld0", lambda e: e.dma_start(out=C.cst[:], in_=C.consts), writes=[b_id])
        S.dma("sp", "ld1", lambda e: e.dma_start(out=ct[:], in_=C.cT), writes=[b_ct])
        S.dma("sp", "ld2", lambda e: e.dma_start(out=bpk[:], in_=C.b_ada_pk), writes=[b_bpk])
        S.dma("sp", "ld3", lambda e: e.dma_start(out=n1[:], in_=C.norm1_pk), writes=[b_n])
        S.dma("sp", "ld4", lambda e: e.dma_start(out=n2[:], in_=C.norm2_pk), writes=[b_n])
        S.dma("sp", "ld5", lambda e: e.dma_start(out=brow[:, 0, :], in_=C.b_ada[:, 2 * D:3 * D].partition_broadcast(128)), writes=[b_brow])
        S.dma("sp", "ld6", lambda e: e.dma_start(out=brow[:, 1, :], in_=C.b_ada[:, 5 * D:6 * D].partition_broadcast(128)), writes=[b_brow])
        S.op("dve", lambda e: e.tensor_copy(C.idtb[:], C.idt), reads=[b_id], writes=[S.buf()])
        S.op("act", lambda e: e.activation(out=sc[:], in_=ct[:], func=AF.Silu), reads=[b_ct], writes=[b_sc])
        S.op("dve", lambda e: e.tensor_copy(scB[:], sc[:].unsqueeze(2).to_broadcast([128, 8, 128])), reads=[b_sc], writes=[b_scB])
        w_ada_v = C.w_ada.rearrange("(k p) n -> p k n", p=128)
        for g in range(6):
            wt, bw = wb[g % 2], b_wb[g % 2]
            S.dma("pool", "wl%d" % (g % 2), lambda e, wt=wt, g=g: e.dma_start(out=wt[:], in_=w_ada_v[:, :, g * 1024:(g + 1) * 1024]), writes=[bw])
            if g in (0, 1, 3, 4):
                gi = {0: 0, 1: 1, 3: 2, 4: 3}[g]
                for j in range(8):
                    for k in range(8):
                        S.op("pe", lambda e, wt=wt, j=j, k=k: e.matmul(pm[:, j:j + 1], wt[:, k, j * 128:(j + 1) * 128], sc[:, k:k + 1], start=(k == 0), stop=(k == 7)),
                             reads=[bw, b_sc], writes=[b_pm])
                S.op("dve", lambda e, gi=gi, g=g: e.tensor_tensor(out=modpk[:, gi, :], in0=pm[:, 0:8], in1=bpk[:, g * 8:(g + 1) * 8], op=ALU.add), reads=[b_pm, b_bpk], writes=[b_mod])
            else:
                gi = 0 if g == 2 else 1
                for hf in range(2):
                    pgt, bpg = pg[hf], b_pg[hf]
                    for k in range(8):
                        S.op("pe", lambda e, wt=wt, pgt=pgt, hf=hf, k=k: e.matmul(pgt[:], scB[:, k, :], wt[:, k, hf * 512:(hf + 1) * 512], start=(k == 0), stop=(k == 7)),
                             reads=[bw, b_scB], writes=[bpg])
                    S.op("dve", lambda e, pgt=pgt, gi=gi, hf=hf: e.tensor_tensor(out=gateB[:, gi, hf * 512:(hf + 1) * 512], in0=pgt[:], in1=brow[:, gi, hf * 512:(hf + 1) * 512], op=ALU.add),
                         reads=[bpg, b_brow], writes=[b_gate])
        S.op("dve", lambda e: e.scalar_tensor_tensor(out=s1[:], in0=modpk[:, 1, :], scalar=1.0, in1=n1[:], op0=ALU.add, op1=ALU.mult), reads=[b_mod, b_n], writes=[b_s])
        S.op("dve", lambda e: e.scalar_tensor_tensor(out=s2[:], in0=modpk[:, 3, :], scalar=1.0, in1=n2[:], op0=ALU.add, op1=ALU.mult), reads=[b_mod, b_n], writes=[b_s])
        if "mod" in dbg:
            S.dma("sp", "st0", lambda e: e.dma_start(out=dbg["mod"][:, 0:32], in_=modpk[:].rearrange("p a b -> p (a b)")), reads=[b_mod])
            S.dma("sp", "st1", lambda e: e.dma_start(out=dbg["mod"][:, 32:40], in_=s1[:]), reads=[b_s])
            S.dma("sp", "st2", lambda e: e.dma_start(out=dbg["gate"], in_=gateB[0:1, :, :].rearrange("p a b -> p (a b)")), reads=[b_gate])
        S.flush()


def norm_phase(C, xsrc, hT_d, svec, shift_idx):
    nc, S, sb, ps = C.nc, C.S, C.sb, C.ps
    idt, modpk = C.idt, C.modpk
    with ExitStack() as st:
        xt = Rot(S, [sb("xt%d" % i, [128, D], F32, st) for i in range(3)])
        xn = Rot(S, [sb("xn%d" % i, [128, D], F32, st) for i in range(3)])
        junk = sb("junk", [128, D], BF16, st)
        b_junk = S.buf()
        ssq = Rot(S, [sb("ssq%d" % i, [128, 2], F32, st) for i in range(3)])
        hTb = Rot(S, [sb("hTb%d" % i, [128, 8, 512], BF16, st) for i in range(2)])
        hb2_bufs = [S.buf(), S.buf()]
        pTa = Rot(S, [ps("pTa%d" % i, [128, 4, 128], F32, st) for i in range(2)], psum=True)
        pTb = Rot(S, [ps("pTb%d" % i, [128, 4, 128], F32, st) for i in range(2)], psum=True)
        b_x = S.buf()
        state = {"cur": None}

        def stage1(t):
            xt_t, b_xt = xt.next()
            S.dma("sp", "xl%d" % (t % 3), lambda e: e.dma_start(out=xt_t[:], in_=xsrc[t * 128:(t + 1) * 128, :]), reads=[b_x], writes=[b_xt])
            sq, b_sq = ssq.next()
            S.op("act", lambda e: e.activation(out=junk[:], in_=xt_t[:], func=AF.Square, accum_out=sq[:, 0:1]), reads=[b_xt], writes=[b_junk, b_sq])
            S.op("act", lambda e: e.activation(out=sq[:, 1:2], in_=sq[:, 0:1], func=AF.Ln, scale=1.0 / D, bias=EPS), reads=[b_sq], writes=[b_sq])
            S.op("act", lambda e: e.activation(out=sq[:, 1:2], in_=sq[:, 1:2], func=AF.Exp, scale=-0.5), reads=[b_sq], writes=[b_sq])
            xn_t, b_xn = xn.next()
            S.op("dve", lambda e: e.tensor_scalar(xn_t[:], xt_t[:], sq[:, 1:2], None, ALU.mult), reads=[b_xt, b_sq], writes=[b_xn])
            return xn_t, b_xn

        def stage2(t, xn_t, b_xn):
            pa, b_pa = pTa.next()
            pb, b_pb = pTb.next()
            for k in range(8):
                dst, bd = (pa, b_pa) if k < 4 else (pb, b_pb)
                S.op("pe", lambda e, k=k, dst=dst: e.transpose(dst[:, k % 4, :], xn_t[:, k * 128:(k + 1) * 128], idt), reads=[b_xn], writes=[bd])
            if t % 4 == 0:
                state["cur"] = hTb.next() + (hb2_bufs[hTb.i % 2],)
            hb, b_hb, b_hb2 = state["cur"]
            tt = t % 4
            for k in range(8):
                if k < 4:
                    S.op("dve", lambda e, k=k: e.tensor_scalar(hb[:, k, tt * 128:(tt + 1) * 128], pa[:, k, :], svec[:, k:k + 1], modpk[:, shift_idx, k:k + 1], ALU.mult, ALU.add),
                         reads=[b_pa], writes=[b_hb])
                else:
                    S.op("act", lambda e, k=k: e.activation(out=hb[:, k, tt * 128:(tt + 1) * 128], in_=pb[:, k - 4, :], func=AF.Identity, scale=svec[:, k:k + 1], bias=modpk[:, shift_idx, k:k + 1]),
                         reads=[b_pb], writes=[b_hb2])
            if tt == 3:
                blk = t // 4
                S.dma("sp", "hs%d" % (blk % 2), lambda e: e.dma_start(out=hT_d[:, :, blk * 512:(blk + 1) * 512].rearrange("k p t -> p k t"), in_=hb[:]), reads=[b_hb, b_hb2], writes=[S.buf()])

        nxt = stage1(0)
        for t in range(C.NT):
            cur = nxt
            if t + 1 < C.NT:
                nxt = stage1(t + 1)
            stage2(t, *cur)
        S.flush()


def load_hT_block(C, hbuf, b_h, blk, eng="sp"):
    C.S.dma(eng, "hl%d" % (C._hl % 2), lambda e: e.dma_start(out=hbuf[:], in_=C.hT_d[:, :, blk * 512:(blk + 1) * 512].rearrange("k p t -> p k t")), writes=[b_h])
    C._hl += 1


def mla_shared_phase(C):
    nc, S, sb, ps = C.nc, C.S, C.sb, C.ps
    C._hl = 0
    with ExitStack() as st:
        hbs = Rot(S, [sb("hb%d" % i, [128, 8, 512], BF16, st) for i in range(2)])
        wkv = sb("wkv", [128, 8, 384], BF16, st)
        posi = sb("posi", [128, 512], I32, st)
        ang = sb("ang", [128, 512], F32, st)
        kf = sb("kf", [128, 512], F32, st)
        rc = sb("rc", [128, 512], F32, st)
        tmpw = sb("tmpw", [128, 512], F32, st)
        sq = [sb("sqc%d" % i, [128, 512], F32, st) for i in range(2)]
        lnb = sb("lnb", [128, 512], F32, st)
        rstd = sb("rstd", [128, 512], F32, st)
        tk = sb("tk", [128, 512], F32, st)
        sqkr = sb("sqkr", [64, 512], F32, st)
        banks = [ps("pb%d" % i, [128, 512], F32, st) for i in range(7)]
        pc = [banks[0], banks[1]]
        pk, pss, pkr, psk = banks[2], banks[3], banks[4], banks[5]
        b_w, b_mpk, b_wqn, b_posi, b_ang, b_kf, b_rc, b_tmpw, b_ln, b_rstd, b_tk, b_sqkr, b_cs, b_ckv, b_krr, b_sskr = [S.buf() for _ in range(16)]
        b_pk, b_pss, b_pkr, b_psk = [S.buf(psum=True) for _ in range(4)]
        b_pc = [S.buf(psum=True), S.buf(psum=True)]
        b_sq = [S.buf(), S.buf()]
        w_in_v = C.w_in.rearrange("(k p) n -> p k n", p=128)
        w_rot_v = C.w_in_rot.rearrange("(k p) n -> p k n", p=128)
        S.dma("pool", "wl0", lambda e: e.dma_start(out=wkv[:, :, 0:320], in_=w_in_v[:, :, C_CKV:C_CKV + 320]), writes=[b_w])
        S.dma("pool", "wl1", lambda e: e.dma_start(out=wkv[:, :, 320:384], in_=w_rot_v[:, :, 512:576]), writes=[b_w])
        S.dma("sp", "ld1", lambda e: e.dma_start(out=C.mpk[:], in_=C.mlapk), writes=[b_mpk])
        mpk, wqn = C.mpk, C.wqn
        S.op("dve", lambda e: e.tensor_copy(wqn[:, 0:1], mpk[:, 0:1]), reads=[b_mpk], writes=[b_wqn])
        S.op("dve", lambda e: e.tensor_tensor(out=wqn[:, 1:2], in0=mpk[:, 1:2], in1=mpk[:, 4:5], op=ALU.mult), reads=[b_mpk], writes=[b_wqn])
        S.op("dve", lambda e: e.tensor_copy(wqn[:, 2:3], mpk[:, 2:3]), reads=[b_mpk], writes=[b_wqn])
        S.op("dve", lambda e: e.tensor_tensor(out=wqn[:, 3:4], in0=mpk[:, 3:4], in1=mpk[:, 4:5], op=ALU.mult), reads=[b_mpk], writes=[b_wqn])
        for j in range(C.NB):
            hb, b_hb = hbs.next()
            load_hT_block(C, hb, b_hb, j)
            bs = slice(j * 512, (j + 1) * 512)
            S.dma("sp", "ld2", lambda e, bs=bs: e.dma_start(out=posi[:], in_=C.pos[:, bs].partition_broadcast(128)), writes=[b_posi])
            S.op("dve", lambda e: e.tensor_copy(ang[:], posi[:]), reads=[b_posi], writes=[b_ang])
            S.op("dve", lambda e: e.tensor_scalar(ang[:], ang[:], mpk[:, 5:6], None, ALU.mult), reads=[b_ang, b_mpk], writes=[b_ang])
            S.op("dve", lambda e: e.tensor_scalar(kf[:], ang[:], 1.0 / TWO_PI, MAGIC, ALU.mult, ALU.add), reads=[b_ang], writes=[b_kf])
            S.op("dve", lambda e: e.tensor_scalar(kf[:], kf[:], -MAGIC, None, ALU.add), reads=[b_kf], writes=[b_kf])
            S.op("dve", lambda e: e.scalar_tensor_tensor(out=ang[:], in0=kf[:], scalar=-CW1, in1=ang[:], op0=ALU.mult, op1=ALU.add), reads=[b_kf, b_ang], writes=[b_ang])
            S.op("dve", lambda e: e.scalar_tensor_tensor(out=ang[:], in0=kf[:], scalar=-CW2, in1=ang[:], op0=ALU.mult, op1=ALU.add), reads=[b_kf, b_ang], writes=[b_ang])
            S.op("dve", lambda e: e.tensor_scalar(ang[:], ang[:], np.pi, -np.pi, ALU.min, ALU.max), reads=[b_ang], writes=[b_ang])
            S.op("dve", lambda e: e.tensor_scalar(kf[:], ang[:], np.pi / 2, -TWO_PI, ALU.is_gt, ALU.mult), reads=[b_ang], writes=[b_kf])
            S.op("dve", lambda e: e.scalar_tensor_tensor(out=rc[:], in0=ang[:], scalar=np.pi / 2, in1=kf[:], op0=ALU.add, op1=ALU.add), reads=[b_ang, b_kf], writes=[b_rc])
            S.op("dve", lambda e: e.tensor_scalar(rc[:], rc[:], np.pi, -np.pi, ALU.min, ALU.max), reads=[b_rc], writes=[b_rc])
            S.op("act", lambda e, bs=bs: e.activation(out=C.cs[0:64, bs], in_=rc[0:64, :], func=AF.Sin), reads=[b_rc], writes=[b_cs])
            S.op("act", lambda e, bs=bs: e.activation(out=C.cs[64:128, bs], in_=ang[64:128, :], func=AF.Sin), reads=[b_ang], writes=[b_cs])
            for r in range(2):
                for k in range(8):
                    S.op("pe", lambda e, r=r, k=k, hb=hb: e.matmul(pc[r][:], wkv[:, k, r * 128:(r + 1) * 128], hb[:, k, :], start=(k == 0), stop=(k == 7)), reads=[b_w, b_hb], writes=[b_pc[r]])
            for k in range(8):
                S.op("pe", lambda e, k=k, hb=hb: e.matmul(pk[:], wkv[:, k, 256:384], hb[:, k, :], start=(k == 0), stop=(k == 7)), reads=[b_w, b_hb], writes=[b_pk])
            for r in range(2):
                S.op("act", lambda e, r=r: e.activation(out=sq[r][:], in_=pc[r][:], func=AF.Square), reads=[b_pc[r]], writes=[b_sq[r]])
            for r in range(2):
                S.op("pe", lambda e, r=r: e.matmul(pss[:], C.ones, sq[r][:], start=(r == 0), stop=(r == 1)), reads=[b_sq[r]], writes=[b_pss])
            S.op("act", lambda e: e.activation(out=lnb[:], in_=pss[:], func=AF.Ln, scale=1.0 / 256, bias=EPS), reads=[b_pss], writes=[b_ln])
            S.op("act", lambda e: e.activation(out=rstd[:], in_=lnb[:], func=AF.Exp, scale=-0.5), reads=[b_ln], writes=[b_rstd])
            for r in range(2):
                S.op("dve", lambda e, r=r, bs=bs: e.scalar_tensor_tensor(out=C.ckvnT[:, r, bs], in0=pc[r][:], scalar=mpk[:, 6 + r:7 + r], in1=rstd[:], op0=ALU.mult, op1=ALU.mult),
                     reads=[b_pc[r], b_rstd, b_mpk], writes=[b_ckv])
            S.op("dve", lambda e, bs=bs: e.scalar_tensor_tensor(out=tk[:], in0=pk[:], scalar=wqn[:, 3:4], in1=C.cs[:, bs], op0=ALU.mult, op1=ALU.mult), reads=[b_pk, b_cs, b_wqn], writes=[b_tk])
            S.op("pe", lambda e: e.matmul(pkr[:], C.dsum, tk[:], start=True, stop=True), reads=[b_tk], writes=[b_pkr])
            S.op("act", lambda e, bs=bs: e.activation(out=C.krr2[:, bs], in_=pkr[:], func=AF.Copy), reads=[b_pkr], writes=[b_krr])
            S.op("act", lambda e: e.activation(out=sqkr[:], in_=pk[0:64, :], func=AF.Square), reads=[b_pk], writes=[b_sqkr])
            for tt in range(4):
                t = j * 4 + tt
                S.op("pe", lambda e, t=t, tt=tt: e.matmul(psk[:, t:t + 1], sqkr[:, tt * 128:(tt + 1) * 128], C.ones[0:64, 0:1], start=True, stop=True), reads=[b_sqkr], writes=[b_psk])
        S.op("dve", lambda e: e.tensor_copy(C.sskr[:], psk[:, 0:C.NT]), reads=[b_psk], writes=[b_sskr])
        if "ckvnT" in C.dbg:
            S.dma("sp", "st0", lambda e: e.dma_start(out=C.dbg["ckvnT"].rearrange("r p t -> p r t"), in_=C.ckvnT[:]), reads=[b_ckv])
            S.dma("sp", "st1", lambda e: e.dma_start(out=C.dbg["krr2"], in_=C.krr2[:]), reads=[b_krr])
            S.dma("sp", "st2", lambda e: e.dma_start(out=C.dbg["cs"], in_=C.cs[:]), reads=[b_cs])
            S.dma("sp", "st3", lambda e: e.dma_start(out=C.dbg["sskr"], in_=C.sskr[:]), reads=[b_sskr])
        S.flush()


def mla_heads_phase(C):
    nc, S, sb, ps = C.nc, C.S, C.sb, C.ps
    NB = C.NB
    SCALE = 192.0 ** -0.5
    with ExitStack() as st:
        hbs = Rot(S, [sb("hb%d" % i, [128, 8, 512], BF16, st) for i in range(2)])
        wqs = Rot(S, [sb("wq%d" % i, [128, 8, 256], BF16, st) for i in range(2)])
        wuk = sb("wuk", [128, 2, 1024], BF16, st)
        wuv = sb("wuv", [128, 2, 1024], BF16, st)
        QTn = sb("QTn", [128, C.S_], BF16, st)
        QTr = sb("QTr", [128, C.S_], BF16, st)
        KTn2 = [sb("KTn%d" % i, [128, C.S_], BF16, st) for i in range(2)]
        Vh2 = [sb("Vh%d" % i, [128, C.NT, 128], BF16, st) for i in range(2)]
        sk2 = [sb("sk%d" % i, [128, C.NT], F32, st) for i in range(2)]
        sqA = sb("sqA", [128, 512], F32, st)
        sqB = sb("sqB", [64, 512], F32, st)
        sqD = sb("sqD", [128, 512], F32, st)
        lnb = sb("lnb", [128, 512], F32, st)
        lnb2 = sb("lnb2", [128, 512], F32, st)
        rq = sb("rq", [128, 512], F32, st)
        tq = sb("tq", [128, 512], F32, st)
        tmp4 = sb("tmp4", [128, 8], F32, st)
        PTs = Rot(S, [sb("PT%d" % i, [128, 512], BF16, st) for i in range(4)])
        accLs = Rot(S, [sb("accL%d" % i, [128, 512], F32, st) for i in range(2)])
        rl = sb("rl", [128, 512], F32, st)
        oT = Rot(S, [sb("oT%d" % i, [128, 512], BF16, st) for i in range(2)])
        banks = [ps("pb%d" % i, [128, 512], F32, st) for i in range(8)]
        pool3 = Rot(S, banks[0:4], psum=True)
        sts = Rot(S, banks[4:6], psum=True)
        pO, b_pO = banks[6], S.buf(psum=True)
        pL, b_pL = banks[7], S.buf(psum=True)
        psk, b_psk = pL, b_pL
        b_wuk, b_sqA, b_sqB, b_sqD, b_ln, b_ln2, b_rq, b_tq, b_tmp4, b_rl = [S.buf() for _ in range(10)]
        b_Q = [S.buf() for _ in range(NB)]
        b_K = [[S.buf() for _ in range(NB)] for _ in range(2)]
        b_V = [[S.buf() for _ in range(NB)] for _ in range(2)]
        b_pers = S.buf()
        w_in_v = C.w_in.rearrange("(k p) n -> p k n", p=128)
        w_rot_v = C.w_in_rot.rearrange("(k p) n -> p k n", p=128)
        S.dma("pool", "wl0", lambda e: e.dma_start(out=wuk[:], in_=C.w_uk.rearrange("(r p) n -> p r n", p=128)), writes=[b_wuk])
        S.dma("pool", "wl1", lambda e: e.dma_start(out=wuv[:], in_=C.w_uv.rearrange("(r p) n -> p r n", p=128)), writes=[b_wuk])
        wqn = C.wqn
        wq_cur = {}

        def proj(h, j):
            hp = h % 2
            KTn, Vh, sk = KTn2[hp], Vh2[hp], sk2[hp]
            if j == 0:
                wq, b_wq = wqs.next()
                wq_cur[h] = (wq, b_wq)
                c0 = C_MQ + h * 192
                S.dma("pool", "wq%d" % hp, lambda e: e.dma_start(out=wq[:, :, 0:192], in_=w_in_v[:, :, c0:c0 + 192]), writes=[b_wq])
                S.dma("pool", "wr%d" % hp, lambda e: e.dma_start(out=wq[:, :, 192:256], in_=w_rot_v[:, :, h * 64:(h + 1) * 64]), writes=[b_wq])
            wq, b_wq = wq_cur[h]
            hs = slice(h * 128, (h + 1) * 128)
            hb, b_hb = hbs.next()
            load_hT_block(C, hb, b_hb, j)
            bs = slice(j * 512, (j + 1) * 512)
            yield
            pA, b_pA = pool3.next()
            for k in range(8):
                S.op("pe", lambda e, k=k: e.matmul(pA[:], wq[:, k, 0:128], hb[:, k, :], start=(k == 0), stop=(k == 7)), reads=[b_wq, b_hb], writes=[b_pA])
                if k % 4 == 3:
                    yield
            pB, b_pB = pool3.next()
            for k in range(8):
                S.op("pe", lambda e, k=k: e.matmul(pB[:], wq[:, k, 128:256], hb[:, k, :], start=(k == 0), stop=(k == 7)), reads=[b_wq, b_hb], writes=[b_pB])
                if k % 4 == 3:
                    yield
            pD, b_pD = pool3.next()
            for r in range(2):
                S.op("pe", lambda e, r=r: e.matmul(pD[:], wuk[:, r, hs], C.ckvnT[:, r, bs], start=(r == 0), stop=(r == 1)), reads=[b_wuk, b_pers], writes=[b_pD])
            yield
            S.op("act", lambda e: e.activation(out=sqA[:], in_=pA[:], func=AF.Square), reads=[b_pA], writes=[b_sqA])
            yield
            S.op("act", lambda e: e.activation(out=sqB[:], in_=pB[0:64, :], func=AF.Square), reads=[b_pB], writes=[b_sqB])
            yield
            S.op("act", lambda e: e.activation(out=sqD[:], in_=pD[:], func=AF.Square), reads=[b_pD], writes=[b_sqD])
            yield
            S.op("act", lambda e: e.activation(out=KTn[:, bs], in_=pD[:], func=AF.Copy, scale=wqn[:, 2:3]), reads=[b_pD, b_pers], writes=[b_K[hp][j]])
            yield
            pS, b_pS = pool3.next()
            S.op("pe", lambda e: e.matmul(pS[:], C.ones, sqA[:], start=True, stop=False), reads=[b_sqA], writes=[b_pS])
            S.op("pe", lambda e: e.matmul(pS[:], C.ones[0:64, :], sqB[:], start=False, stop=True), reads=[b_sqB], writes=[b_pS])
            for tt in range(4):
                S.op("pe", lambda e, tt=tt: e.matmul(psk[:, tt:tt + 1], sqD[:, tt * 128:(tt + 1) * 128], C.ones[:, 0:1], start=True, stop=True), reads=[b_sqD], writes=[b_psk])
            yield
            S.op("act", lambda e: e.activation(out=lnb[:], in_=pS[:], func=AF.Ln, scale=1.0 / 192, bias=EPS), reads=[b_pS], writes=[b_ln])
            yield
            S.op("act", lambda e: e.activation(out=rq[:], in_=lnb[:], func=AF.Exp, scale=-0.5), reads=[b_ln], writes=[b_rq])
            yield
            S.op("dve", lambda e: e.scalar_tensor_tensor(out=QTn[:, bs], in0=pA[:], scalar=wqn[:, 0:1], in1=rq[:], op0=ALU.mult, op1=ALU.mult), reads=[b_pA, b_rq, b_pers], writes=[b_Q[j]])
            yield
            S.op("dve", lambda e: e.scalar_tensor_tensor(out=tq[:], in0=pB[:], scalar=wqn[:, 1:2], in1=C.cs[:, bs], op0=ALU.mult, op1=ALU.mult), reads=[b_pB, b_pers], writes=[b_tq])
            yield
            S.op("dve", lambda e: e.tensor_tensor(out=QTr[:, bs], in0=tq[:], in1=rq[:], op=ALU.mult), reads=[b_tq, b_rq], writes=[b_Q[j]])
            yield
            S.op("dve", lambda e: e.tensor_tensor(out=tmp4[:, 0:4], in0=psk[:, 0:4], in1=C.sskr[:, 4 * j:4 * j + 4], op=ALU.add), reads=[b_psk, b_pers], writes=[b_tmp4])
            S.op("act", lambda e: e.activation(out=tmp4[:, 4:8], in_=tmp4[:, 0:4], func=AF.Ln, scale=1.0 / 192, bias=EPS), reads=[b_tmp4], writes=[b_tmp4])
            S.op("act", lambda e: e.activation(out=sk[:, 4 * j:4 * j + 4], in_=tmp4[:, 4:8], func=AF.Exp, scale=-0.5, bias=float(np.log(SCALE))), reads=[b_tmp4], writes=[b_K[hp][j]])
            yield
            pV, b_pV = pool3.next()
            for tt in range(4):
                if tt == 2:
                    yield
                for r in range(2):
                    S.op("pe", lambda e, tt=tt, r=r: e.matmul(pV[:, tt * 128:(tt + 1) * 128], C.ckvnT[:, r, j * 512 + tt * 128:j * 512 + (tt + 1) * 128], wuv[:, r, hs], start=(r == 0), stop=(r == 1)),
                         reads=[b_wuk, b_pers], writes=[b_pV])
            yield
            S.op("dve", lambda e: e.tensor_copy(Vh[:, 4 * j:4 * j + 4, :], pV[:].rearrange("p (a b) -> p a b", b=128)), reads=[b_pV], writes=[b_V[hp][j]])
            yield

        def attn(h, j, pg):
            hp = h % 2
            KTn, Vh, sk = KTn2[hp], Vh2[hp], sk2[hp]
            bs = slice(j * 512, (j + 1) * 512)
            nkt = 4 * j + 4
            accL, b_accL = accLs.next()

            def issue_st(kt):
                i = kt - 4 * j
                q0 = 128 * i if i > 0 else 0
                stp, b_st = sts.next()
                kb = kt // 4
                S.op("pe", lambda e: e.matmul(stp[:, q0:512], KTn[:, kt * 128:(kt + 1) * 128], QTn[:, j * 512 + q0:(j + 1) * 512], start=True, stop=False),
                     reads=[b_K[hp][kb], b_Q[j]], writes=[b_st])
                S.op("pe", lambda e: e.matmul(stp[:, q0:512], C.krr2[:, kt * 128:(kt + 1) * 128], QTr[:, j * 512 + q0:(j + 1) * 512], start=False, stop=True),
                     reads=[b_pers, b_Q[j]], writes=[b_st])
                return stp, b_st, q0, i, kb

            nxt = issue_st(0)
            for kt in range(nkt):
                stp, b_st, q0, i, kb = nxt
                if kt + 1 < nkt:
                    nxt = issue_st(kt + 1)
                PT, b_PT = PTs.next()
                S.op("act", lambda e, PT=PT, stp=stp, kt=kt, q0=q0: e.activation(out=PT[:, q0:512], in_=stp[:, q0:512], func=AF.Exp, scale=sk[:, kt:kt + 1]), reads=[b_st, b_K[hp][kb]], writes=[b_PT])
                if i >= 0:
                    S.op("pool", lambda e, PT=PT, q0=q0: e.memset(PT[64:128, q0:q0 + 64], 0.0), reads=[], writes=[b_PT])
                S.op("pe", lambda e, PT=PT, kt=kt, q0=q0: e.matmul(pO[:, q0:512], Vh[:, kt, :], PT[:, q0:512], start=(kt == 0), stop=(kt == nkt - 1)), reads=[b_PT, b_V[hp][kb]], writes=[b_pO])
                if kt == 0:
                    S.op("dve", lambda e, PT=PT: e.tensor_copy(accL[:], PT[:]), reads=[b_PT], writes=[b_accL])
                else:
                    S.op("dve", lambda e, PT=PT, q0=q0: e.tensor_tensor(out=accL[:, q0:512], in0=accL[:, q0:512], in1=PT[:, q0:512], op=ALU.add), reads=[b_PT, b_accL], writes=[b_accL])
                if pg is not None:
                    next(pg, None)
                    next(pg, None)
            if pg is not None:
                for _ in pg:
                    pass
            S.op("pe", lambda e: e.matmul(pL[:], C.ones, accL[:], start=True, stop=True), reads=[b_accL], writes=[b_pL])
            S.op("act", lambda e: e.activation(out=lnb2[:], in_=pL[:], func=AF.Ln), reads=[b_pL], writes=[b_ln2])
            S.op("act", lambda e: e.activation(out=rl[:], in_=lnb2[:], func=AF.Exp, scale=-1.0), reads=[b_ln2], writes=[b_rl])
            o_t, b_o = oT.next()
            S.op("dve", lambda e: e.tensor_tensor(out=o_t[:], in0=pO[:], in1=rl[:], op=ALU.mult), reads=[b_pO, b_rl], writes=[b_o])
            S.dma("sp", "os%d" % (j % 2), lambda e: e.dma_start(out=C.oTb_d[h, :, bs], in_=o_t[:]), reads=[b_o], writes=[S.buf()])

        items = [(h, j) for h in range(NH) for j in range(NB)]
        for _ in proj(*items[0]):
            pass
        for idx, it in enumerate(items):
            pg = proj(*items[idx + 1]) if idx + 1 < len(items) else None
            attn(it[0], it[1], pg)
        S.flush()


def dn_proj_phase(C):
    nc, S, sb, ps = C.nc, C.S, C.sb, C.ps
    NB = C.NB
    C._hl = 0
    with ExitStack() as st:
        hbs = Rot(S, [sb("hb%d" % i, [128, 8, 512], BF16, st) for i in range(2)])
        wdn = sb("wdn", [128, 8, 4096], BF16, st)
        wab = sb("wab", [128, 8, 16], BF16, st)
        cw = sb("cw", [128, 24, 4], F32, st)
        halo = sb("halo", [128, 24, 3], F32, st)
        xcs = Rot(S, [sb("xc%d" % i, [128, 515], F32, st) for i in range(3)])
        accs = Rot(S, [sb("acc%d" % i, [128, 512], F32, st) for i in range(3)])
        yqk = sb("yqk", [128, 16, 512], F32, st)
        qkT = sb("qkT", [128, 16, 512], BF16, st)
        yv = sb("yv", [128, 8, 512], BF16, st)
        szb = sb("szb", [128, 8, 512], BF16, st)
        sqs = Rot(S, [sb("sq%d" % i, [128, 512], F32, st) for i in range(2)])
        lnbs = Rot(S, [sb("lnb%d" % i, [128, 512], F32, st) for i in range(2)])
        rrs = Rot(S, [sb("rr%d" % i, [128, 512], F32, st) for i in range(2)])
        abrow = sb("abrow", [128, 16], F32, st)
        nA = sb("nA", [128, 8], F32, st)
        gz = sb("gz", [128, 4, 16], F32, st)
        banks = [ps("pb%d" % i, [128, 512], F32, st) for i in range(7)]
        pps = Rot(S, banks[0:4], psum=True)
        psss = Rot(S, banks[4:6], psum=True)
        pg, b_pg = banks[6], S.buf(psum=True)
        b_w, b_cw, b_ab, b_nA, b_gz, b_yv, b_sz, b_qkT, b_gates = [S.buf() for _ in range(9)]
        b_halo = [S.buf() for _ in range(24)]
        b_yqk = [S.buf() for _ in range(16)]
        w_in_v = C.w_in.rearrange("(k p) n -> p k n", p=128)
        for q4 in range(4):
            S.dma("pool", "wl%d" % (q4 % 2), lambda e, q4=q4: e.dma_start(out=wdn[:, :, q4 * 1024:(q4 + 1) * 1024], in_=w_in_v[:, :, q4 * 1024:(q4 + 1) * 1024]), writes=[b_w])
        S.dma("pool", "wl0", lambda e: e.dma_start(out=wab[:], in_=w_in_v[:, :, C_AL:C_AL + 16]), writes=[b_w])
        S.dma("sp", "ld1", lambda e: e.dma_start(out=cw[:], in_=C.conv_pk), writes=[b_cw])
        S.dma("sp", "ld2", lambda e: e.dma_start(out=abrow[:], in_=C.ab_row.partition_broadcast(128)), writes=[b_ab])
        S.op("act", lambda e: e.activation(out=nA[:], in_=abrow[:, 0:8], func=AF.Exp), reads=[b_ab], writes=[b_nA])
        S.op("dve", lambda e: e.tensor_scalar(nA[:], nA[:], -1.0, None, ALU.mult), reads=[b_nA], writes=[b_nA])
        for c in range(24):
            S.op("pool", lambda e, c=c: e.memset(halo[:, c, :], 0.0), writes=[b_halo[c]])
        for j in range(NB):
            hb, b_hb = hbs.next()
            load_hT_block(C, hb, b_hb, j)
            bs = slice(j * 512, (j + 1) * 512)
            def conv_stage1(c, hb, b_hb):
                pp, b_pp = pps.next()
                for k in range(8):
                    S.op("pe", lambda e, k=k: e.matmul(pp[:], wdn[:, k, c * 128:(c + 1) * 128], hb[:, k, :], start=(k == 0), stop=(k == 7)), reads=[b_w, b_hb], writes=[b_pp])
                if c >= 24:
                    return ("z", pp, b_pp)
                xc, b_xc = xcs.next()
                acc, b_acc = accs.next()
                S.op("act", lambda e: e.activation(out=xc[:, 3:515], in_=pp[:], func=AF.Copy), reads=[b_pp], writes=[b_xc])
                S.op("act", lambda e: e.activation(out=acc[:], in_=pp[:], func=AF.Copy, scale=cw[:, c, 3:4]), reads=[b_pp, b_cw], writes=[b_acc])
                S.op("pool", lambda e: e.tensor_copy(xc[:, 0:3], halo[:, c, :]), reads=[b_halo[c]], writes=[b_xc])
                S.op("pool", lambda e: e.tensor_copy(halo[:, c, :], xc[:, 512:515]), reads=[b_xc], writes=[b_halo[c]])
                for jj in (2, 1, 0):
                    S.op("dve", lambda e, jj=jj: e.scalar_tensor_tensor(out=acc[:], in0=xc[:, jj:jj + 512], scalar=cw[:, c, jj:jj + 1], in1=acc[:], op0=ALU.mult, op1=ALU.add),
                         reads=[b_xc, b_acc, b_cw], writes=[b_acc])
                return ("c", acc, b_acc)

            def conv_stage2(c, kind, src, b_src):
                if kind == "z":
                    S.op("act", lambda e: e.activation(out=szb[:, c - 24, :], in_=src[:], func=AF.Silu), reads=[b_src], writes=[b_sz])
                elif c < 16:
                    S.op("act", lambda e: e.activation(out=yqk[:, c, :], in_=src[:], func=AF.Silu), reads=[b_src], writes=[b_yqk[c]])
                else:
                    S.op("act", lambda e: e.activation(out=yv[:, c - 16, :], in_=src[:], func=AF.Silu), reads=[b_src], writes=[b_yv])

            prev = None
            for c in range(32):
                cur = (c,) + conv_stage1(c, hb, b_hb)
                if prev is not None:
                    conv_stage2(*prev)
                prev = cur
            conv_stage2(*prev)
            for tt in range(4):
                for k in range(8):
                    S.op("pe", lambda e, hb=hb, tt=tt, k=k: e.matmul(pg[:, tt * 16:(tt + 1) * 16], hb[:, k, tt * 128:(tt + 1) * 128], wab[:, k, :], start=(k == 0), stop=(k == 7)), reads=[b_w, b_hb], writes=[b_pg])
            pgv = pg[:, 0:64].rearrange("p (a b) -> p a b", b=16)
            S.op("dve", lambda e: e.tensor_tensor(out=gz[:, :, 0:8], in0=pgv[:, :, 0:8], in1=abrow[:, 8:16].unsqueeze(1).to_broadcast([128, 4, 8]), op=ALU.add), reads=[b_pg, b_ab], writes=[b_gz])
            S.op("act", lambda e: e.activation(out=gz[:, :, 0:8], in_=gz[:, :, 0:8], func=AF.Exp), reads=[b_gz], writes=[b_gz])
            S.op("act", lambda e: e.activation(out=gz[:, :, 0:8], in_=gz[:, :, 0:8], func=AF.Ln, bias=1.0), reads=[b_gz], writes=[b_gz])
            S.op("dve", lambda e, j=j: e.tensor_tensor(out=C.gates[:, 4 * j:4 * j + 4, 0:8], in0=gz[:, :, 0:8], in1=nA[:].unsqueeze(1).to_broadcast([128, 4, 8]), op=ALU.mult), reads=[b_gz, b_nA], writes=[b_gates])
            S.op("act", lambda e: e.activation(out=gz[:, :, 8:16], in_=pgv[:, :, 8:16], func=AF.Exp, scale=-1.0), reads=[b_pg], writes=[b_gz])
            S.op("dve", lambda e: e.tensor_scalar(gz[:, :, 8:16], gz[:, :, 8:16], 1.0, None, ALU.add), reads=[b_gz], writes=[b_gz])
            S.op("dve", lambda e, j=j: e.reciprocal(C.gates[:, 4 * j:4 * j + 4, 8:16], gz[:, :, 8:16]), reads=[b_gz], writes=[b_gates])
            def l2_stage1(c):
                sq, b_sq = sqs.next()
                pss, b_pss = psss.next()
                S.op("act", lambda e: e.activation(out=sq[:], in_=yqk[:, c, :], func=AF.Square), reads=[b_yqk[c]], writes=[b_sq])
                S.op("pe", lambda e: e.matmul(pss[:], C.ones, sq[:], start=True, stop=True), reads=[b_sq], writes=[b_pss])
                return pss, b_pss

            def l2_stage2(c, pss, b_pss):
                lnb, b_ln = lnbs.next()
                rr, b_rr = rrs.next()
                S.op("act", lambda e: e.activation(out=lnb[:], in_=pss[:], func=AF.Ln, bias=EPS), reads=[b_pss], writes=[b_ln])
                bias = float(np.log(128.0 ** -0.5)) if c < 8 else 0.0
                S.op("act", lambda e: e.activation(out=rr[:], in_=lnb[:], func=AF.Exp, scale=-0.5, bias=bias), reads=[b_ln], writes=[b_rr])
                S.op("dve", lambda e: e.tensor_tensor(out=qkT[:, c, :], in0=yqk[:, c, :], in1=rr[:], op=ALU.mult), reads=[b_yqk[c], b_rr], writes=[b_qkT])

            prev = None
            for c in range(16):
                cur = (c,) + l2_stage1(c)
                if prev is not None:
                    l2_stage2(*prev)
                prev = cur
            l2_stage2(*prev)
            S.dma("sp", "st0", lambda e, bs=bs: e.dma_start(out=C.qk_d[:, :, bs].rearrange("c p t -> p c t"), in_=qkT[:]), reads=[b_qkT], writes=[S.buf()])
            S.dma("sp", "st1", lambda e, bs=bs: e.dma_start(out=C.v_d[:, :, bs].rearrange("c p t -> p c t"), in_=yv[:]), reads=[b_yv], writes=[S.buf()])
            S.dma("sp", "st2", lambda e, bs=bs: e.dma_start(out=C.sz_d[:, :, bs].rearrange("c p t -> p c t"), in_=szb[:]), reads=[b_sz], writes=[S.buf()])
        if "gates" in C.dbg:
            S.dma("sp", "st3", lambda e: e.dma_start(out=C.dbg["gates"], in_=C.gates[:].rearrange("p a b -> p (a b)")), reads=[b_gates])
        S.flush()


def dn_rule_phase(C):
    nc, S, sb, ps = C.nc, C.S, C.sb, C.ps
    NB = C.NB
    U = C.cst[:, 3, :]
    Ms = C.cst[:, 4, :]
    Fm = C.cst[:, 5, :]
    idt = C.idt
    stage = getattr(C, 'rule_stage', 99)

    def bc_h(ap2):
        return ap2.unsqueeze(1).to_broadcast([128, 8, 128])

    def bc_c(ap2):
        return ap2.unsqueeze(2).to_broadcast([128, 8, 128])

    with ExitStack() as st:
        def t3(name, dt=F32, n=1):
            return Rot(S, [sb("%s%d" % (name, i), [128, 8, 128], dt, st) for i in range(n)])

        inb = Rot(S, [sb("inb%d" % i, [128, 32, 256], BF16, st) for i in range(2)])
        outb = Rot(S, [sb("outb%d" % i, [128, 8, 512], BF16, st) for i in range(2)])
        dnw = sb("dnw", [128, 1], F32, st)
        Sst = sb("Sst", [128, 8, 128], F32, st)
        Sbf = sb("Sbf", [128, 8, 128], BF16, st)
        b_S, b_Sbf, b_dnw = S.buf(), S.buf(), S.buf()
        GU = t3("GU")
        gsm = Rot(S, [sb("gsm%d" % i, [128, 48], F32, st) for i in range(2)])
        d1s, d2s = t3("d1"), t3("d2")
        E1s, E2s, EgRs = t3("E1"), t3("E2"), t3("EgR", F32, 2)
        EMs, DTs = t3("EM"), t3("DT")
        intraTs = t3("intraT", BF16, 2)
        Ms_ = t3("M", F32, 2)
        Mts = t3("Mt", F32, 2)
        Tts = t3("Tt", F32, 2)
        Ttbs = t3("Ttb", BF16)
        kbgs, kdecs, vbs = t3("kbg", BF16), t3("kdec", BF16, 2), t3("vb", BF16)
        us = t3("u", F32, 2)
        wTs = t3("wT", BF16, 2)
        qeTs = t3("qeT", BF16, 2)
        vnews = t3("vnew", BF16, 2)
        oraws = t3("oraw", F32)
        sqo = t3("sqo", F32)
        lno = t3("lno", F32)
        ro = t3("ro", F32)
        o1s = t3("o1", F32)
        banks = [ps("pb%d" % i, [128, 1024], F32, st) for i in range(4)]
        slotsAB = Rot(S, banks[0:2], psum=True)
        po_bank, b_po_bank = banks[2], S.buf(psum=True)
        slotsC = Rot(S, [banks[3][:, 0:512], banks[3][:, 512:1024]], psum=True)

        def slot3():
            p, b = slotsAB.next()
            return p[:].rearrange("p (a b) -> p a b", b=128), p, b

        def slotc():
            p, b = slotsC.next()
            return p.rearrange("p (a b) -> p a b", b=128), b

        S.dma("sp", "ld1", lambda e: e.dma_start(out=dnw[:], in_=C.dnw_pk), writes=[b_dnw])
        S.op("pool", lambda e: e.memset(Sst[:], 0.0), writes=[b_S])
        S.op("pool", lambda e: e.memset(Sbf[:], 0.0), writes=[b_Sbf])
        b_pers = S.buf()
        state = {"in": None, "out": None}
        handoff = {}

        def gen_AB(t):
            j, tt = t // 4, t % 4
            ic = slice((t % 2) * 128, (t % 2 + 1) * 128)
            if t % 2 == 0:
                state["in"] = inb.next()
                ib, b_ib = state["in"]
                hs_ = slice(t * 128, (t + 2) * 128)
                S.dma("sp", "il0%d" % ((t // 2) % 2), lambda e: e.dma_start(out=ib[:, 0:16, :], in_=C.qk_d[:, :, hs_].rearrange("c p t -> p c t")), writes=[b_ib])
                S.dma("sp", "il1%d" % ((t // 2) % 2), lambda e: e.dma_start(out=ib[:, 16:24, :], in_=C.v_d[:, :, hs_].rearrange("c p t -> p c t")), writes=[b_ib])
                S.dma("sp", "il2%d" % ((t // 2) % 2), lambda e: e.dma_start(out=ib[:, 24:32, :], in_=C.sz_d[:, :, hs_].rearrange("c p t -> p c t")), writes=[b_ib])
            if tt == 0:
                state["out"] = outb.next()
            ib, b_ib = state["in"]
            ob, b_ob = state["out"]
            qT = ib[:, 0:8, ic]
            kT = ib[:, 8:16, ic]
            vT = ib[:, 16:24, ic]
            szT = ib[:, 24:32, ic]
            g = C.gates[:, t, 0:8]
            beta = C.gates[:, t, 8:16]
            H = {"szT": szT, "b_ib": b_ib, "ob": ob, "b_ob": b_ob, "t": t}
            handoff[t] = H
            gu, b_gu = GU.next()
            S.op("dve", lambda e: e.tensor_tensor(out=gu[:], in0=bc_h(U), in1=bc_c(g), op=ALU.mult), reads=[b_pers], writes=[b_gu])
            yield
            gcR, gcRf, b_gcR = slot3()
            for hg in range(2):
                S.op("pe", lambda e, hg=hg: e.matmul(gcRf[:, hg * 512:(hg + 1) * 512], C.ones, gu[:, hg * 4:(hg + 1) * 4, :].rearrange("p a b -> p (a b)"), start=True, stop=True),
                     reads=[b_gu], writes=[b_gcR])
            yield
            gs, b_gs = gsm.next()
            gc = gs[:, 0:8]
            d1, b_d1 = d1s.next()
            d2, b_d2 = d2s.next()
            E1, b_E1 = E1s.next()
            E2, b_E2 = E2s.next()
            EgR, b_EgR = EgRs.next()
            H["EgR"], H["b_EgR"] = EgR, b_EgR
            pgc, pgcf, b_pgc = slot3()
            S.op("pe", lambda e: e.matmul(pgcf[:, 0:8], U, g, start=True, stop=True), reads=[b_pers], writes=[b_pgc])
            S.op("pe", lambda e: e.matmul(pgcf[:, 8:16], Fm, g, start=True, stop=True), reads=[b_pers], writes=[b_pgc])
            S.op("dve", lambda e: e.tensor_copy(gs[:, 0:16], pgcf[:, 0:16]), reads=[b_pgc], writes=[b_gs])
            yield
            S.op("dve", lambda e: e.scalar_tensor_tensor(out=d1[:], in0=gcR, scalar=-1.0, in1=bc_c(gc), op0=ALU.mult, op1=ALU.add), reads=[b_gcR, b_gs], writes=[b_d1])
            yield
            S.op("dve", lambda e: e.scalar_tensor_tensor(out=d2[:], in0=gcR, scalar=1.0, in1=bc_c(gc), op0=ALU.mult, op1=ALU.subtract), reads=[b_gcR, b_gs], writes=[b_d2])
            S.op("act", lambda e: e.activation(out=EgR[:], in_=gcR, func=AF.Exp), reads=[b_gcR], writes=[b_EgR])
            yield
            S.op("pool", lambda e: e.tensor_scalar(d1[:], d1[:], 0.0, -3.0e38, ALU.min, ALU.max), reads=[b_d1], writes=[b_d1])
            S.op("pool", lambda e: e.tensor_scalar(d2[:], d2[:], 0.0, -3.0e38, ALU.min, ALU.max), reads=[b_d2], writes=[b_d2])
            S.op("act", lambda e: e.activation(out=E1[:], in_=d1[:], func=AF.Exp), reads=[b_d1], writes=[b_E1])
            yield
            S.op("act", lambda e: e.activation(out=E2[:], in_=d2[:], func=AF.Exp), reads=[b_d2], writes=[b_E2])
            S.op("dve", lambda e: e.tensor_tensor(out=gs[:, 24:32], in0=gs[:, 8:16], in1=gs[:, 0:8], op=ALU.subtract), reads=[b_gs], writes=[b_gs])
            S.op("act", lambda e: e.activation(out=gs[:, 16:24], in_=gs[:, 0:8], func=AF.Exp), reads=[b_gs], writes=[b_gs])
            S.op("act", lambda e: e.activation(out=gs[:, 24:32], in_=gs[:, 24:32], func=AF.Exp), reads=[b_gs], writes=[b_gs])
            S.op("dve", lambda e: e.tensor_tensor(out=gs[:, 32:40], in0=gs[:, 16:24], in1=beta, op=ALU.mult), reads=[b_gs, b_pers], writes=[b_gs])
            S.op("dve", lambda e: e.tensor_scalar(gs[:, 40:48], beta, -1.0, None, ALU.mult), reads=[b_pers], writes=[b_gs])
            edl, bg, nbeta = gs[:, 24:32], gs[:, 32:40], gs[:, 40:48]
            yield
            if stage < 2:
                return
            KK, KKf, b_KK = slot3()
            for h in range(8):
                S.op("pe", lambda e, h=h: e.matmul(KK[:, h, :], kT[:, h, :], kT[:, h, :], start=True, stop=True), reads=[b_ib], writes=[b_KK])
            yield
            EM, b_EM = EMs.next()
            S.op("dve", lambda e: e.tensor_tensor(out=EM[:], in0=E1[:], in1=bc_h(Ms), op=ALU.mult), reads=[b_E1, b_pers], writes=[b_EM])
            yield
            M0, b_M0 = Ms_.next()
            for h in range(8):
                S.op("dve", lambda e, h=h: e.scalar_tensor_tensor(out=M0[:, h, :], in0=KK[:, h, :], scalar=nbeta[:, h:h + 1], in1=EM[:, h, :], op0=ALU.mult, op1=ALU.mult),
                     reads=[b_KK, b_EM, b_gs], writes=[b_M0])
                if h % 2 == 1:
                    yield
            QK, QKf, b_QK = slot3()
            for h in range(8):
                S.op("pe", lambda e, h=h: e.matmul(QK[:, h, :], kT[:, h, :], qT[:, h, :], start=True, stop=True), reads=[b_ib], writes=[b_QK])
            yield
            DT, b_DT = DTs.next()
            S.op("dve", lambda e: e.tensor_tensor(out=DT[:], in0=E2[:], in1=bc_h(U), op=ALU.mult), reads=[b_E2, b_pers], writes=[b_DT])
            yield
            intraT, b_intraT = intraTs.next()
            H["intraT"], H["b_intraT"] = intraT, b_intraT
            S.op("dve", lambda e: e.tensor_tensor(out=intraT[:], in0=QK, in1=DT[:], op=ALU.mult), reads=[b_QK, b_DT], writes=[b_intraT])
            yield
            if stage < 3:
                return
            pk, pkf, b_pk = slot3()
            pkb = pkf[:, 0:512].bitcast(BF16).rearrange("p (a b) -> p a b", b=128)
            for h in range(8):
                S.op("pe", lambda e, h=h: e.transpose(pkb[:, h, :], kT[:, h, :], C.idtb[:]), reads=[b_ib, b_pers], writes=[b_pk])
            yield
            kbg, b_kbg = kbgs.next()
            kdec, b_kdec = kdecs.next()
            H["kdec"], H["b_kdec"] = kdec, b_kdec
            S.op("dve", lambda e: e.tensor_tensor(out=kbg[:], in0=pkb, in1=bc_c(bg), op=ALU.mult), reads=[b_pk, b_gs], writes=[b_kbg])
            yield
            S.op("dve", lambda e: e.tensor_tensor(out=kdec[:], in0=pkb, in1=bc_c(edl), op=ALU.mult), reads=[b_pk, b_gs], writes=[b_kdec])
            yield
            pv, pvf, b_pv = slot3()
            pvb = pvf[:, 0:512].bitcast(BF16).rearrange("p (a b) -> p a b", b=128)
            for h in range(8):
                S.op("pe", lambda e, h=h: e.transpose(pvb[:, h, :], vT[:, h, :], C.idtb[:]), reads=[b_ib, b_pers], writes=[b_pv])
            yield
            vb, b_vb = vbs.next()
            S.op("dve", lambda e: e.tensor_tensor(out=vb[:], in0=pvb, in1=bc_c(beta), op=ALU.mult), reads=[b_pv, b_pers], writes=[b_vb])
            yield
            qeT, b_qeT = qeTs.next()
            H["qeT"], H["b_qeT"] = qeT, b_qeT
            S.op("dve", lambda e: e.tensor_tensor(out=qeT[:], in0=qT, in1=EgR[:], op=ALU.mult), reads=[b_ib, b_EgR], writes=[b_qeT])
            yield
            pBt, pBtf, b_pBt = slot3()
            for h in range(8):
                S.op("pe", lambda e, h=h: e.transpose(pBt[:, h, :], M0[:, h, :], idt), reads=[b_M0, b_pers], writes=[b_pBt])
            yield
            Mt0, b_Mt0 = Mts.next()
            Tt, b_Tt = Tts.next()
            S.op("act", lambda e: e.activation(out=Mt0[:], in_=pBt, func=AF.Copy), reads=[b_pBt], writes=[b_Mt0])
            S.op("dve", lambda e, Tt=Tt: e.tensor_tensor(out=Tt[:], in0=pBt, in1=bc_h(idt), op=ALU.add), reads=[b_pBt, b_pers], writes=[b_Tt])
            yield
            Mc, b_Mc, Mtc, b_Mtc = M0, b_M0, Mt0, b_Mt0
            for lvl in range(1, 6):
                pM, pMf, b_pM = slot3()
                for h in range(8):
                    S.op("pe", lambda e, h=h, pM=pM, Mc=Mc, Mtc=Mtc: e.matmul(pM[:, h, :], Mtc[:, h, :], Mc[:, h, :], start=True, stop=True), reads=[b_Mc, b_Mtc], writes=[b_pM])
                    if h % 2 == 1:
                        yield
                Mn, b_Mn = Ms_.next()
                S.op("act", lambda e, Mn=Mn, pM=pM: e.activation(out=Mn[:], in_=pM, func=AF.Copy), reads=[b_pM], writes=[b_Mn])
                if lvl < 5:
                    pMt, pMtf, b_pMt = slot3()
                    for h in range(8):
                        S.op("pe", lambda e, h=h, pMt=pMt, Mc=Mc, Mtc=Mtc: e.matmul(pMt[:, h, :], Mc[:, h, :], Mtc[:, h, :], start=True, stop=True), reads=[b_Mc, b_Mtc], writes=[b_pMt])
                        if h % 2 == 1:
                            yield
                    Mtn, b_Mtn = Mts.next()
                    S.op("act", lambda e, Mtn=Mtn, pMt=pMt: e.activation(out=Mtn[:], in_=pMt, func=AF.Copy), reads=[b_pMt], writes=[b_Mtn])
                pT, pTf, b_pT = slot3()
                for h in range(8):
                    S.op("pe", lambda e, h=h, pT=pT, Mn=Mn, Tt=Tt: e.matmul(pT[:, h, :], Mn[:, h, :], Tt[:, h, :], start=True, stop=True), reads=[b_Mn, b_Tt], writes=[b_pT])
                    if h % 2 == 1:
                        yield
                Ttn, b_Ttn = Tts.next()
                S.op("dve", lambda e, Ttn=Ttn, pT=pT, Tt=Tt: e.tensor_tensor(out=Ttn[:], in0=pT, in1=Tt[:], op=ALU.add), reads=[b_pT, b_Tt], writes=[b_Ttn])
                yield
                Tt, b_Tt = Ttn, b_Ttn
                Mc, b_Mc = Mn, b_Mn
                if lvl < 5:
                    Mtc, b_Mtc = Mtn, b_Mtn
            Ttb, b_Ttb = Ttbs.next()
            S.op("act", lambda e, Tt=Tt: e.activation(out=Ttb[:], in_=Tt[:], func=AF.Copy), reads=[b_Tt], writes=[b_Ttb])
            yield
            if stage < 4:
                return
            pu, puf, b_pu = slot3()
            for h in range(8):
                S.op("pe", lambda e, h=h: e.matmul(pu[:, h, :], Ttb[:, h, :], vb[:, h, :], start=True, stop=True), reads=[b_Ttb, b_vb], writes=[b_pu])
            yield
            u, b_u = us.next()
            H["u"], H["b_u"] = u, b_u
            S.op("act", lambda e: e.activation(out=u[:], in_=pu, func=AF.Copy), reads=[b_pu], writes=[b_u])
            pw, pwf, b_pw = slot3()
            for h in range(8):
                S.op("pe", lambda e, h=h: e.matmul(pw[:, h, :], kbg[:, h, :], Ttb[:, h, :], start=True, stop=True), reads=[b_kbg, b_Ttb], writes=[b_pw])
            yield
            wT, b_wT = wTs.next()
            H["wT"], H["b_wT"] = wT, b_wT
            S.op("act", lambda e: e.activation(out=wT[:], in_=pw, func=AF.Copy), reads=[b_pw], writes=[b_wT])
            H["done"] = True
            yield

        def gen_C(t):
            H = handoff.pop(t)
            if not H.get("done") or stage < 5:
                return
            j, tt = t // 4, t % 4
            tc = slice(tt * 128, (tt + 1) * 128)
            EgR, b_EgR = H["EgR"], H["b_EgR"]
            intraT, b_intraT = H["intraT"], H["b_intraT"]
            qeT, b_qeT = H["qeT"], H["b_qeT"]
            kdec, b_kdec = H["kdec"], H["b_kdec"]
            u, b_u, wT, b_wT = H["u"], H["b_u"], H["wT"], H["b_wT"]
            szT, b_ib, ob, b_ob = H["szT"], H["b_ib"], H["ob"], H["b_ob"]
            po, b_po = po_bank[:].rearrange("p (a b) -> p a b", b=128), b_po_bank
            def gen_chunk(ci):
                pr = slice(ci * 64, (ci + 1) * 64)
                cc = slice(ci * 64, (ci + 1) * 64)
                lc = ci * 64 + 63
                vnew, b_vnew = vnews.next()
                for hg in range(2):
                    pa, b_pa = slotc()
                    for h4 in range(4):
                        h = hg * 4 + h4
                        S.op("pe", lambda e, h=h, h4=h4, pa=pa: e.matmul(pa[pr, h4, :], wT[:, h, cc], Sbf[:, h, :], start=True, stop=True), reads=[b_wT, b_Sbf], writes=[b_pa])
                    yield
                    S.op("dve", lambda e, hg=hg, pa=pa: e.tensor_tensor(out=vnew[pr, hg * 4:(hg + 1) * 4, :], in0=u[pr, hg * 4:(hg + 1) * 4, :], in1=pa[pr, :, :], op=ALU.subtract), reads=[b_u, b_pa], writes=[b_vnew])
                    yield
                for h in range(8):
                    S.op("pe", lambda e, h=h: e.matmul(po[:, h, cc], Sbf[:, h, :], qeT[:, h, cc], start=True, stop=False), reads=[b_Sbf, b_qeT], writes=[b_po])
                    S.op("pe", lambda e, h=h, vnew=vnew: e.matmul(po[:, h, cc], vnew[pr, h, :], intraT[pr, h, cc], start=False, stop=True), reads=[b_vnew, b_intraT], writes=[b_po])
                    if h % 2 == 1:
                        yield
                for hg in range(2):
                    pS, b_pS = slotc()
                    for h4 in range(4):
                        h = hg * 4 + h4
                        S.op("pe", lambda e, h=h, h4=h4, pS=pS, vnew=vnew: e.matmul(pS[:, h4, :], kdec[pr, h, :], vnew[pr, h, :], start=True, stop=True), reads=[b_kdec, b_vnew], writes=[b_pS])
                    yield
                    for h4 in range(4):
                        h = hg * 4 + h4
                        S.op("dve", lambda e, h=h, h4=h4, pS=pS: e.scalar_tensor_tensor(out=Sst[:, h, :], in0=Sst[:, h, :], scalar=EgR[:, h, lc:lc + 1], in1=pS[:, h4, :], op0=ALU.mult, op1=ALU.add),
                             reads=[b_S, b_EgR, b_pS], writes=[b_S])
                    yield
                S.op("act", lambda e: e.activation(out=Sbf[:], in_=Sst[:], func=AF.Copy), reads=[b_S], writes=[b_Sbf])
                yield

            for ci in range(2):
                yield from gen_chunk(ci)
            oraw, b_oraw = oraws.next()
            S.op("act", lambda e: e.activation(out=oraw[:], in_=po, func=AF.Copy), reads=[b_po], writes=[b_oraw])
            yield
            if stage < 6:
                return
            sq, b_sq = sqo.next()
            S.op("act", lambda e: e.activation(out=sq[:], in_=oraw[:], func=AF.Square), reads=[b_oraw], writes=[b_sq])
            yield
            pq, b_pq = slotc()
            pq2, b_pq2 = slotc()
            S.op("pe", lambda e: e.matmul(pq.rearrange("p a b -> p (a b)"), C.ones, sq[:, 0:4, :].rearrange("p a b -> p (a b)"), start=True, stop=True), reads=[b_sq], writes=[b_pq])
            S.op("pe", lambda e: e.matmul(pq2.rearrange("p a b -> p (a b)"), C.ones, sq[:, 4:8, :].rearrange("p a b -> p (a b)"), start=True, stop=True), reads=[b_sq], writes=[b_pq2])
            yield
            ln_, b_ln = lno.next()
            r_, b_r = ro.next()
            S.op("act", lambda e: e.activation(out=ln_[:, 0:4, :], in_=pq, func=AF.Ln, scale=1.0 / 128, bias=EPS), reads=[b_pq], writes=[b_ln])
            S.op("act", lambda e: e.activation(out=ln_[:, 4:8, :], in_=pq2, func=AF.Ln, scale=1.0 / 128, bias=EPS), reads=[b_pq2], writes=[b_ln])
            yield
            S.op("act", lambda e: e.activation(out=r_[:], in_=ln_[:], func=AF.Exp, scale=-0.5), reads=[b_ln], writes=[b_r])
            yield
            o1, b_o1 = o1s.next()
            S.op("dve", lambda e: e.scalar_tensor_tensor(out=o1[:], in0=oraw[:], scalar=dnw[:, 0:1], in1=r_[:], op0=ALU.mult, op1=ALU.mult), reads=[b_oraw, b_r, b_dnw], writes=[b_o1])
            yield
            S.op("dve", lambda e: e.tensor_tensor(out=ob[:, :, tc], in0=o1[:], in1=szT, op=ALU.mult), reads=[b_o1, b_ib], writes=[b_ob])
            if tt == 3:
                bs = slice(j * 512, (j + 1) * 512)
                S.dma("sp", "os%d" % (j % 2), lambda e: e.dma_start(out=C.oTa_d[:, :, bs].rearrange("h p t -> p h t"), in_=ob[:]), reads=[b_ob], writes=[S.buf()])
            yield

        def run_all(gens):
            gens = list(gens)
            while gens:
                for g_ in list(gens):
                    try:
                        next(g_)
                    except StopIteration:
                        gens.remove(g_)

        run_all([gen_AB(0)])
        for t in range(1, C.NT):
            run_all([gen_AB(t), gen_C(t - 1)])
        run_all([gen_C(C.NT - 1)])
        S.flush()


def outproj_phase(C):
    nc, S, sb, ps = C.nc, C.S, C.sb, C.ps
    NB = C.NB
    C._hl = 0
    with ExitStack() as st:
        wg = sb("wg", [128, 8, 2048], BF16, st)
        wod = sb("wod", [128, 8, 1024], BF16, st)
        wom = sb("wom", [128, 8, 1024], BF16, st)
        wo = sb("wo", [128, 8, 1024], BF16, st)
        tmps = Rot(S, [sb("tmp%d" % i, [128, 512], F32, st) for i in range(2)])
        hbs = Rot(S, [sb("hb%d" % i, [128, 8, 512], BF16, st) for i in range(2)])
        oas = Rot(S, [sb("oa%d" % i, [128, 8, 512], BF16, st) for i in range(2)])
        obks = Rot(S, [sb("obk%d" % i, [128, 8, 512], BF16, st) for i in range(2)])
        mixs = Rot(S, [sb("mix%d" % i, [128, 8, 512], BF16, st) for i in range(2)])
        xts = Rot(S, [sb("xt%d" % i, [128, 1024], F32, st) for i in range(3)])
        sg = Rot(S, [sb("sg%d" % i, [128, 512], F32, st) for i in range(4)])
        mm_ = Rot(S, [sb("mm%d" % i, [128, 512], F32, st) for i in range(4)])
        banks = Rot(S, [ps("pb%d" % i, [128, 512], F32, st) for i in range(8)], psum=True)
        b_wg, b_wod, b_wom, b_wo, b_pers, b_x = [S.buf() for _ in range(6)]
        w_in_v = C.w_in.rearrange("(k p) n -> p k n", p=128)
        S.dma("pool", "wl0", lambda e: e.dma_start(out=wg[:, :, 0:1024], in_=w_in_v[:, :, C_GD:C_GD + 1024]), writes=[b_wg])
        S.dma("pool", "wl1", lambda e: e.dma_start(out=wg[:, :, 1024:2048], in_=w_in_v[:, :, C_GM:C_GM + 1024]), writes=[b_wg])
        S.dma("pool", "wl2", lambda e: e.dma_start(out=wod[:], in_=C.w_out_dn.rearrange("(h p) n -> p h n", p=128)), writes=[b_wod])
        S.dma("pool", "wl3", lambda e: e.dma_start(out=wom[:], in_=C.w_out_mla.rearrange("(h p) n -> p h n", p=128)), writes=[b_wom])
        S.dma("pool", "wl4", lambda e: e.dma_start(out=wo[:], in_=C.w_o.rearrange("(k p) n -> p k n", p=128)), writes=[b_wo])
        def outp_block(j, hbp, oap, obkp):
            hb, b_hb = hbp
            oa, b_oa = oap
            obk, b_obk = obkp
            bs = slice(j * 512, (j + 1) * 512)
            load_hT_block(C, hb, b_hb, j)
            S.dma("sp", "al0%d" % (j % 2), lambda e, bs=bs: e.dma_start(out=oa[:], in_=C.oTa_d[:, :, bs].rearrange("h p t -> p h t")), writes=[b_oa])
            S.dma("sp", "al1%d" % (j % 2), lambda e, bs=bs: e.dma_start(out=obk[:], in_=C.oTb_d[:, :, bs].rearrange("h p t -> p h t")), writes=[b_obk])
            mix, b_mix = mixs.next()
            for c in range(8):
                cs_ = slice(c * 128, (c + 1) * 128)
                p1, b_p1 = banks.next()
                for k in range(8):
                    S.op("pe", lambda e, p1=p1, k=k, cs_=cs_: e.matmul(p1[:], wg[:, k, cs_], hb[:, k, :], start=(k == 0), stop=(k == 7)), reads=[b_wg, b_hb], writes=[b_p1])
                p2, b_p2 = banks.next()
                for k in range(8):
                    S.op("pe", lambda e, p2=p2, k=k, c=c: e.matmul(p2[:], wg[:, k, 1024 + c * 128:1024 + (c + 1) * 128], hb[:, k, :], start=(k == 0), stop=(k == 7)), reads=[b_wg, b_hb], writes=[b_p2])
                p3, b_p3 = banks.next()
                for h in range(8):
                    S.op("pe", lambda e, p3=p3, h=h, cs_=cs_: e.matmul(p3[:], wod[:, h, cs_], oa[:, h, :], start=(h == 0), stop=(h == 7)), reads=[b_wod, b_oa], writes=[b_p3])
                p4, b_p4 = banks.next()
                for h in range(8):
                    S.op("pe", lambda e, p4=p4, h=h, cs_=cs_: e.matmul(p4[:], wom[:, h, cs_], obk[:, h, :], start=(h == 0), stop=(h == 7)), reads=[b_wom, b_obk], writes=[b_p4])
                s1, b_s1 = sg.next()
                s2, b_s2 = sg.next()
                S.op("act", lambda e, s1=s1, p1=p1: e.activation(out=s1[:], in_=p1[:], func=AF.Sigmoid), reads=[b_p1], writes=[b_s1])
                S.op("act", lambda e, s2=s2, p2=p2: e.activation(out=s2[:], in_=p2[:], func=AF.Sigmoid), reads=[b_p2], writes=[b_s2])
                m1, b_m1 = mm_.next()
                m2, b_m2 = mm_.next()
                S.op("dve", lambda e, m1=m1, s1=s1, p3=p3: e.tensor_tensor(out=m1[:], in0=s1[:], in1=p3[:], op=ALU.mult), reads=[b_s1, b_p3], writes=[b_m1])
                S.op("dve", lambda e, m2=m2, s2=s2, p4=p4: e.tensor_tensor(out=m2[:], in0=s2[:], in1=p4[:], op=ALU.mult), reads=[b_s2, b_p4], writes=[b_m2])
                S.op("pool", lambda e, mix=mix, m1=m1, m2=m2, c=c: e.tensor_tensor(out=mix[:, c, :], in0=m1[:], in1=m2[:], op=ALU.add), reads=[b_m1, b_m2], writes=[b_mix])
            for tt in range(4):
                t = j * 4 + tt
                xt, b_xt = xts.next()
                S.dma("sp", "xl%d" % (t % 3), lambda e, xt=xt, t=t: e.dma_start(out=xt[:], in_=C.x[t * 128:(t + 1) * 128, :]), reads=[b_x], writes=[b_xt])
                for hf in range(2):
                    pw, b_pw = banks.next()
                    for k in range(8):
                        S.op("pe", lambda e, pw=pw, mix=mix, k=k, tt=tt, hf=hf: e.matmul(pw[:], mix[:, k, tt * 128:(tt + 1) * 128], wo[:, k, hf * 512:(hf + 1) * 512], start=(k == 0), stop=(k == 7)), reads=[b_mix, b_wo], writes=[b_pw])
                    tmp, b_tmp = tmps.next()
                    S.op("dve", lambda e, tmp=tmp, pw=pw, hf=hf: e.tensor_tensor(out=tmp[:], in0=pw[:], in1=C.gateB[:, 0, hf * 512:(hf + 1) * 512], op=ALU.mult), reads=[b_pw, b_pers], writes=[b_tmp])
                    S.op("pool", lambda e, xt=xt, tmp=tmp, hf=hf: e.tensor_tensor(out=xt[:, hf * 512:(hf + 1) * 512], in0=xt[:, hf * 512:(hf + 1) * 512], in1=tmp[:], op=ALU.add), reads=[b_xt, b_tmp], writes=[b_xt])
                S.dma("sp", "xs%d" % (t % 3), lambda e, xt=xt, t=t: e.dma_start(out=C.out[t * 128:(t + 1) * 128, :], in_=xt[:]), reads=[b_xt], writes=[S.buf()])

        for j in range(NB):
            outp_block(j, hbs.next(), oas.next(), obks.next())
        S.flush()


def ffn_phase(C):
    nc, S, sb, ps = C.nc, C.S, C.sb, C.ps
    NB = C.NB
    NC_ = D_FF // 128
    C._hl = 0
    with ExitStack() as st:
        wup = sb("wup", [128, 8, 2 * D_FF], BF16, st)
        wdn = sb("wdnf", [128, NC_, 1024], BF16, st)
        tmps = Rot(S, [sb("tmp%d" % i, [128, 512], F32, st) for i in range(2)])
        fcw = sb("fcw", [128, NC_, 3], F32, st)
        fcb = sb("fcb", [128, NC_], F32, st)
        halo = sb("halo", [128, NC_, 2], F32, st)
        hb, b_hb = sb("hb", [128, 8, 512], BF16, st), S.buf()
        gT, b_gT = sb("gT", [128, NC_, 512], BF16, st), S.buf()
        xc, b_xc = sb("xc", [128, 514], F32, st), S.buf()
        accs = Rot(S, [sb("acc%d" % i, [128, 512], F32, st) for i in range(1)])
        gas = Rot(S, [sb("ga%d" % i, [128, 512], F32, st) for i in range(2)])
        xts = Rot(S, [sb("xt%d" % i, [128, 1024], F32, st) for i in range(2)])
        banks = Rot(S, [ps("pb%d" % i, [128, 512], F32, st) for i in range(8)], psum=True)
        b_wup, b_wdn, b_pers, b_c, b_x = [S.buf() for _ in range(5)]
        b_halo = [S.buf() for _ in range(NC_)]
        wup_v = C.w_up.rearrange("(k p) n -> p k n", p=128)
        for q in range(11):
            S.dma("pool", "wl%d" % (q % 2), lambda e, q=q: e.dma_start(out=wup[:, :, q * 512:(q + 1) * 512], in_=wup_v[:, :, q * 512:(q + 1) * 512]), writes=[b_wup])
        S.dma("sp", "ld1", lambda e: e.dma_start(out=fcw[:], in_=C.fconv_pk), writes=[b_c])
        S.dma("sp", "ld2", lambda e: e.dma_start(out=fcb[:], in_=C.fconvb_pk), writes=[b_c])
        for q in range(2):
            S.dma("pool", "wl%d" % (2 + q), lambda e, q=q: e.dma_start(out=wdn[:, q * 11:(q + 1) * 11, :], in_=C.w_down[q * 1408:(q + 1) * 1408, :].rearrange("(c p) n -> p c n", p=128)), writes=[b_wdn])
        for c in range(NC_):
            S.op("pool", lambda e, c=c: e.memset(halo[:, c, :], 0.0), writes=[b_halo[c]])
        for j in range(NB):
            load_hT_block(C, hb, b_hb, j)
            for c in range(NC_):
                pa, b_pa = banks.next()
                for k in range(8):
                    S.op("pe", lambda e, pa=pa, k=k, c=c: e.matmul(pa[:], wup[:, k, c * 128:(c + 1) * 128], hb[:, k, :], start=(k == 0), stop=(k == 7)), reads=[b_wup, b_hb], writes=[b_pa])
                pv, b_pv = banks.next()
                for k in range(8):
                    S.op("pe", lambda e, pv=pv, k=k, c=c: e.matmul(pv[:], wup[:, k, D_FF + c * 128:D_FF + (c + 1) * 128], hb[:, k, :], start=(k == 0), stop=(k == 7)), reads=[b_wup, b_hb], writes=[b_pv])
                acc, b_acc = accs.next()
                S.op("act", lambda e, pa=pa: e.activation(out=xc[:, 2:514], in_=pa[:], func=AF.Copy), reads=[b_pa], writes=[b_xc])
                S.op("act", lambda e, acc=acc, pa=pa, c=c: e.activation(out=acc[:], in_=pa[:], func=AF.Identity, scale=fcw[:, c, 2:3], bias=fcb[:, c:c + 1]), reads=[b_pa, b_c], writes=[b_acc])
                S.op("pool", lambda e, c=c: e.tensor_copy(xc[:, 0:2], halo[:, c, :]), reads=[b_halo[c]], writes=[b_xc])
                S.op("pool", lambda e, c=c: e.tensor_copy(halo[:, c, :], xc[:, 512:514]), reads=[b_xc], writes=[b_halo[c]])
                for jj in (1, 0):
                    S.op("dve", lambda e, acc=acc, c=c, jj=jj: e.scalar_tensor_tensor(out=acc[:], in0=xc[:, jj:jj + 512], scalar=fcw[:, c, jj:jj + 1], in1=acc[:], op0=ALU.mult, op1=ALU.add),
                         reads=[b_xc, b_acc, b_c], writes=[b_acc])
                ga, b_ga = gas.next()
                S.op("act", lambda e, ga=ga, acc=acc: e.activation(out=ga[:], in_=acc[:], func=AF.Gelu), reads=[b_acc], writes=[b_ga])
                S.op("dve", lambda e, ga=ga, pv=pv, c=c: e.tensor_tensor(out=gT[:, c, :], in0=ga[:], in1=pv[:], op=ALU.mult), reads=[b_ga, b_pv], writes=[b_gT])
            for tt in range(4):
                t = j * 4 + tt
                xt, b_xt = xts.next()
                S.dma("sp", "xl%d" % (t % 2), lambda e, xt=xt, t=t: e.dma_start(out=xt[:], in_=C.out[t * 128:(t + 1) * 128, :]), reads=[b_x], writes=[b_xt])
                for hf in range(2):
                    pw, b_pw = banks.next()
                    for c in range(NC_):
                        S.op("pe", lambda e, pw=pw, c=c, tt=tt, hf=hf: e.matmul(pw[:], gT[:, c, tt * 128:(tt + 1) * 128], wdn[:, c, hf * 512:(hf + 1) * 512], start=(c == 0), stop=(c == NC_ - 1)), reads=[b_gT, b_wdn], writes=[b_pw])
                    tmp, b_tmp = tmps.next()
                    S.op("dve", lambda e, tmp=tmp, pw=pw, hf=hf: e.tensor_tensor(out=tmp[:], in0=pw[:], in1=C.gateB[:, 1, hf * 512:(hf + 1) * 512], op=ALU.mult), reads=[b_pw, b_pers], writes=[b_tmp])
                    S.op("pool", lambda e, xt=xt, tmp=tmp, hf=hf: e.tensor_tensor(out=xt[:, hf * 512:(hf + 1) * 512], in0=xt[:, hf * 512:(hf + 1) * 512], in1=tmp[:], op=ALU.add), reads=[b_xt, b_tmp], writes=[b_xt])
                S.dma("sp", "xs%d" % (t % 2), lambda e, xt=xt, t=t: e.dma_start(out=C.out[t * 128:(t + 1) * 128, :], in_=xt[:]), reads=[b_xt], writes=[S.buf()])
        S.flush()


def _consts():
    c = np.zeros((128, 6, 128), np.float32)
    c[:, 0, :] = np.eye(128, dtype=np.float32)
    c[:, 1, :] = 1.0
    i = np.arange(128)
    c[:, 2, :] = ((i[:, None] % 64) == (i[None, :] % 64)).astype(np.float32)
    same = (i[:, None] // 64) == (i[None, :] // 64)
    c[:, 3, :] = (same & (i[:, None] <= i[None, :])).astype(np.float32)
    c[:, 4, :] = (same & (i[:, None] > i[None, :])).astype(np.float32)
    c[:, 5, :] = same.astype(np.float32)
    return c


def weight_inputs(inp):
    f = np.float32
    d = {}
    d["w_ada"] = np.ascontiguousarray(inp["w_ada"][0], dtype=f)
    d["b_ada_pk"] = np.ascontiguousarray(inp["b_ada"][0].reshape(48, 128).T)
    d["b_ada"] = np.ascontiguousarray(inp["b_ada"].reshape(1, -1))
    d["norm1_pk"] = np.ascontiguousarray(inp["norm1_w"][0].reshape(8, 128).T)
    d["norm2_pk"] = np.ascontiguousarray(inp["norm2_w"][0].reshape(8, 128).T)
    d["consts"] = _consts()
    w_in = np.ascontiguousarray(inp["w_in"][0], dtype=f)
    d["w_in"] = w_in
    perm = np.concatenate([np.arange(32, 64), np.arange(0, 32)])
    cols = []
    for h in range(NH):
        cols.append(C_MQ + h * 192 + 128 + perm)
    cols.append(C_KR + perm)
    d["w_in_rot"] = np.ascontiguousarray(w_in[:, np.concatenate(cols)])
    qn = inp["mla_q_norm_w"][0]
    kn = inp["mla_k_norm_w"][0]
    kvn = inp["mla_kv_norm_w"][0]
    m = np.zeros((128, 8), f)
    m[:, 0] = qn[:128]
    m[:, 1] = np.concatenate([qn[128:192], qn[128:192][perm]])
    m[:, 2] = kn[:128]
    m[:, 3] = np.concatenate([kn[128:192], kn[128:192][perm]])
    m[:, 4] = np.concatenate([np.ones(64), -np.ones(32), np.ones(32)])
    inv = (10000.0 ** (-np.arange(32, dtype=np.float32) / 32)).astype(f)
    m[:, 5] = inv[(np.arange(128) % 64) % 32]
    m[:, 6] = kvn[:128]
    m[:, 7] = kvn[128:]
    d["mlapk"] = m
    d["w_uk"] = np.ascontiguousarray(inp["mla_w_uk"][0], dtype=f)
    d["w_uv"] = np.ascontiguousarray(inp["mla_w_uv"][0], dtype=f)
    d["conv_pk"] = np.ascontiguousarray(inp["dn_conv_w"][0].T.reshape(24, 128, 4).transpose(1, 0, 2))
    d["ab_row"] = np.ascontiguousarray(np.concatenate([inp["dn_a_log"][0], inp["dn_dt_bias"][0]])[None, :].astype(f))
    d["dnw_pk"] = np.ascontiguousarray(inp["dn_norm_w"][0].reshape(128, 1))
    d["w_out_dn"] = np.ascontiguousarray(inp["w_out_dn"][0], dtype=f)
    d["w_out_mla"] = np.ascontiguousarray(inp["w_out_mla"][0], dtype=f)
    d["w_o"] = np.ascontiguousarray(inp["w_o"][0], dtype=f)
    d["w_up"] = np.ascontiguousarray(inp["ffn_w_up"][0], dtype=f)
    d["w_down"] = np.ascontiguousarray(inp["ffn_w_down"][0], dtype=f)
    d["fconv_pk"] = np.ascontiguousarray(inp["ffn_conv_w"][0].T.reshape(22, 128, 3).transpose(1, 0, 2))
    d["fconvb_pk"] = np.ascontiguousarray(inp["ffn_conv_b"][0].reshape(22, 128).T)
    return d


def host_inputs(inp, b):
    d = {}
    d["x"] = np.ascontiguousarray(inp["x"][b])
    d["cT"] = np.ascontiguousarray(inp["c"][b].reshape(8, 128).T)
    d["pos"] = np.ascontiguousarray(inp["positions"][b][None, :].astype(np.int32))
    return d


_NC_CACHE = {}


def kernel(**inputs):
    inp = {k: np.asarray(v) for k, v in inputs.items()}
    B, S_, _ = inp["x"].shape
    NB = S_ // 512
    if NB not in _NC_CACHE:
        _NC_CACHE[NB] = build(NB=NB)
    nc = _NC_CACHE[NB]
    wi = weight_inputs(inp)
    in_maps = [dict(wi, **host_inputs(inp, b)) for b in range(B)]
    res = run_bass_kernel_spmd(nc, in_maps, core_ids=list(range(B)))
    return np.stack([np.asarray(res.results[b]["out"]) for b in range(B)], axis=0).astype(np.float32)
```

```python
import numpy as np
from contextlib import ExitStack
import concourse.bass as bass
import concourse.mybir as mybir
from concourse.bass_utils import run_bass_kernel_spmd

F32 = mybir.dt.float32
BF16 = mybir.dt.bfloat16
I32 = mybir.dt.int32
AF = mybir.ActivationFunctionType
ALU = mybir.AluOpType

D = 1024
SEQ = 4096
NH = 8
D_IN = 8016
D_FF = 2816
EPS = 1e-6
C_Q, C_K, C_V, C_Z, C_AL, C_BE, C_MQ, C_CKV, C_KR, C_GD, C_GM = 0, 1024, 2048, 3072, 4096, 4104, 4112, 5648, 5904, 5968, 6992

ENGS = ("pe", "act", "dve", "pool", "sp")
ENGMAP = {"pe": "tensor", "act": "scalar", "dve": "vector", "pool": "gpsimd", "sp": "sync"}


class Buf:
    __slots__ = ("name", "last_w", "readers", "psum")

    def __init__(self, name="", psum=False):
        self.name = name
        self.psum = psum
        self.last_w = None
        self.readers = []


class Op:
    __slots__ = ("eng", "fn", "idx", "eidx", "deps", "signal", "sigcount", "chan", "chan_count", "is_dma")


class Sched:
    def __init__(self, nc, stack):
        self.nc = nc
        self.stack = stack
        self.esem = {e: stack.enter_context(nc.semaphore("s_" + e)) for e in ENGS}
        self.csem = {}
        self.sig_base = {e: 0 for e in ENGS}
        self.chan_counts = {}
        self.bufs = []
        self.rr_serial = True
        self._reset()
        self.nphase = 0

    def _reset(self):
        self.ops = []
        self.eng_ops = {e: [] for e in ENGS}
        for b in self.bufs:
            b.last_w = None
            b.readers = []
        self.bufs = []

    def buf(self, name="", psum=False):
        b = Buf(name, psum)
        self.bufs.append(b)
        return b

    def _add(self, eng, fn, reads, writes, chan=None):
        op = Op()
        op.eng = eng
        op.fn = fn
        op.idx = len(self.ops)
        op.eidx = len(self.eng_ops[eng])
        op.deps = set()
        op.signal = False
        op.sigcount = 0
        op.chan = chan
        op.is_dma = chan is not None
        op.chan_count = 0
        if chan is not None:
            if chan not in self.csem:
                self.csem[chan] = self.stack.enter_context(self.nc.semaphore("c_" + str(chan)))
            self.chan_counts[chan] = self.chan_counts.get(chan, 0) + 1
            op.chan_count = self.chan_counts[chan]
        for b in reads:
            if b.last_w is not None:
                op.deps.add(b.last_w)
            if b.psum:
                for r in b.readers:
                    if self.ops[r].eng != eng:
                        op.deps.add(r)
        for b in writes:
            if b.last_w is not None:
                op.deps.add(b.last_w)
            for r in b.readers:
                op.deps.add(r)
        for b in reads:
            b.readers.append(op.idx)
        for b in writes:
            b.last_w = op.idx
            b.readers = []
        op.deps.discard(op.idx)
        self.ops.append(op)
        self.eng_ops[eng].append(op)
        return op

    def op(self, eng, fn, reads=(), writes=()):
        return self._add(eng, fn, list(reads), list(writes), None)

    def dma(self, eng, chan, fn, reads=(), writes=()):
        return self._add(eng, fn, list(reads), list(writes), chan)

    def flush(self, final=False):
        nc = self.nc
        ops = self.ops
        if final:
            fin = self._add("sp", None, [], [])
            lastper = {}
            for op in ops:
                if op.is_dma:
                    lastper[op.chan] = op.idx
            fin.deps = set(lastper.values())
        for op in ops:
            keep = set()
            for d in op.deps:
                p = ops[d]
                if p.is_dma or op.is_dma:
                    keep.add(d)
                    continue
                if p.eng == op.eng:
                    if p.eng == "pe":
                        continue
                    pass
                keep.add(d)
            op.deps = keep
            for d in keep:
                if not ops[d].is_dma:
                    ops[d].signal = True
        for e in ENGS:
            for op in reversed(self.eng_ops[e]):
                if not op.is_dma and op.fn is not None:
                    op.signal = True
                    break
        for e in ENGS:
            c = self.sig_base[e]
            for op in self.eng_ops[e]:
                if op.signal:
                    c += 1
                op.sigcount = c
        bar_e = dict(self.sig_base)
        bar_c = {}
        for ch, cnt in self.chan_counts.items():
            n_this = sum(1 for op in ops if op.is_dma and op.chan == ch)
            bar_c[ch] = 16 * (cnt - n_this)
        with nc.Block() as block:
            for e in ENGS:
                eops = self.eng_ops[e]
                if not eops:
                    continue

                def body(eng, eops=eops, e=e):
                    waited = {}
                    if self.nphase > 0:
                        for e2 in ENGS:
                            if e2 != e and bar_e[e2] > 0:
                                eng.wait_ge(self.esem[e2], bar_e[e2])
                                waited[("e", e2)] = bar_e[e2]
                        for ch, v in bar_c.items():
                            if v > 0:
                                eng.wait_ge(self.csem[ch], v)
                                waited[("c", ch)] = v
                    for op in eops:
                        need = {}
                        for d in op.deps:
                            p = ops[d]
                            if p.is_dma:
                                key = ("c", p.chan)
                                val = 16 * p.chan_count
                            else:
                                key = ("e", p.eng)
                                val = p.sigcount
                            if need.get(key, 0) < val:
                                need[key] = val
                        for key, val in need.items():
                            if waited.get(key, 0) >= val:
                                continue
                            waited[key] = val
                            sem = self.csem[key[1]] if key[0] == "c" else self.esem[key[1]]
                            eng.wait_ge(sem, val)
                        if op.fn is None:
                            continue
                        inst = op.fn(eng)
                        if op.is_dma:
                            inst.then_inc(self.csem[op.chan], 16)
                        elif op.signal:
                            inst.then_inc(self.esem[e], 1)

                getattr(block, ENGMAP[e])(body)
        for e in ENGS:
            if self.eng_ops[e]:
                self.sig_base[e] = self.eng_ops[e][-1].sigcount
        self.nphase += 1
        self._reset()


class Rot:
    def __init__(self, S, items, psum=False):
        self.items = [(t, S.buf(psum=psum)) for t in items]
        self.i = 0

    def next(self):
        it = self.items[self.i % len(self.items)]
        self.i += 1
        return it


class Ctx:
    pass


TWO_PI = 2.0 * np.pi
CW1 = 6.28125
CW2 = float(np.float32(TWO_PI - CW1))
MAGIC = 12582912.0


ALL_PHASES = ("norm1", "mla_s", "mla_h", "dn_p", "dn_r", "outp", "ffn")


def build(NB=8, debug=(), phases=ALL_PHASES):
    C = Ctx()
    S_ = NB * 512
    C.NB, C.S_, C.NT = NB, S_, NB * 4
    nc = bass.Bass("TRN2", target_bir_lowering=False)
    C.nc = nc

    def din(name, shape, dt=F32):
        return nc.dram_tensor(name, list(shape), dt, kind="ExternalInput").ap()

    def dscr(name, shape, dt):
        return nc.dram_tensor(name, list(shape), dt, kind="Internal").ap()

    C.x = din("x", [S_, D])
    C.cT = din("cT", [128, 8])
    C.pos = din("pos", [1, S_], I32)
    C.w_ada = din("w_ada", [D, 6 * D])
    C.b_ada_pk = din("b_ada_pk", [128, 48])
    C.b_ada = din("b_ada", [1, 6 * D])
    C.norm1_pk = din("norm1_pk", [128, 8])
    C.norm2_pk = din("norm2_pk", [128, 8])
    C.consts = din("consts", [128, 6, 128])
    C.w_in = din("w_in", [D, D_IN])
    C.w_in_rot = din("w_in_rot", [D, 576])
    C.mlapk = din("mlapk", [128, 8])
    C.w_uk = din("w_uk", [256, 1024])
    C.w_uv = din("w_uv", [256, 1024])
    C.conv_pk = din("conv_pk", [128, 24, 4])
    C.ab_row = din("ab_row", [1, 16])
    C.dnw_pk = din("dnw_pk", [128, 1])
    C.w_out_dn = din("w_out_dn", [1024, 1024])
    C.w_out_mla = din("w_out_mla", [1024, 1024])
    C.w_o = din("w_o", [1024, 1024])
    C.w_up = din("w_up", [1024, 2 * D_FF])
    C.w_down = din("w_down", [D_FF, 1024])
    C.fconv_pk = din("fconv_pk", [128, 22, 3])
    C.fconvb_pk = din("fconvb_pk", [128, 22])
    C.out = nc.dram_tensor("out", [S_, D], F32, kind="ExternalOutput").ap()
    dbg = {}
    for name, shape, dt in debug:
        dbg[name] = nc.dram_tensor(name, list(shape), dt, kind="ExternalOutput").ap()
    C.dbg = dbg
    C.hT_d = dbg["hT"] if "hT" in dbg else dscr("hT_d", [8, 128, S_], BF16)
    C.oTb_d = dbg["oTb"] if "oTb" in dbg else dscr("oTb_d", [8, 128, S_], BF16)
    C.oTa_d = dbg["oTa"] if "oTa" in dbg else dscr("oTa_d", [8, 128, S_], BF16)
    C.qk_d = dbg["qk"] if "qk" in dbg else dscr("qk_d", [16, 128, S_], BF16)
    C.v_d = dbg["v"] if "v" in dbg else dscr("v_d", [8, 128, S_], BF16)
    C.sz_d = dbg["sz"] if "sz" in dbg else dscr("sz_d", [8, 128, S_], BF16)

    with ExitStack() as gst:
        S = Sched(nc, gst)
        C.S = S

        cnt = [0]

        def sb(name, shape, dt=F32, st=gst):
            cnt[0] += 1
            return st.enter_context(nc.sbuf_tensor("sb%d_%s" % (cnt[0], name), list(shape), dt))

        def ps(name, shape, dt=F32, st=gst):
            cnt[0] += 1
            return st.enter_context(nc.psum_tensor("ps%d_%s" % (cnt[0], name), list(shape), dt))

        C.sb, C.ps = sb, ps
        C.cst = sb("cst", [128, 6, 128])
        C.idtb = sb("idtb", [128, 128], BF16)
        C.idt = C.cst[:, 0, :]
        C.ones = C.cst[:, 1, :]
        C.dsum = C.cst[:, 2, :]
        C.modpk = sb("modpk", [128, 4, 8])
        C.s1 = sb("s1", [128, 8])
        C.s2 = sb("s2", [128, 8])
        C.gateB = sb("gateB", [128, 2, D])

        phase0(C)
        if "norm1" in phases:
            norm_phase(C, C.x, C.hT_d, C.s1, 0)
        if "mla_s" in phases:
            with ExitStack() as mst:
                C.ckvnT = sb("ckvnT", [128, 2, S_], BF16, mst)
                C.krr2 = sb("krr2", [128, S_], BF16, mst)
                C.cs = sb("cs", [128, S_], F32, mst)
                C.sskr = sb("sskr", [128, C.NT], F32, mst)
                C.mpk = sb("mpk", [128, 8], F32, mst)
                C.wqn = sb("wqrr", [128, 4], F32, mst)
                mla_shared_phase(C)
                if "mla_h" in phases:
                    mla_heads_phase(C)
        if "dn_p" in phases:
            with ExitStack() as mst:
                C.gates = sb("gates", [128, C.NT, 16], F32, mst)
                dn_proj_phase(C)
                if "dn_r" in phases:
                    dn_rule_phase(C)
        if "outp" in phases:
            outproj_phase(C)
        if "ffn" in phases:
            norm_phase(C, C.out, C.hT_d, C.s2, 2)
            ffn_phase(C)
        S.flush(final=True)
    return nc


def phase0(C):
    nc, S, sb, ps = C.nc, C.S, C.sb, C.ps
    modpk, s1, s2, gateB, dbg = C.modpk, C.s1, C.s2, C.gateB, C.dbg
    with ExitStack() as st:
        ct = sb("ct", [128, 8], F32, st)
        sc = sb("sc", [128, 8], BF16, st)
        scB = sb("scB", [128, 8, 128], BF16, st)
        bpk = sb("bpk", [128, 48], F32, st)
        n1 = sb("n1", [128, 8], F32, st)
        n2 = sb("n2", [128, 8], F32, st)
        brow = sb("brow", [128, 2, D], F32, st)
        wb = [sb("wadab%d" % i, [128, 8, 1024], BF16, st) for i in range(2)]
        pm = ps("pm", [128, 512], F32, st)
        pg = [ps("pg%d" % i, [128, 512], F32, st) for i in range(2)]
        b_id, b_ct, b_sc, b_scB, b_bpk, b_n, b_brow, b_mod, b_s, b_gate = [S.buf() for _ in range(10)]
        b_pm = S.buf(psum=True)
        b_wb = [S.buf(), S.buf()]
        b_pg = [S.buf(psum=True), S.buf(psum=True)]
        S.dma("sp", "ld0", lambda e: e.dma_start(out=C.cst[:], in_=C.consts), writes=[b_id])
        S.dma("sp", "ld1", lambda e: e.dma_start(out=ct[:], in_=C.cT), writes=[b_ct])
        S.dma("sp", "ld2", lambda e: e.dma_start(out=bpk[:], in_=C.b_ada_pk), writes=[b_bpk])
        S.dma("sp", "ld3", lambda e: e.dma_start(out=n1[:], in_=C.norm1_pk), writes=[b_n])
        S.dma("sp", "ld4", lambda e: e.dma_start(out=n2[:], in_=C.norm2_pk), writes=[b_n])
        S.dma("sp", "ld5", lambda e: e.dma_start(out=brow[:, 0, :], in_=C.b_ada[:, 2 * D:3 * D].partition_broadcast(128)), writes=[b_brow])
        S.dma("sp", "ld6", lambda e: e.dma_start(out=brow[:, 1, :], in_=C.b_ada[:, 5 * D:6 * D].partition_broadcast(128)), writes=[b_brow])
        S.op("dve", lambda e: e.tensor_copy(C.idtb[:], C.idt), reads=[b_id], writes=[S.buf()])
        S.op("act", lambda e: e.activation(out=sc[:], in_=ct[:], func=AF.Silu), reads=[b_ct], writes=[b_sc])
        S.op("dve", lambda e: e.tensor_copy(scB[:], sc[:].unsqueeze(2).to_broadcast([128, 8, 128])), reads=[b_sc], writes=[b_scB])
        w_ada_v = C.w_ada.rearrange("(k p) n -> p k n", p=128)
        for g in range(6):
            wt, bw = wb[g % 2], b_wb[g % 2]
            S.dma("pool", "wl%d" % (g % 2), lambda e, wt=wt, g=g: e.dma_start(out=wt[:], in_=w_ada_v[:, :, g * 1024:(g + 1) * 1024]), writes=[bw])
            if g in (0, 1, 3, 4):
                gi = {0: 0, 1: 1, 3: 2, 4: 3}[g]
                for j in range(8):
                    for k in range(8):
                        S.op("pe", lambda e, wt=wt, j=j, k=k: e.matmul(pm[:, j:j + 1], wt[:, k, j * 128:(j + 1) * 128], sc[:, k:k + 1], start=(k == 0), stop=(k == 7)),
                             reads=[bw, b_sc], writes=[b_pm])
                S.op("dve", lambda e, gi=gi, g=g: e.tensor_tensor(out=modpk[:, gi, :], in0=pm[:, 0:8], in1=bpk[:, g * 8:(g + 1) * 8], op=ALU.add), reads=[b_pm, b_bpk], writes=[b_mod])
            else:
                gi = 0 if g == 2 else 1
                for hf in range(2):
                    pgt, bpg = pg[hf], b_pg[hf]
                    for k in range(8):
                        S.op("pe", lambda e, wt=wt, pgt=pgt, hf=hf, k=k: e.matmul(pgt[:], scB[:, k, :], wt[:, k, hf * 512:(hf + 1) * 512], start=(k == 0), stop=(k == 7)),
                             reads=[bw, b_scB], writes=[bpg])
                    S.op("dve", lambda e, pgt=pgt, gi=gi, hf=hf: e.tensor_tensor(out=gateB[:, gi, hf * 512:(hf + 1) * 512], in0=pgt[:], in1=brow[:, gi, hf * 512:(hf + 1) * 512], op=ALU.add),
                         reads=[bpg, b_brow], writes=[b_gate])
        S.op("dve", lambda e: e.scalar_tensor_tensor(out=s1[:], in0=modpk[:, 1, :], scalar=1.0, in1=n1[:], op0=ALU.add, op1=ALU.mult), reads=[b_mod, b_n], writes=[b_s])
        S.op("dve", lambda e: e.scalar_tensor_tensor(out=s2[:], in0=modpk[:, 3, :], scalar=1.0, in1=n2[:], op0=ALU.add, op1=ALU.mult), reads=[b_mod, b_n], writes=[b_s])
        if "mod" in dbg:
            S.dma("sp", "st0", lambda e: e.dma_start(out=dbg["mod"][:, 0:32], in_=modpk[:].rearrange("p a b -> p (a b)")), reads=[b_mod])
            S.dma("sp", "st1", lambda e: e.dma_start(out=dbg["mod"][:, 32:40], in_=s1[:]), reads=[b_s])
            S.dma("sp", "st2", lambda e: e.dma_start(out=dbg["gate"], in_=gateB[0:1, :, :].rearrange("p a b -> p (a b)")), reads=[b_gate])
        S.flush()


def norm_phase(C, xsrc, hT_d, svec, shift_idx):
    nc, S, sb, ps = C.nc, C.S, C.sb, C.ps
    idt, modpk = C.idt, C.modpk
    with ExitStack() as st:
        xt = Rot(S, [sb("xt%d" % i, [128, D], F32, st) for i in range(3)])
        xn = Rot(S, [sb("xn%d" % i, [128, D], F32, st) for i in range(3)])
        junk = sb("junk", [128, D], BF16, st)
        b_junk = S.buf()
        ssq = Rot(S, [sb("ssq%d" % i, [128, 2], F32, st) for i in range(3)])
        hTb = Rot(S, [sb("hTb%d" % i, [128, 8, 512], BF16, st) for i in range(2)])
        hb2_bufs = [S.buf(), S.buf()]
        pTa = Rot(S, [ps("pTa%d" % i, [128, 4, 128], F32, st) for i in range(2)], psum=True)
        pTb = Rot(S, [ps("pTb%d" % i, [128, 4, 128], F32, st) for i in range(2)], psum=True)
        b_x = S.buf()
        state = {"cur": None}

        def stage1(t):
            xt_t, b_xt = xt.next()
            S.dma("sp", "xl%d" % (t % 3), lambda e: e.dma_start(out=xt_t[:], in_=xsrc[t * 128:(t + 1) * 128, :]), reads=[b_x], writes=[b_xt])
            sq, b_sq = ssq.next()
            S.op("act", lambda e: e.activation(out=junk[:], in_=xt_t[:], func=AF.Square, accum_out=sq[:, 0:1]), reads=[b_xt], writes=[b_junk, b_sq])
            S.op("act", lambda e: e.activation(out=sq[:, 1:2], in_=sq[:, 0:1], func=AF.Ln, scale=1.0 / D, bias=EPS), reads=[b_sq], writes=[b_sq])
            S.op("act", lambda e: e.activation(out=sq[:, 1:2], in_=sq[:, 1:2], func=AF.Exp, scale=-0.5), reads=[b_sq], writes=[b_sq])
            xn_t, b_xn = xn.next()
            S.op("dve", lambda e: e.tensor_scalar(xn_t[:], xt_t[:], sq[:, 1:2], None, ALU.mult), reads=[b_xt, b_sq], writes=[b_xn])
            return xn_t, b_xn

        def stage2(t, xn_t, b_xn):
            pa, b_pa = pTa.next()
            pb, b_pb = pTb.next()
            for k in range(8):
                dst, bd = (pa, b_pa) if k < 4 else (pb, b_pb)
                S.op("pe", lambda e, k=k, dst=dst: e.transpose(dst[:, k % 4, :], xn_t[:, k * 128:(k + 1) * 128], idt), reads=[b_xn], writes=[bd])
            if t % 4 == 0:
                state["cur"] = hTb.next() + (hb2_bufs[hTb.i % 2],)
            hb, b_hb, b_hb2 = state["cur"]
            tt = t % 4
            for k in range(8):
                if k < 4:
                    S.op("dve", lambda e, k=k: e.tensor_scalar(hb[:, k, tt * 128:(tt + 1) * 128], pa[:, k, :], svec[:, k:k + 1], modpk[:, shift_idx, k:k + 1], ALU.mult, ALU.add),
                         reads=[b_pa], writes=[b_hb])
                else:
                    S.op("act", lambda e, k=k: e.activation(out=hb[:, k, tt * 128:(tt + 1) * 128], in_=pb[:, k - 4, :], func=AF.Identity, scale=svec[:, k:k + 1], bias=modpk[:, shift_idx, k:k + 1]),
                         reads=[b_pb], writes=[b_hb2])
            if tt == 3:
                blk = t // 4
                S.dma("sp", "hs%d" % (blk % 2), lambda e: e.dma_start(out=hT_d[:, :, blk * 512:(blk + 1) * 512].rearrange("k p t -> p k t"), in_=hb[:]), reads=[b_hb, b_hb2], writes=[S.buf()])

        nxt = stage1(0)
        for t in range(C.NT):
            cur = nxt
            if t + 1 < C.NT:
                nxt = stage1(t + 1)
            stage2(t, *cur)
        S.flush()


def load_hT_block(C, hbuf, b_h, blk, eng="sp"):
    C.S.dma(eng, "hl%d" % (C._hl % 2), lambda e: e.dma_start(out=hbuf[:], in_=C.hT_d[:, :, blk * 512:(blk + 1) * 512].rearrange("k p t -> p k t")), writes=[b_h])
    C._hl += 1


def mla_shared_phase(C):
    nc, S, sb, ps = C.nc, C.S, C.sb, C.ps
    C._hl = 0
    with ExitStack() as st:
        hbs = Rot(S, [sb("hb%d" % i, [128, 8, 512], BF16, st) for i in range(2)])
        wkv = sb("wkv", [128, 8, 384], BF16, st)
        posi = sb("posi", [128, 512], I32, st)
        ang = sb("ang", [128, 512], F32, st)
        kf = sb("kf", [128, 512], F32, st)
        rc = sb("rc", [128, 512], F32, st)
        tmpw = sb("tmpw", [128, 512], F32, st)
        sq = [sb("sqc%d" % i, [128, 512], F32, st) for i in range(2)]
        lnb = sb("lnb", [128, 512], F32, st)
        rstd = sb("rstd", [128, 512], F32, st)
        tk = sb("tk", [128, 512], F32, st)
        sqkr = sb("sqkr", [64, 512], F32, st)
        banks = [ps("pb%d" % i, [128, 512], F32, st) for i in range(7)]
        pc = [banks[0], banks[1]]
        pk, pss, pkr, psk = banks[2], banks[3], banks[4], banks[5]
        b_w, b_mpk, b_wqn, b_posi, b_ang, b_kf, b_rc, b_tmpw, b_ln, b_rstd, b_tk, b_sqkr, b_cs, b_ckv, b_krr, b_sskr = [S.buf() for _ in range(16)]
        b_pk, b_pss, b_pkr, b_psk = [S.buf(psum=True) for _ in range(4)]
        b_pc = [S.buf(psum=True), S.buf(psum=True)]
        b_sq = [S.buf(), S.buf()]
        w_in_v = C.w_in.rearrange("(k p) n -> p k n", p=128)
        w_rot_v = C.w_in_rot.rearrange("(k p) n -> p k n", p=128)
        S.dma("pool", "wl0", lambda e: e.dma_start(out=wkv[:, :, 0:320], in_=w_in_v[:, :, C_CKV:C_CKV + 320]), writes=[b_w])
        S.dma("pool", "wl1", lambda e: e.dma_start(out=wkv[:, :, 320:384], in_=w_rot_v[:, :, 512:576]), writes=[b_w])
        S.dma("sp", "ld1", lambda e: e.dma_start(out=C.mpk[:], in_=C.mlapk), writes=[b_mpk])
        mpk, wqn = C.mpk, C.wqn
        S.op("dve", lambda e: e.tensor_copy(wqn[:, 0:1], mpk[:, 0:1]), reads=[b_mpk], writes=[b_wqn])
        S.op("dve", lambda e: e.tensor_tensor(out=wqn[:, 1:2], in0=mpk[:, 1:2], in1=mpk[:, 4:5], op=ALU.mult), reads=[b_mpk], writes=[b_wqn])
        S.op("dve", lambda e: e.tensor_copy(wqn[:, 2:3], mpk[:, 2:3]), reads=[b_mpk], writes=[b_wqn])
        S.op("dve", lambda e: e.tensor_tensor(out=wqn[:, 3:4], in0=mpk[:, 3:4], in1=mpk[:, 4:5], op=ALU.mult), reads=[b_mpk], writes=[b_wqn])
        for j in range(C.NB):
            hb, b_hb = hbs.next()
            load_hT_block(C, hb, b_hb, j)
            bs = slice(j * 512, (j + 1) * 512)
            S.dma("sp", "ld2", lambda e, bs=bs: e.dma_start(out=posi[:], in_=C.pos[:, bs].partition_broadcast(128)), writes=[b_posi])
            S.op("dve", lambda e: e.tensor_copy(ang[:], posi[:]), reads=[b_posi], writes=[b_ang])
            S.op("dve", lambda e: e.tensor_scalar(ang[:], ang[:], mpk[:, 5:6], None, ALU.mult), reads=[b_ang, b_mpk], writes=[b_ang])
            S.op("dve", lambda e: e.tensor_scalar(kf[:], ang[:], 1.0 / TWO_PI, MAGIC, ALU.mult, ALU.add), reads=[b_ang], writes=[b_kf])
            S.op("dve", lambda e: e.tensor_scalar(kf[:], kf[:], -MAGIC, None, ALU.add), reads=[b_kf], writes=[b_kf])
            S.op("dve", lambda e: e.scalar_tensor_tensor(out=ang[:], in0=kf[:], scalar=-CW1, in1=ang[:], op0=ALU.mult, op1=ALU.add), reads=[b_kf, b_ang], writes=[b_ang])
            S.op("dve", lambda e: e.scalar_tensor_tensor(out=ang[:], in0=kf[:], scalar=-CW2, in1=ang[:], op0=ALU.mult, op1=ALU.add), reads=[b_kf, b_ang], writes=[b_ang])
            S.op("dve", lambda e: e.tensor_scalar(ang[:], ang[:], np.pi, -np.pi, ALU.min, ALU.max), reads=[b_ang], writes=[b_ang])
            S.op("dve", lambda e: e.tensor_scalar(kf[:], ang[:], np.pi / 2, -TWO_PI, ALU.is_gt, ALU.mult), reads=[b_ang], writes=[b_kf])
            S.op("dve", lambda e: e.scalar_tensor_tensor(out=rc[:], in0=ang[:], scalar=np.pi / 2, in1=kf[:], op0=ALU.add, op1=ALU.add), reads=[b_ang, b_kf], writes=[b_rc])
            S.op("dve", lambda e: e.tensor_scalar(rc[:], rc[:], np.pi, -np.pi, ALU.min, ALU.max), reads=[b_rc], writes=[b_rc])
            S.op("act", lambda e, bs=bs: e.activation(out=C.cs[0:64, bs], in_=rc[0:64, :], func=AF.Sin), reads=[b_rc], writes=[b_cs])
            S.op("act", lambda e, bs=bs: e.activation(out=C.cs[64:128, bs], in_=ang[64:128, :], func=AF.Sin), reads=[b_ang], writes=[b_cs])
            for r in range(2):
                for k in range(8):
                    S.op("pe", lambda e, r=r, k=k, hb=hb: e.matmul(pc[r][:], wkv[:, k, r * 128:(r + 1) * 128], hb[:, k, :], start=(k == 0), stop=(k == 7)), reads=[b_w, b_hb], writes=[b_pc[r]])
            for k in range(8):
                S.op("pe", lambda e, k=k, hb=hb: e.matmul(pk[:], wkv[:, k, 256:384], hb[:, k, :], start=(k == 0), stop=(k == 7)), reads=[b_w, b_hb], writes=[b_pk])
            for r in range(2):
                S.op("act", lambda e, r=r: e.activation(out=sq[r][:], in_=pc[r][:], func=AF.Square), reads=[b_pc[r]], writes=[b_sq[r]])
            for r in range(2):
                S.op("pe", lambda e, r=r: e.matmul(pss[:], C.ones, sq[r][:], start=(r == 0), stop=(r == 1)), reads=[b_sq[r]], writes=[b_pss])
            S.op("act", lambda e: e.activation(out=lnb[:], in_=pss[:], func=AF.Ln, scale=1.0 / 256, bias=EPS), reads=[b_pss], writes=[b_ln])
            S.op("act", lambda e: e.activation(out=rstd[:], in_=lnb[:], func=AF.Exp, scale=-0.5), reads=[b_ln], writes=[b_rstd])
            for r in range(2):
                S.op("dve", lambda e, r=r, bs=bs: e.scalar_tensor_tensor(out=C.ckvnT[:, r, bs], in0=pc[r][:], scalar=mpk[:, 6 + r:7 + r], in1=rstd[:], op0=ALU.mult, op1=ALU.mult),
                     reads=[b_pc[r], b_rstd, b_mpk], writes=[b_ckv])
            S.op("dve", lambda e, bs=bs: e.scalar_tensor_tensor(out=tk[:], in0=pk[:], scalar=wqn[:, 3:4], in1=C.cs[:, bs], op0=ALU.mult, op1=ALU.mult), reads=[b_pk, b_cs, b_wqn], writes=[b_tk])
            S.op("pe", lambda e: e.matmul(pkr[:], C.dsum, tk[:], start=True, stop=True), reads=[b_tk], writes=[b_pkr])
            S.op("act", lambda e, bs=bs: e.activation(out=C.krr2[:, bs], in_=pkr[:], func=AF.Copy), reads=[b_pkr], writes=[b_krr])
            S.op("act", lambda e: e.activation(out=sqkr[:], in_=pk[0:64, :], func=AF.Square), reads=[b_pk], writes=[b_sqkr])
            for tt in range(4):
                t = j * 4 + tt
                S.op("pe", lambda e, t=t, tt=tt: e.matmul(psk[:, t:t + 1], sqkr[:, tt * 128:(tt + 1) * 128], C.ones[0:64, 0:1], start=True, stop=True), reads=[b_sqkr], writes=[b_psk])
        S.op("dve", lambda e: e.tensor_copy(C.sskr[:], psk[:, 0:C.NT]), reads=[b_psk], writes=[b_sskr])
        if "ckvnT" in C.dbg:
            S.dma("sp", "st0", lambda e: e.dma_start(out=C.dbg["ckvnT"].rearrange("r p t -> p r t"), in_=C.ckvnT[:]), reads=[b_ckv])
            S.dma("sp", "st1", lambda e: e.dma_start(out=C.dbg["krr2"], in_=C.krr2[:]), reads=[b_krr])
            S.dma("sp", "st2", lambda e: e.dma_start(out=C.dbg["cs"], in_=C.cs[:]), reads=[b_cs])
            S.dma("sp", "st3", lambda e: e.dma_start(out=C.dbg["sskr"], in_=C.sskr[:]), reads=[b_sskr])
        S.flush()


def mla_heads_phase(C):
    nc, S, sb, ps = C.nc, C.S, C.sb, C.ps
    NB = C.NB
    SCALE = 192.0 ** -0.5
    with ExitStack() as st:
        hbs = Rot(S, [sb("hb%d" % i, [128, 8, 512], BF16, st) for i in range(2)])
        wqs = Rot(S, [sb("wq%d" % i, [128, 8, 256], BF16, st) for i in range(2)])
        wuk = sb("wuk", [128, 2, 1024], BF16, st)
        wuv = sb("wuv", [128, 2, 1024], BF16, st)
        QTn = sb("QTn", [128, C.S_], BF16, st)
        QTr = sb("QTr", [128, C.S_], BF16, st)
        KTn2 = [sb("KTn%d" % i, [128, C.S_], BF16, st) for i in range(2)]
        Vh2 = [sb("Vh%d" % i, [128, C.NT, 128], BF16, st) for i in range(2)]
        sk2 = [sb("sk%d" % i, [128, C.NT], F32, st) for i in range(2)]
        sqA = sb("sqA", [128, 512], F32, st)
        sqB = sb("sqB", [64, 512], F32, st)
        sqD = sb("sqD", [128, 512], F32, st)
        lnb = sb("lnb", [128, 512], F32, st)
        lnb2 = sb("lnb2", [128, 512], F32, st)
        rq = sb("rq", [128, 512], F32, st)
        tq = sb("tq", [128, 512], F32, st)
        tmp4 = sb("tmp4", [128, 8], F32, st)
        PTs = Rot(S, [sb("PT%d" % i, [128, 512], BF16, st) for i in range(4)])
        accLs = Rot(S, [sb("accL%d" % i, [128, 512], F32, st) for i in range(2)])
        rl = sb("rl", [128, 512], F32, st)
        oT = Rot(S, [sb("oT%d" % i, [128, 512], BF16, st) for i in range(2)])
        banks = [ps("pb%d" % i, [128, 512], F32, st) for i in range(8)]
        pool3 = Rot(S, banks[0:4], psum=True)
        sts = Rot(S, banks[4:6], psum=True)
        pO, b_pO = banks[6], S.buf(psum=True)
        pL, b_pL = banks[7], S.buf(psum=True)
        psk, b_psk = pL, b_pL
        b_wuk, b_sqA, b_sqB, b_sqD, b_ln, b_ln2, b_rq, b_tq, b_tmp4, b_rl = [S.buf() for _ in range(10)]
        b_Q = [S.buf() for _ in range(NB)]
        b_K = [[S.buf() for _ in range(NB)] for _ in range(2)]
        b_V = [[S.buf() for _ in range(NB)] for _ in range(2)]
        b_pers = S.buf()
        w_in_v = C.w_in.rearrange("(k p) n -> p k n", p=128)
        w_rot_v = C.w_in_rot.rearrange("(k p) n -> p k n", p=128)
        S.dma("pool", "wl0", lambda e: e.dma_start(out=wuk[:], in_=C.w_uk.rearrange("(r p) n -> p r n", p=128)), writes=[b_wuk])
        S.dma("pool", "wl1", lambda e: e.dma_start(out=wuv[:], in_=C.w_uv.rearrange("(r p) n -> p r n", p=128)), writes=[b_wuk])
        wqn = C.wqn
        wq_cur = {}

        def proj(h, j):
            hp = h % 2
            KTn, Vh, sk = KTn2[hp], Vh2[hp], sk2[hp]
            if j == 0:
                wq, b_wq = wqs.next()
                wq_cur[h] = (wq, b_wq)
                c0 = C_MQ + h * 192
                S.dma("pool", "wq%d" % hp, lambda e: e.dma_start(out=wq[:, :, 0:192], in_=w_in_v[:, :, c0:c0 + 192]), writes=[b_wq])
                S.dma("pool", "wr%d" % hp, lambda e: e.dma_start(out=wq[:, :, 192:256], in_=w_rot_v[:, :, h * 64:(h + 1) * 64]), writes=[b_wq])
            wq, b_wq = wq_cur[h]
            hs = slice(h * 128, (h + 1) * 128)
            hb, b_hb = hbs.next()
            load_hT_block(C, hb, b_hb, j)
            bs = slice(j * 512, (j + 1) * 512)
            yield
            pA, b_pA = pool3.next()
            for k in range(8):
                S.op("pe", lambda e, k=k: e.matmul(pA[:], wq[:, k, 0:128], hb[:, k, :], start=(k == 0), stop=(k == 7)), reads=[b_wq, b_hb], writes=[b_pA])
                if k % 4 == 3:
                    yield
            pB, b_pB = pool3.next()
            for k in range(8):
                S.op("pe", lambda e, k=k: e.matmul(pB[:], wq[:, k, 128:256], hb[:, k, :], start=(k == 0), stop=(k == 7)), reads=[b_wq, b_hb], writes=[b_pB])
                if k % 4 == 3:
                    yield
            pD, b_pD = pool3.next()
            for r in range(2):
                S.op("pe", lambda e, r=r: e.matmul(pD[:], wuk[:, r, hs], C.ckvnT[:, r, bs], start=(r == 0), stop=(r == 1)), reads=[b_wuk, b_pers], writes=[b_pD])
            yield
            S.op("act", lambda e: e.activation(out=sqA[:], in_=pA[:], func=AF.Square), reads=[b_pA], writes=[b_sqA])
            yield
            S.op("act", lambda e: e.activation(out=sqB[:], in_=pB[0:64, :], func=AF.Square), reads=[b_pB], writes=[b_sqB])
            yield
            S.op("act", lambda e: e.activation(out=sqD[:], in_=pD[:], func=AF.Square), reads=[b_pD], writes=[b_sqD])
            yield
            S.op("act", lambda e: e.activation(out=KTn[:, bs], in_=pD[:], func=AF.Copy, scale=wqn[:, 2:3]), reads=[b_pD, b_pers], writes=[b_K[hp][j]])
            yield
            pS, b_pS = pool3.next()
            S.op("pe", lambda e: e.matmul(pS[:], C.ones, sqA[:], start=True, stop=False), reads=[b_sqA], writes=[b_pS])
            S.op("pe", lambda e: e.matmul(pS[:], C.ones[0:64, :], sqB[:], start=False, stop=True), reads=[b_sqB], writes=[b_pS])
            for tt in range(4):
                S.op("pe", lambda e, tt=tt: e.matmul(psk[:, tt:tt + 1], sqD[:, tt * 128:(tt + 1) * 128], C.ones[:, 0:1], start=True, stop=True), reads=[b_sqD], writes=[b_psk])
            yield
            S.op("act", lambda e: e.activation(out=lnb[:], in_=pS[:], func=AF.Ln, scale=1.0 / 192, bias=EPS), reads=[b_pS], writes=[b_ln])
            yield
            S.op("act", lambda e: e.activation(out=rq[:], in_=lnb[:], func=AF.Exp, scale=-0.5), reads=[b_ln], writes=[b_rq])
            yield
            S.op("dve", lambda e: e.scalar_tensor_tensor(out=QTn[:, bs], in0=pA[:], scalar=wqn[:, 0:1], in1=rq[:], op0=ALU.mult, op1=ALU.mult), reads=[b_pA, b_rq, b_pers], writes=[b_Q[j]])
            yield
            S.op("dve", lambda e: e.scalar_tensor_tensor(out=tq[:], in0=pB[:], scalar=wqn[:, 1:2], in1=C.cs[:, bs], op0=ALU.mult, op1=ALU.mult), reads=[b_pB, b_pers], writes=[b_tq])
            yield
            S.op("dve", lambda e: e.tensor_tensor(out=QTr[:, bs], in0=tq[:], in1=rq[:], op=ALU.mult), reads=[b_tq, b_rq], writes=[b_Q[j]])
            yield
            S.op("dve", lambda e: e.tensor_tensor(out=tmp4[:, 0:4], in0=psk[:, 0:4], in1=C.sskr[:, 4 * j:4 * j + 4], op=ALU.add), reads=[b_psk, b_pers], writes=[b_tmp4])
            S.op("act", lambda e: e.activation(out=tmp4[:, 4:8], in_=tmp4[:, 0:4], func=AF.Ln, scale=1.0 / 192, bias=EPS), reads=[b_tmp4], writes=[b_tmp4])
            S.op("act", lambda e: e.activation(out=sk[:, 4 * j:4 * j + 4], in_=tmp4[:, 4:8], func=AF.Exp, scale=-0.5, bias=float(np.log(SCALE))), reads=[b_tmp4], writes=[b_K[hp][j]])
            yield
            pV, b_pV = pool3.next()
            for tt in range(4):
                if tt == 2:
                    yield
                for r in range(2):
                    S.op("pe", lambda e, tt=tt, r=r: e.matmul(pV[:, tt * 128:(tt + 1) * 128], C.ckvnT[:, r, j * 512 + tt * 128:j * 512 + (tt + 1) * 128], wuv[:, r, hs], start=(r == 0), stop=(r == 1)),
                         reads=[b_wuk, b_pers], writes=[b_pV])
            yield
            S.op("dve", lambda e: e.tensor_copy(Vh[:, 4 * j:4 * j + 4, :], pV[:].rearrange("p (a b) -> p a b", b=128)), reads=[b_pV], writes=[b_V[hp][j]])
            yield

        def attn(h, j, pg):
            hp = h % 2
            KTn, Vh, sk = KTn2[hp], Vh2[hp], sk2[hp]
            bs = slice(j * 512, (j + 1) * 512)
            nkt = 4 * j + 4
            accL, b_accL = accLs.next()

            def issue_st(kt):
                i = kt - 4 * j
                q0 = 128 * i if i > 0 else 0
                stp, b_st = sts.next()
                kb = kt // 4
                S.op("pe", lambda e: e.matmul(stp[:, q0:512], KTn[:, kt * 128:(kt + 1) * 128], QTn[:, j * 512 + q0:(j + 1) * 512], start=True, stop=False),
                     reads=[b_K[hp][kb], b_Q[j]], writes=[b_st])
                S.op("pe", lambda e: e.matmul(stp[:, q0:512], C.krr2[:, kt * 128:(kt + 1) * 128], QTr[:, j * 512 + q0:(j + 1) * 512], start=False, stop=True),
                     reads=[b_pers, b_Q[j]], writes=[b_st])
                return stp, b_st, q0, i, kb

            nxt = issue_st(0)
            for kt in range(nkt):
                stp, b_st, q0, i, kb = nxt
                if kt + 1 < nkt:
                    nxt = issue_st(kt + 1)
                PT, b_PT = PTs.next()
                S.op("act", lambda e, PT=PT, stp=stp, kt=kt, q0=q0: e.activation(out=PT[:, q0:512], in_=stp[:, q0:512], func=AF.Exp, scale=sk[:, kt:kt + 1]), reads=[b_st, b_K[hp][kb]], writes=[b_PT])
                if i >= 0:
                    S.op("pool", lambda e, PT=PT, q0=q0: e.memset(PT[64:128, q0:q0 + 64], 0.0), reads=[], writes=[b_PT])
                S.op("pe", lambda e, PT=PT, kt=kt, q0=q0: e.matmul(pO[:, q0:512], Vh[:, kt, :], PT[:, q0:512], start=(kt == 0), stop=(kt == nkt - 1)), reads=[b_PT, b_V[hp][kb]], writes=[b_pO])
                if kt == 0:
                    S.op("dve", lambda e, PT=PT: e.tensor_copy(accL[:], PT[:]), reads=[b_PT], writes=[b_accL])
                else:
                    S.op("dve", lambda e, PT=PT, q0=q0: e.tensor_tensor(out=accL[:, q0:512], in0=accL[:, q0:512], in1=PT[:, q0:512], op=ALU.add), reads=[b_PT, b_accL], writes=[b_accL])
                if pg is not None:
                    next(pg, None)
                    next(pg, None)
            if pg is not None:
                for _ in pg:
                    pass
            S.op("pe", lambda e: e.matmul(pL[:], C.ones, accL[:], start=True, stop=True), reads=[b_accL], writes=[b_pL])
            S.op("act", lambda e: e.activation(out=lnb2[:], in_=pL[:], func=AF.Ln), reads=[b_pL], writes=[b_ln2])
            S.op("act", lambda e: e.activation(out=rl[:], in_=lnb2[:], func=AF.Exp, scale=-1.0), reads=[b_ln2], writes=[b_rl])
            o_t, b_o = oT.next()
            S.op("dve", lambda e: e.tensor_tensor(out=o_t[:], in0=pO[:], in1=rl[:], op=ALU.mult), reads=[b_pO, b_rl], writes=[b_o])
            S.dma("sp", "os%d" % (j % 2), lambda e: e.dma_start(out=C.oTb_d[h, :, bs], in_=o_t[:]), reads=[b_o], writes=[S.buf()])

        items = [(h, j) for h in range(NH) for j in range(NB)]
        for _ in proj(*items[0]):
            pass
        for idx, it in enumerate(items):
            pg = proj(*items[idx + 1]) if idx + 1 < len(items) else None
            attn(it[0], it[1], pg)
        S.flush()


def dn_proj_phase(C):
    nc, S, sb, ps = C.nc, C.S, C.sb, C.ps
    NB = C.NB
    C._hl = 0
    with ExitStack() as st:
        hbs = Rot(S, [sb("hb%d" % i, [128, 8, 512], BF16, st) for i in range(2)])
        wdn = sb("wdn", [128, 8, 4096], BF16, st)
        wab = sb("wab", [128, 8, 16], BF16, st)
        cw = sb("cw", [128, 24, 4], F32, st)
        halo = sb("halo", [128, 24, 3], F32, st)
        xcs = Rot(S, [sb("xc%d" % i, [128, 515], F32, st) for i in range(3)])
        accs = Rot(S, [sb("acc%d" % i, [128, 512], F32, st) for i in range(3)])
        yqk = sb("yqk", [128, 16, 512], F32, st)
        qkT = sb("qkT", [128, 16, 512], BF16, st)
        yv = sb("yv", [128, 8, 512], BF16, st)
        szb = sb("szb", [128, 8, 512], BF16, st)
        sqs = Rot(S, [sb("sq%d" % i, [128, 512], F32, st) for i in range(2)])
        lnbs = Rot(S, [sb("lnb%d" % i, [128, 512], F32, st) for i in range(2)])
        rrs = Rot(S, [sb("rr%d" % i, [128, 512], F32, st) for i in range(2)])
        abrow = sb("abrow", [128, 16], F32, st)
        nA = sb("nA", [128, 8], F32, st)
        gz = sb("gz", [128, 4, 16], F32, st)
        banks = [ps("pb%d" % i, [128, 512], F32, st) for i in range(7)]
        pps = Rot(S, banks[0:4], psum=True)
        psss = Rot(S, banks[4:6], psum=True)
        pg, b_pg = banks[6], S.buf(psum=True)
        b_w, b_cw, b_ab, b_nA, b_gz, b_yv, b_sz, b_qkT, b_gates = [S.buf() for _ in range(9)]
        b_halo = [S.buf() for _ in range(24)]
        b_yqk = [S.buf() for _ in range(16)]
        w_in_v = C.w_in.rearrange("(k p) n -> p k n", p=128)
        for q4 in range(4):
            S.dma("pool", "wl%d" % (q4 % 2), lambda e, q4=q4: e.dma_start(out=wdn[:, :, q4 * 1024:(q4 + 1) * 1024], in_=w_in_v[:, :, q4 * 1024:(q4 + 1) * 1024]), writes=[b_w])
        S.dma("pool", "wl0", lambda e: e.dma_start(out=wab[:], in_=w_in_v[:, :, C_AL:C_AL + 16]), writes=[b_w])
        S.dma("sp", "ld1", lambda e: e.dma_start(out=cw[:], in_=C.conv_pk), writes=[b_cw])
        S.dma("sp", "ld2", lambda e: e.dma_start(out=abrow[:], in_=C.ab_row.partition_broadcast(128)), writes=[b_ab])
        S.op("act", lambda e: e.activation(out=nA[:], in_=abrow[:, 0:8], func=AF.Exp), reads=[b_ab], writes=[b_nA])
        S.op("dve", lambda e: e.tensor_scalar(nA[:], nA[:], -1.0, None, ALU.mult), reads=[b_nA], writes=[b_nA])
        for c in range(24):
            S.op("pool", lambda e, c=c: e.memset(halo[:, c, :], 0.0), writes=[b_halo[c]])
        for j in range(NB):
            hb, b_hb = hbs.next()
            load_hT_block(C, hb, b_hb, j)
            bs = slice(j * 512, (j + 1) * 512)
            def conv_stage1(c, hb, b_hb):
                pp, b_pp = pps.next()
                for k in range(8):
                    S.op("pe", lambda e, k=k: e.matmul(pp[:], wdn[:, k, c * 128:(c + 1) * 128], hb[:, k, :], start=(k == 0), stop=(k == 7)), reads=[b_w, b_hb], writes=[b_pp])
                if c >= 24:
                    return ("z", pp, b_pp)
                xc, b_xc = xcs.next()
                acc, b_acc = accs.next()
                S.op("act", lambda e: e.activation(out=xc[:, 3:515], in_=pp[:], func=AF.Copy), reads=[b_pp], writes=[b_xc])
                S.op("act", lambda e: e.activation(out=acc[:], in_=pp[:], func=AF.Copy, scale=cw[:, c, 3:4]), reads=[b_pp, b_cw], writes=[b_acc])
                S.op("pool", lambda e: e.tensor_copy(xc[:, 0:3], halo[:, c, :]), reads=[b_halo[c]], writes=[b_xc])
                S.op("pool", lambda e: e.tensor_copy(halo[:, c, :], xc[:, 512:515]), reads=[b_xc], writes=[b_halo[c]])
                for jj in (2, 1, 0):
                    S.op("dve", lambda e, jj=jj: e.scalar_tensor_tensor(out=acc[:], in0=xc[:, jj:jj + 512], scalar=cw[:, c, jj:jj + 1], in1=acc[:], op0=ALU.mult, op1=ALU.add),
                         reads=[b_xc, b_acc, b_cw], writes=[b_acc])
                return ("c", acc, b_acc)

            def conv_stage2(c, kind, src, b_src):
                if kind == "z":
                    S.op("act", lambda e: e.activation(out=szb[:, c - 24, :], in_=src[:], func=AF.Silu), reads=[b_src], writes=[b_sz])
                elif c < 16:
                    S.op("act", lambda e: e.activation(out=yqk[:, c, :], in_=src[:], func=AF.Silu), reads=[b_src], writes=[b_yqk[c]])
                else:
                    S.op("act", lambda e: e.activation(out=yv[:, c - 16, :], in_=src[:], func=AF.Silu), reads=[b_src], writes=[b_yv])

            prev = None
            for c in range(32):
                cur = (c,) + conv_stage1(c, hb, b_hb)
                if prev is not None:
                    conv_stage2(*prev)
                prev = cur
            conv_stage2(*prev)
            for tt in range(4):
                for k in range(8):
                    S.op("pe", lambda e, hb=hb, tt=tt, k=k: e.matmul(pg[:, tt * 16:(tt + 1) * 16], hb[:, k, tt * 128:(tt + 1) * 128], wab[:, k, :], start=(k == 0), stop=(k == 7)), reads=[b_w, b_hb], writes=[b_pg])
            pgv = pg[:, 0:64].rearrange("p (a b) -> p a b", b=16)
            S.op("dve", lambda e: e.tensor_tensor(out=gz[:, :, 0:8], in0=pgv[:, :, 0:8], in1=abrow[:, 8:16].unsqueeze(1).to_broadcast([128, 4, 8]), op=ALU.add), reads=[b_pg, b_ab], writes=[b_gz])
            S.op("act", lambda e: e.activation(out=gz[:, :, 0:8], in_=gz[:, :, 0:8], func=AF.Exp), reads=[b_gz], writes=[b_gz])
            S.op("act", lambda e: e.activation(out=gz[:, :, 0:8], in_=gz[:, :, 0:8], func=AF.Ln, bias=1.0), reads=[b_gz], writes=[b_gz])
            S.op("dve", lambda e, j=j: e.tensor_tensor(out=C.gates[:, 4 * j:4 * j + 4, 0:8], in0=gz[:, :, 0:8], in1=nA[:].unsqueeze(1).to_broadcast([128, 4, 8]), op=ALU.mult), reads=[b_gz, b_nA], writes=[b_gates])
            S.op("act", lambda e: e.activation(out=gz[:, :, 8:16], in_=pgv[:, :, 8:16], func=AF.Exp, scale=-1.0), reads=[b_pg], writes=[b_gz])
            S.op("dve", lambda e: e.tensor_scalar(gz[:, :, 8:16], gz[:, :, 8:16], 1.0, None, ALU.add), reads=[b_gz], writes=[b_gz])
            S.op("dve", lambda e, j=j: e.reciprocal(C.gates[:, 4 * j:4 * j + 4, 8:16], gz[:, :, 8:16]), reads=[b_gz], writes=[b_gates])
            def l2_stage1(c):
                sq, b_sq = sqs.next()
                pss, b_pss = psss.next()
                S.op("act", lambda e: e.activation(out=sq[:], in_=yqk[:, c, :], func=AF.Square), reads=[b_yqk[c]], writes=[b_sq])
                S.op("pe", lambda e: e.matmul(pss[:], C.ones, sq[:], start=True, stop=True), reads=[b_sq], writes=[b_pss])
                return pss, b_pss

            def l2_stage2(c, pss, b_pss):
                lnb, b_ln = lnbs.next()
                rr, b_rr = rrs.next()
                S.op("act", lambda e: e.activation(out=lnb[:], in_=pss[:], func=AF.Ln, bias=EPS), reads=[b_pss], writes=[b_ln])
                bias = float(np.log(128.0 ** -0.5)) if c < 8 else 0.0
                S.op("act", lambda e: e.activation(out=rr[:], in_=lnb[:], func=AF.Exp, scale=-0.5, bias=bias), reads=[b_ln], writes=[b_rr])
                S.op("dve", lambda e: e.tensor_tensor(out=qkT[:, c, :], in0=yqk[:, c, :], in1=rr[:], op=ALU.mult), reads=[b_yqk[c], b_rr], writes=[b_qkT])

            prev = None
            for c in range(16):
                cur = (c,) + l2_stage1(c)
                if prev is not None:
                    l2_stage2(*prev)
                prev = cur
            l2_stage2(*prev)
            S.dma("sp", "st0", lambda e, bs=bs: e.dma_start(out=C.qk_d[:, :, bs].rearrange("c p t -> p c t"), in_=qkT[:]), reads=[b_qkT], writes=[S.buf()])
            S.dma("sp", "st1", lambda e, bs=bs: e.dma_start(out=C.v_d[:, :, bs].rearrange("c p t -> p c t"), in_=yv[:]), reads=[b_yv], writes=[S.buf()])
            S.dma("sp", "st2", lambda e, bs=bs: e.dma_start(out=C.sz_d[:, :, bs].rearrange("c p t -> p c t"), in_=szb[:]), reads=[b_sz], writes=[S.buf()])
        if "gates" in C.dbg:
            S.dma("sp", "st3", lambda e: e.dma_start(out=C.dbg["gates"], in_=C.gates[:].rearrange("p a b -> p (a b)")), reads=[b_gates])
        S.flush()


def dn_rule_phase(C):
    nc, S, sb, ps = C.nc, C.S, C.sb, C.ps
    NB = C.NB
    U = C.cst[:, 3, :]
    Ms = C.cst[:, 4, :]
    Fm = C.cst[:, 5, :]
    idt = C.idt
    stage = getattr(C, 'rule_stage', 99)

    def bc_h(ap2):
        return ap2.unsqueeze(1).to_broadcast([128, 8, 128])

    def bc_c(ap2):
        return ap2.unsqueeze(2).to_broadcast([128, 8, 128])

    with ExitStack() as st:
        def t3(name, dt=F32, n=1):
            return Rot(S, [sb("%s%d" % (name, i), [128, 8, 128], dt, st) for i in range(n)])

        inb = Rot(S, [sb("inb%d" % i, [128, 32, 256], BF16, st) for i in range(2)])
        outb = Rot(S, [sb("outb%d" % i, [128, 8, 512], BF16, st) for i in range(2)])
        dnw = sb("dnw", [128, 1], F32, st)
        Sst = sb("Sst", [128, 8, 128], F32, st)
        Sbf = sb("Sbf", [128, 8, 128], BF16, st)
        b_S, b_Sbf, b_dnw = S.buf(), S.buf(), S.buf()
        GU = t3("GU")
        gsm = Rot(S, [sb("gsm%d" % i, [128, 48], F32, st) for i in range(2)])
        d1s, d2s = t3("d1"), t3("d2")
        E1s, E2s, EgRs = t3("E1"), t3("E2"), t3("EgR", F32, 2)
        EMs, DTs = t3("EM"), t3("DT")
        intraTs = t3("intraT", BF16, 2)
        Ms_ = t3("M", F32, 2)
        Mts = t3("Mt", F32, 2)
        Tts = t3("Tt", F32, 2)
        Ttbs = t3("Ttb", BF16)
        kbgs, kdecs, vbs = t3("kbg", BF16), t3("kdec", BF16, 2), t3("vb", BF16)
        us = t3("u", F32, 2)
        wTs = t3("wT", BF16, 2)
        qeTs = t3("qeT", BF16, 2)
        vnews = t3("vnew", BF16, 2)
        oraws = t3("oraw", F32)
        sqo = t3("sqo", F32)
        lno = t3("lno", F32)
        ro = t3("ro", F32)
        o1s = t3("o1", F32)
        banks = [ps("pb%d" % i, [128, 1024], F32, st) for i in range(4)]
        slotsAB = Rot(S, banks[0:2], psum=True)
        po_bank, b_po_bank = banks[2], S.buf(psum=True)
        slotsC = Rot(S, [banks[3][:, 0:512], banks[3][:, 512:1024]], psum=True)

        def slot3():
            p, b = slotsAB.next()
            return p[:].rearrange("p (a b) -> p a b", b=128), p, b

        def slotc():
            p, b = slotsC.next()
            return p.rearrange("p (a b) -> p a b", b=128), b

        S.dma("sp", "ld1", lambda e: e.dma_start(out=dnw[:], in_=C.dnw_pk), writes=[b_dnw])
        S.op("pool", lambda e: e.memset(Sst[:], 0.0), writes=[b_S])
        S.op("pool", lambda e: e.memset(Sbf[:], 0.0), writes=[b_Sbf])
        b_pers = S.buf()
        state = {"in": None, "out": None}
        handoff = {}

        def gen_AB(t):
            j, tt = t // 4, t % 4
            ic = slice((t % 2) * 128, (t % 2 + 1) * 128)
            if t % 2 == 0:
                state["in"] = inb.next()
                ib, b_ib = state["in"]
                hs_ = slice(t * 128, (t + 2) * 128)
                S.dma("sp", "il0%d" % ((t // 2) % 2), lambda e: e.dma_start(out=ib[:, 0:16, :], in_=C.qk_d[:, :, hs_].rearrange("c p t -> p c t")), writes=[b_ib])
                S.dma("sp", "il1%d" % ((t // 2) % 2), lambda e: e.dma_start(out=ib[:, 16:24, :], in_=C.v_d[:, :, hs_].rearrange("c p t -> p c t")), writes=[b_ib])
                S.dma("sp", "il2%d" % ((t // 2) % 2), lambda e: e.dma_start(out=ib[:, 24:32, :], in_=C.sz_d[:, :, hs_].rearrange("c p t -> p c t")), writes=[b_ib])
            if tt == 0:
                state["out"] = outb.next()
            ib, b_ib = state["in"]
            ob, b_ob = state["out"]
            qT = ib[:, 0:8, ic]
            kT = ib[:, 8:16, ic]
            vT = ib[:, 16:24, ic]
            szT = ib[:, 24:32, ic]
            g = C.gates[:, t, 0:8]
            beta = C.gates[:, t, 8:16]
            H = {"szT": szT, "b_ib": b_ib, "ob": ob, "b_ob": b_ob, "t": t}
            handoff[t] = H
            gu, b_gu = GU.next()
            S.op("dve", lambda e: e.tensor_tensor(out=gu[:], in0=bc_h(U), in1=bc_c(g), op=ALU.mult), reads=[b_pers], writes=[b_gu])
            yield
            gcR, gcRf, b_gcR = slot3()
            for hg in range(2):
                S.op("pe", lambda e, hg=hg: e.matmul(gcRf[:, hg * 512:(hg + 1) * 512], C.ones, gu[:, hg * 4:(hg + 1) * 4, :].rearrange("p a b -> p (a b)"), start=True, stop=True),
                     reads=[b_gu], writes=[b_gcR])
            yield
            gs, b_gs = gsm.next()
            gc = gs[:, 0:8]
            d1, b_d1 = d1s.next()
            d2, b_d2 = d2s.next()
            E1, b_E1 = E1s.next()
            E2, b_E2 = E2s.next()
            EgR, b_EgR = EgRs.next()
            H["EgR"], H["b_EgR"] = EgR, b_EgR
            pgc, pgcf, b_pgc = slot3()
            S.op("pe", lambda e: e.matmul(pgcf[:, 0:8], U, g, start=True, stop=True), reads=[b_pers], writes=[b_pgc])
            S.op("pe", lambda e: e.matmul(pgcf[:, 8:16], Fm, g, start=True, stop=True), reads=[b_pers], writes=[b_pgc])
            S.op("dve", lambda e: e.tensor_copy(gs[:, 0:16], pgcf[:, 0:16]), reads=[b_pgc], writes=[b_gs])
            yield
            S.op("dve", lambda e: e.scalar_tensor_tensor(out=d1[:], in0=gcR, scalar=-1.0, in1=bc_c(gc), op0=ALU.mult, op1=ALU.add), reads=[b_gcR, b_gs], writes=[b_d1])
            yield
            S.op("dve", lambda e: e.scalar_tensor_tensor(out=d2[:], in0=gcR, scalar=1.0, in1=bc_c(gc), op0=ALU.mult, op1=ALU.subtract), reads=[b_gcR, b_gs], writes=[b_d2])
            S.op("act", lambda e: e.activation(out=EgR[:], in_=gcR, func=AF.Exp), reads=[b_gcR], writes=[b_EgR])
            yield
            S.op("pool", lambda e: e.tensor_scalar(d1[:], d1[:], 0.0, -3.0e38, ALU.min, ALU.max), reads=[b_d1], writes=[b_d1])
            S.op("pool", lambda e: e.tensor_scalar(d2[:], d2[:], 0.0, -3.0e38, ALU.min, ALU.max), reads=[b_d2], writes=[b_d2])
            S.op("act", lambda e: e.activation(out=E1[:], in_=d1[:], func=AF.Exp), reads=[b_d1], writes=[b_E1])
            yield
            S.op("act", lambda e: e.activation(out=E2[:], in_=d2[:], func=AF.Exp), reads=[b_d2], writes=[b_E2])
            S.op("dve", lambda e: e.tensor_tensor(out=gs[:, 24:32], in0=gs[:, 8:16], in1=gs[:, 0:8], op=ALU.subtract), reads=[b_gs], writes=[b_gs])
            S.op("act", lambda e: e.activation(out=gs[:, 16:24], in_=gs[:, 0:8], func=AF.Exp), reads=[b_gs], writes=[b_gs])
            S.op("act", lambda e: e.activation(out=gs[:, 24:32], in_=gs[:, 24:32], func=AF.Exp), reads=[b_gs], writes=[b_gs])
            S.op("dve", lambda e: e.tensor_tensor(out=gs[:, 32:40], in0=gs[:, 16:24], in1=beta, op=ALU.mult), reads=[b_gs, b_pers], writes=[b_gs])
            S.op("dve", lambda e: e.tensor_scalar(gs[:, 40:48], beta, -1.0, None, ALU.mult), reads=[b_pers], writes=[b_gs])
            edl, bg, nbeta = gs[:, 24:32], gs[:, 32:40], gs[:, 40:48]
            yield
            if stage < 2:
                return
            KK, KKf, b_KK = slot3()
            for h in range(8):
                S.op("pe", lambda e, h=h: e.matmul(KK[:, h, :], kT[:, h, :], kT[:, h, :], start=True, stop=True), reads=[b_ib], writes=[b_KK])
            yield
            EM, b_EM = EMs.next()
            S.op("dve", lambda e: e.tensor_tensor(out=EM[:], in0=E1[:], in1=bc_h(Ms), op=ALU.mult), reads=[b_E1, b_pers], writes=[b_EM])
            yield
            M0, b_M0 = Ms_.next()
            for h in range(8):
                S.op("dve", lambda e, h=h: e.scalar_tensor_tensor(out=M0[:, h, :], in0=KK[:, h, :], scalar=nbeta[:, h:h + 1], in1=EM[:, h, :], op0=ALU.mult, op1=ALU.mult),
                     reads=[b_KK, b_EM, b_gs], writes=[b_M0])
                if h % 2 == 1:
                    yield
            QK, QKf, b_QK = slot3()
            for h in range(8):
                S.op("pe", lambda e, h=h: e.matmul(QK[:, h, :], kT[:, h, :], qT[:, h, :], start=True, stop=True), reads=[b_ib], writes=[b_QK])
            yield
            DT, b_DT = DTs.next()
            S.op("dve", lambda e: e.tensor_tensor(out=DT[:], in0=E2[:], in1=bc_h(U), op=ALU.mult), reads=[b_E2, b_pers], writes=[b_DT])
            yield
            intraT, b_intraT = intraTs.next()
            H["intraT"], H["b_intraT"] = intraT, b_intraT
            S.op("dve", lambda e: e.tensor_tensor(out=intraT[:], in0=QK, in1=DT[:], op=ALU.mult), reads=[b_QK, b_DT], writes=[b_intraT])
            yield
            if stage < 3:
                return
            pk, pkf, b_pk = slot3()
            pkb = pkf[:, 0:512].bitcast(BF16).rearrange("p (a b) -> p a b", b=128)
            for h in range(8):
                S.op("pe", lambda e, h=h: e.transpose(pkb[:, h, :], kT[:, h, :], C.idtb[:]), reads=[b_ib, b_pers], writes=[b_pk])
            yield
            kbg, b_kbg = kbgs.next()
            kdec, b_kdec = kdecs.next()
            H["kdec"], H["b_kdec"] = kdec, b_kdec
            S.op("dve", lambda e: e.tensor_tensor(out=kbg[:], in0=pkb, in1=bc_c(bg), op=ALU.mult), reads=[b_pk, b_gs], writes=[b_kbg])
            yield
            S.op("dve", lambda e: e.tensor_tensor(out=kdec[:], in0=pkb, in1=bc_c(edl), op=ALU.mult), reads=[b_pk, b_gs], writes=[b_kdec])
            yield
            pv, pvf, b_pv = slot3()
            pvb = pvf[:, 0:512].bitcast(BF16).rearrange("p (a b) -> p a b", b=128)
            for h in range(8):
                S.op("pe", lambda e, h=h: e.transpose(pvb[:, h, :], vT[:, h, :], C.idtb[:]), reads=[b_ib, b_pers], writes=[b_pv])
            yield
            vb, b_vb = vbs.next()
            S.op("dve", lambda e: e.tensor_tensor(out=vb[:], in0=pvb, in1=bc_c(beta), op=ALU.mult), reads=[b_pv, b_pers], writes=[b_vb])
            yield
            qeT, b_qeT = qeTs.next()
            H["qeT"], H["b_qeT"] = qeT, b_qeT
            S.op("dve", lambda e: e.tensor_tensor(out=qeT[:], in0=qT, in1=EgR[:], op=ALU.mult), reads=[b_ib, b_EgR], writes=[b_qeT])
            yield
            pBt, pBtf, b_pBt = slot3()
            for h in range(8):
                S.op("pe", lambda e, h=h: e.transpose(pBt[:, h, :], M0[:, h, :], idt), reads=[b_M0, b_pers], writes=[b_pBt])
            yield
            Mt0, b_Mt0 = Mts.next()
            Tt, b_Tt = Tts.next()
            S.op("act", lambda e: e.activation(out=Mt0[:], in_=pBt, func=AF.Copy), reads=[b_pBt], writes=[b_Mt0])
            S.op("dve", lambda e, Tt=Tt: e.tensor_tensor(out=Tt[:], in0=pBt, in1=bc_h(idt), op=ALU.add), reads=[b_pBt, b_pers], writes=[b_Tt])
            yield
            Mc, b_Mc, Mtc, b_Mtc = M0, b_M0, Mt0, b_Mt0
            for lvl in range(1, 6):
                pM, pMf, b_pM = slot3()
                for h in range(8):
                    S.op("pe", lambda e, h=h, pM=pM, Mc=Mc, Mtc=Mtc: e.matmul(pM[:, h, :], Mtc[:, h, :], Mc[:, h, :], start=True, stop=True), reads=[b_Mc, b_Mtc], writes=[b_pM])
                    if h % 2 == 1:
                        yield
                Mn, b_Mn = Ms_.next()
                S.op("act", lambda e, Mn=Mn, pM=pM: e.activation(out=Mn[:], in_=pM, func=AF.Copy), reads=[b_pM], writes=[b_Mn])
                if lvl < 5:
                    pMt, pMtf, b_pMt = slot3()
                    for h in range(8):
                        S.op("pe", lambda e, h=h, pMt=pMt, Mc=Mc, Mtc=Mtc: e.matmul(pMt[:, h, :], Mc[:, h, :], Mtc[:, h, :], start=True, stop=True), reads=[b_Mc, b_Mtc], writes=[b_pMt])
                        if h % 2 == 1:
                            yield
                    Mtn, b_Mtn = Mts.next()
                    S.op("act", lambda e, Mtn=Mtn, pMt=pMt: e.activation(out=Mtn[:], in_=pMt, func=AF.Copy), reads=[b_pMt], writes=[b_Mtn])
                pT, pTf, b_pT = slot3()
                for h in range(8):
                    S.op("pe", lambda e, h=h, pT=pT, Mn=Mn, Tt=Tt: e.matmul(pT[:, h, :], Mn[:, h, :], Tt[:, h, :], start=True, stop=True), reads=[b_Mn, b_Tt], writes=[b_pT])
                    if h % 2 == 1:
                        yield
                Ttn, b_Ttn = Tts.next()
                S.op("dve", lambda e, Ttn=Ttn, pT=pT, Tt=Tt: e.tensor_tensor(out=Ttn[:], in0=pT, in1=Tt[:], op=ALU.add), reads=[b_pT, b_Tt], writes=[b_Ttn])
                yield
                Tt, b_Tt = Ttn, b_Ttn
                Mc, b_Mc = Mn, b_Mn
                if lvl < 5:
                    Mtc, b_Mtc = Mtn, b_Mtn
            Ttb, b_Ttb = Ttbs.next()
            S.op("act", lambda e, Tt=Tt: e.activation(out=Ttb[:], in_=Tt[:], func=AF.Copy), reads=[b_Tt], writes=[b_Ttb])
            yield
            if stage < 4:
                return
            pu, puf, b_pu = slot3()
            for h in range(8):
                S.op("pe", lambda e, h=h: e.matmul(pu[:, h, :], Ttb[:, h, :], vb[:, h, :], start=True, stop=True), reads=[b_Ttb, b_vb], writes=[b_pu])
            yield
            u, b_u = us.next()
            H["u"], H["b_u"] = u, b_u
            S.op("act", lambda e: e.activation(out=u[:], in_=pu, func=AF.Copy), reads=[b_pu], writes=[b_u])
            pw, pwf, b_pw = slot3()
            for h in range(8):
                S.op("pe", lambda e, h=h: e.matmul(pw[:, h, :], kbg[:, h, :], Ttb[:, h, :], start=True, stop=True), reads=[b_kbg, b_Ttb], writes=[b_pw])
            yield
            wT, b_wT = wTs.next()
            H["wT"], H["b_wT"] = wT, b_wT
            S.op("act", lambda e: e.activation(out=wT[:], in_=pw, func=AF.Copy), reads=[b_pw], writes=[b_wT])
            H["done"] = True
            yield

        def gen_C(t):
            H = handoff.pop(t)
            if not H.get("done") or stage < 5:
                return
            j, tt = t // 4, t % 4
            tc = slice(tt * 128, (tt + 1) * 128)
            EgR, b_EgR = H["EgR"], H["b_EgR"]
            intraT, b_intraT = H["intraT"], H["b_intraT"]
            qeT, b_qeT = H["qeT"], H["b_qeT"]
            kdec, b_kdec = H["kdec"], H["b_kdec"]
            u, b_u, wT, b_wT = H["u"], H["b_u"], H["wT"], H["b_wT"]
            szT, b_ib, ob, b_ob = H["szT"], H["b_ib"], H["ob"], H["b_ob"]
            po, b_po = po_bank[:].rearrange("p (a b) -> p a b", b=128), b_po_bank
            def gen_chunk(ci):
                pr = slice(ci * 64, (ci + 1) * 64)
                cc = slice(ci * 64, (ci + 1) * 64)
                lc = ci * 64 + 63
                vnew, b_vnew = vnews.next()
                for hg in range(2):
                    pa, b_pa = slotc()
                    for h4 in range(4):
                        h = hg * 4 + h4
                        S.op("pe", lambda e, h=h, h4=h4, pa=pa: e.matmul(pa[pr, h4, :], wT[:, h, cc], Sbf[:, h, :], start=True, stop=True), reads=[b_wT, b_Sbf], writes=[b_pa])
                    yield
                    S.op("dve", lambda e, hg=hg, pa=pa: e.tensor_tensor(out=vnew[pr, hg * 4:(hg + 1) * 4, :], in0=u[pr, hg * 4:(hg + 1) * 4, :], in1=pa[pr, :, :], op=ALU.subtract), reads=[b_u, b_pa], writes=[b_vnew])
                    yield
                for h in range(8):
                    S.op("pe", lambda e, h=h: e.matmul(po[:, h, cc], Sbf[:, h, :], qeT[:, h, cc], start=True, stop=False), reads=[b_Sbf, b_qeT], writes=[b_po])
                    S.op("pe", lambda e, h=h, vnew=vnew: e.matmul(po[:, h, cc], vnew[pr, h, :], intraT[pr, h, cc], start=False, stop=True), reads=[b_vnew, b_intraT], writes=[b_po])
                    if h % 2 == 1:
                        yield
                for hg in range(2):
                    pS, b_pS = slotc()
                    for h4 in range(4):
                        h = hg * 4 + h4
                        S.op("pe", lambda e, h=h, h4=h4, pS=pS, vnew=vnew: e.matmul(pS[:, h4, :], kdec[pr, h, :], vnew[pr, h, :], start=True, stop=True), reads=[b_kdec, b_vnew], writes=[b_pS])
                    yield
                    for h4 in range(4):
                        h = hg * 4 + h4
                        S.op("dve", lambda e, h=h, h4=h4, pS=pS: e.scalar_tensor_tensor(out=Sst[:, h, :], in0=Sst[:, h, :], scalar=EgR[:, h, lc:lc + 1], in1=pS[:, h4, :], op0=ALU.mult, op1=ALU.add),
                             reads=[b_S, b_EgR, b_pS], writes=[b_S])
                    yield
                S.op("act", lambda e: e.activation(out=Sbf[:], in_=Sst[:], func=AF.Copy), reads=[b_S], writes=[b_Sbf])
                yield

            for ci in range(2):
                yield from gen_chunk(ci)
            oraw, b_oraw = oraws.next()
            S.op("act", lambda e: e.activation(out=oraw[:], in_=po, func=AF.Copy), reads=[b_po], writes=[b_oraw])
            yield
            if stage < 6:
                return
            sq, b_sq = sqo.next()
            S.op("act", lambda e: e.activation(out=sq[:], in_=oraw[:], func=AF.Square), reads=[b_oraw], writes=[b_sq])
            yield
            pq, b_pq = slotc()
            pq2, b_pq2 = slotc()
            S.op("pe", lambda e: e.matmul(pq.rearrange("p a b -> p (a b)"), C.ones, sq[:, 0:4, :].rearrange("p a b -> p (a b)"), start=True, stop=True), reads=[b_sq], writes=[b_pq])
            S.op("pe", lambda e: e.matmul(pq2.rearrange("p a b -> p (a b)"), C.ones, sq[:, 4:8, :].rearrange("p a b -> p (a b)"), start=True, stop=True), reads=[b_sq], writes=[b_pq2])
            yield
            ln_, b_ln = lno.next()
            r_, b_r = ro.next()
            S.op("act", lambda e: e.activation(out=ln_[:, 0:4, :], in_=pq, func=AF.Ln, scale=1.0 / 128, bias=EPS), reads=[b_pq], writes=[b_ln])
            S.op("act", lambda e: e.activation(out=ln_[:, 4:8, :], in_=pq2, func=AF.Ln, scale=1.0 / 128, bias=EPS), reads=[b_pq2], writes=[b_ln])
            yield
            S.op("act", lambda e: e.activation(out=r_[:], in_=ln_[:], func=AF.Exp, scale=-0.5), reads=[b_ln], writes=[b_r])
            yield
            o1, b_o1 = o1s.next()
            S.op("dve", lambda e: e.scalar_tensor_tensor(out=o1[:], in0=oraw[:], scalar=dnw[:, 0:1], in1=r_[:], op0=ALU.mult, op1=ALU.mult), reads=[b_oraw, b_r, b_dnw], writes=[b_o1])
            yield
            S.op("dve", lambda e: e.tensor_tensor(out=ob[:, :, tc], in0=o1[:], in1=szT, op=ALU.mult), reads=[b_o1, b_ib], writes=[b_ob])
            if tt == 3:
                bs = slice(j * 512, (j + 1) * 512)
                S.dma("sp", "os%d" % (j % 2), lambda e: e.dma_start(out=C.oTa_d[:, :, bs].rearrange("h p t -> p h t"), in_=ob[:]), reads=[b_ob], writes=[S.buf()])
            yield

        def run_all(gens):
            gens = list(gens)
            while gens:
                for g_ in list(gens):
                    try:
                        next(g_)
                    except StopIteration:
                        gens.remove(g_)

        run_all([gen_AB(0)])
        for t in range(1, C.NT):
            run_all([gen_AB(t), gen_C(t - 1)])
        run_all([gen_C(C.NT - 1)])
        S.flush()


def outproj_phase(C):
    nc, S, sb, ps = C.nc, C.S, C.sb, C.ps
    NB = C.NB
    C._hl = 0
    with ExitStack() as st:
        wg = sb("wg", [128, 8, 2048], BF16, st)
        wod = sb("wod", [128, 8, 1024], BF16, st)
        wom = sb("wom", [128, 8, 1024], BF16, st)
        wo = sb("wo", [128, 8, 1024], BF16, st)
        tmps = Rot(S, [sb("tmp%d" % i, [128, 512], F32, st) for i in range(2)])
        hbs = Rot(S, [sb("hb%d" % i, [128, 8, 512], BF16, st) for i in range(2)])
        oas = Rot(S, [sb("oa%d" % i, [128, 8, 512], BF16, st) for i in range(2)])
        obks = Rot(S, [sb("obk%d" % i, [128, 8, 512], BF16, st) for i in range(2)])
        mixs = Rot(S, [sb("mix%d" % i, [128, 8, 512], BF16, st) for i in range(2)])
        xts = Rot(S, [sb("xt%d" % i, [128, 1024], F32, st) for i in range(3)])
        sg = Rot(S, [sb("sg%d" % i, [128, 512], F32, st) for i in range(4)])
        mm_ = Rot(S, [sb("mm%d" % i, [128, 512], F32, st) for i in range(4)])
        banks = Rot(S, [ps("pb%d" % i, [128, 512], F32, st) for i in range(8)], psum=True)
        b_wg, b_wod, b_wom, b_wo, b_pers, b_x = [S.buf() for _ in range(6)]
        w_in_v = C.w_in.rearrange("(k p) n -> p k n", p=128)
        S.dma("pool", "wl0", lambda e: e.dma_start(out=wg[:, :, 0:1024], in_=w_in_v[:, :, C_GD:C_GD + 1024]), writes=[b_wg])
        S.dma("pool", "wl1", lambda e: e.dma_start(out=wg[:, :, 1024:2048], in_=w_in_v[:, :, C_GM:C_GM + 1024]), writes=[b_wg])
        S.dma("pool", "wl2", lambda e: e.dma_start(out=wod[:], in_=C.w_out_dn.rearrange("(h p) n -> p h n", p=128)), writes=[b_wod])
        S.dma("pool", "wl3", lambda e: e.dma_start(out=wom[:], in_=C.w_out_mla.rearrange("(h p) n -> p h n", p=128)), writes=[b_wom])
        S.dma("pool", "wl4", lambda e: e.dma_start(out=wo[:], in_=C.w_o.rearrange("(k p) n -> p k n", p=128)), writes=[b_wo])
        def outp_block(j, hbp, oap, obkp):
            hb, b_hb = hbp
            oa, b_oa = oap
            obk, b_obk = obkp
            bs = slice(j * 512, (j + 1) * 512)
            load_hT_block(C, hb, b_hb, j)
            S.dma("sp", "al0%d" % (j % 2), lambda e, bs=bs: e.dma_start(out=oa[:], in_=C.oTa_d[:, :, bs].rearrange("h p t -> p h t")), writes=[b_oa])
            S.dma("sp", "al1%d" % (j % 2), lambda e, bs=bs: e.dma_start(out=obk[:], in_=C.oTb_d[:, :, bs].rearrange("h p t -> p h t")), writes=[b_obk])
            mix, b_mix = mixs.next()
            for c in range(8):
                cs_ = slice(c * 128, (c + 1) * 128)
                p1, b_p1 = banks.next()
                for k in range(8):
                    S.op("pe", lambda e, p1=p1, k=k, cs_=cs_: e.matmul(p1[:], wg[:, k, cs_], hb[:, k, :], start=(k == 0), stop=(k == 7)), reads=[b_wg, b_hb], writes=[b_p1])
                p2, b_p2 = banks.next()
                for k in range(8):
                    S.op("pe", lambda e, p2=p2, k=k, c=c: e.matmul(p2[:], wg[:, k, 1024 + c * 128:1024 + (c + 1) * 128], hb[:, k, :], start=(k == 0), stop=(k == 7)), reads=[b_wg, b_hb], writes=[b_p2])
                p3, b_p3 = banks.next()
                for h in range(8):
                    S.op("pe", lambda e, p3=p3, h=h, cs_=cs_: e.matmul(p3[:], wod[:, h, cs_], oa[:, h, :], start=(h == 0), stop=(h == 7)), reads=[b_wod, b_oa], writes=[b_p3])
                p4, b_p4 = banks.next()
                for h in range(8):
                    S.op("pe", lambda e, p4=p4, h=h, cs_=cs_: e.matmul(p4[:], wom[:, h, cs_], obk[:, h, :], start=(h == 0), stop=(h == 7)), reads=[b_wom, b_obk], writes=[b_p4])
                s1, b_s1 = sg.next()
                s2, b_s2 = sg.next()
                S.op("act", lambda e, s1=s1, p1=p1: e.activation(out=s1[:], in_=p1[:], func=AF.Sigmoid), reads=[b_p1], writes=[b_s1])
                S.op("act", lambda e, s2=s2, p2=p2: e.activation(out=s2[:], in_=p2[:], func=AF.Sigmoid), reads=[b_p2], writes=[b_s2])
                m1, b_m1 = mm_.next()
                m2, b_m2 = mm_.next()
                S.op("dve", lambda e, m1=m1, s1=s1, p3=p3: e.tensor_tensor(out=m1[:], in0=s1[:], in1=p3[:], op=ALU.mult), reads=[b_s1, b_p3], writes=[b_m1])
                S.op("dve", lambda e, m2=m2, s2=s2, p4=p4: e.tensor_tensor(out=m2[:], in0=s2[:], in1=p4[:], op=ALU.mult), reads=[b_s2, b_p4], writes=[b_m2])
                S.op("pool", lambda e, mix=mix, m1=m1, m2=m2, c=c: e.tensor_tensor(out=mix[:, c, :], in0=m1[:], in1=m2[:], op=ALU.add), reads=[b_m1, b_m2], writes=[b_mix])
            for tt in range(4):
                t = j * 4 + tt
                xt, b_xt = xts.next()
                S.dma("sp", "xl%d" % (t % 3), lambda e, xt=xt, t=t: e.dma_start(out=xt[:], in_=C.x[t * 128:(t + 1) * 128, :]), reads=[b_x], writes=[b_xt])
                for hf in range(2):
                    pw, b_pw = banks.next()
                    for k in range(8):
                        S.op("pe", lambda e, pw=pw, mix=mix, k=k, tt=tt, hf=hf: e.matmul(pw[:], mix[:, k, tt * 128:(tt + 1) * 128], wo[:, k, hf * 512:(hf + 1) * 512], start=(k == 0), stop=(k == 7)), reads=[b_mix, b_wo], writes=[b_pw])
                    tmp, b_tmp = tmps.next()
                    S.op("dve", lambda e, tmp=tmp, pw=pw, hf=hf: e.tensor_tensor(out=tmp[:], in0=pw[:], in1=C.gateB[:, 0, hf * 512:(hf + 1) * 512], op=ALU.mult), reads=[b_pw, b_pers], writes=[b_tmp])
                    S.op("pool", lambda e, xt=xt, tmp=tmp, hf=hf: e.tensor_tensor(out=xt[:, hf * 512:(hf + 1) * 512], in0=xt[:, hf * 512:(hf + 1) * 512], in1=tmp[:], op=ALU.add), reads=[b_xt, b_tmp], writes=[b_xt])
                S.dma("sp", "xs%d" % (t % 3), lambda e, xt=xt, t=t: e.dma_start(out=C.out[t * 128:(t + 1) * 128, :], in_=xt[:]), reads=[b_xt], writes=[S.buf()])

        for j in range(NB):
            outp_block(j, hbs.next(), oas.next(), obks.next())
        S.flush()


def ffn_phase(C):
    nc, S, sb, ps = C.nc, C.S, C.sb, C.ps
    NB = C.NB
    NC_ = D_FF // 128
    C._hl = 0
    with ExitStack() as st:
        wup = sb("wup", [128, 8, 2 * D_FF], BF16, st)
        wdn = sb("wdnf", [128, NC_, 1024], BF16, st)
        tmps = Rot(S, [sb("tmp%d" % i, [128, 512], F32, st) for i in range(2)])
        fcw = sb("fcw", [128, NC_, 3], F32, st)
        fcb = sb("fcb", [128, NC_], F32, st)
        halo = sb("halo", [128, NC_, 2], F32, st)
        hb, b_hb = sb("hb", [128, 8, 512], BF16, st), S.buf()
        gT, b_gT = sb("gT", [128, NC_, 512], BF16, st), S.buf()
        xc, b_xc = sb("xc", [128, 514], F32, st), S.buf()
        accs = Rot(S, [sb("acc%d" % i, [128, 512], F32, st) for i in range(1)])
        gas = Rot(S, [sb("ga%d" % i, [128, 512], F32, st) for i in range(2)])
        xts = Rot(S, [sb("xt%d" % i, [128, 1024], F32, st) for i in range(2)])
        banks = Rot(S, [ps("pb%d" % i, [128, 512], F32, st) for i in range(8)], psum=True)
        b_wup, b_wdn, b_pers, b_c, b_x = [S.buf() for _ in range(5)]
        b_halo = [S.buf() for _ in range(NC_)]
        wup_v = C.w_up.rearrange("(k p) n -> p k n", p=128)
        for q in range(11):
            S.dma("pool", "wl%d" % (q % 2), lambda e, q=q: e.dma_start(out=wup[:, :, q * 512:(q + 1) * 512], in_=wup_v[:, :, q * 512:(q + 1) * 512]), writes=[b_wup])
        S.dma("sp", "ld1", lambda e: e.dma_start(out=fcw[:], in_=C.fconv_pk), writes=[b_c])
        S.dma("sp", "ld2", lambda e: e.dma_start(out=fcb[:], in_=C.fconvb_pk), writes=[b_c])
        for q in range(2):
            S.dma("pool", "wl%d" % (2 + q), lambda e, q=q: e.dma_start(out=wdn[:, q * 11:(q + 1) * 11, :], in_=C.w_down[q * 1408:(q + 1) * 1408, :].rearrange("(c p) n -> p c n", p=128)), writes=[b_wdn])
        for c in range(NC_):
            S.op("pool", lambda e, c=c: e.memset(halo[:, c, :], 0.0), writes=[b_halo[c]])
        for j in range(NB):
            load_hT_block(C, hb, b_hb, j)
            for c in range(NC_):
                pa, b_pa = banks.next()
                for k in range(8):
                    S.op("pe", lambda e, pa=pa, k=k, c=c: e.matmul(pa[:], wup[:, k, c * 128:(c + 1) * 128], hb[:, k, :], start=(k == 0), stop=(k == 7)), reads=[b_wup, b_hb], writes=[b_pa])
                pv, b_pv = banks.next()
                for k in range(8):
                    S.op("pe", lambda e, pv=pv, k=k, c=c: e.matmul(pv[:], wup[:, k, D_FF + c * 128:D_FF + (c + 1) * 128], hb[:, k, :], start=(k == 0), stop=(k == 7)), reads=[b_wup, b_hb], writes=[b_pv])
                acc, b_acc = accs.next()
                S.op("act", lambda e, pa=pa: e.activation(out=xc[:, 2:514], in_=pa[:], func=AF.Copy), reads=[b_pa], writes=[b_xc])
                S.op("act", lambda e, acc=acc, pa=pa, c=c: e.activation(out=acc[:], in_=pa[:], func=AF.Identity, scale=fcw[:, c, 2:3], bias=fcb[:, c:c + 1]), reads=[b_pa, b_c], writes=[b_acc])
                S.op("pool", lambda e, c=c: e.tensor_copy(xc[:, 0:2], halo[:, c, :]), reads=[b_halo[c]], writes=[b_xc])
                S.op("pool", lambda e, c=c: e.tensor_copy(halo[:, c, :], xc[:, 512:514]), reads=[b_xc], writes=[b_halo[c]])
                for jj in (1, 0):
                    S.op("dve", lambda e, acc=acc, c=c, jj=jj: e.scalar_tensor_tensor(out=acc[:], in0=xc[:, jj:jj + 512], scalar=fcw[:, c, jj:jj + 1], in1=acc[:], op0=ALU.mult, op1=ALU.add),
                         reads=[b_xc, b_acc, b_c], writes=[b_acc])
                ga, b_ga = gas.next()
                S.op("act", lambda e, ga=ga, acc=acc: e.activation(out=ga[:], in_=acc[:], func=AF.Gelu), reads=[b_acc], writes=[b_ga])
                S.op("dve", lambda e, ga=ga, pv=pv, c=c: e.tensor_tensor(out=gT[:, c, :], in0=ga[:], in1=pv[:], op=ALU.mult), reads=[b_ga, b_pv], writes=[b_gT])
            for tt in range(4):
                t = j * 4 + tt
                xt, b_xt = xts.next()
                S.dma("sp", "xl%d" % (t % 2), lambda e, xt=xt, t=t: e.dma_start(out=xt[:], in_=C.out[t * 128:(t + 1) * 128, :]), reads=[b_x], writes=[b_xt])
                for hf in range(2):
                    pw, b_pw = banks.next()
                    for c in range(NC_):
                        S.op("pe", lambda e, pw=pw, c=c, tt=tt, hf=hf: e.matmul(pw[:], gT[:, c, tt * 128:(tt + 1) * 128], wdn[:, c, hf * 512:(hf + 1) * 512], start=(c == 0), stop=(c == NC_ - 1)), reads=[b_gT, b_wdn], writes=[b_pw])
                    tmp, b_tmp = tmps.next()
                    S.op("dve", lambda e, tmp=tmp, pw=pw, hf=hf: e.tensor_tensor(out=tmp[:], in0=pw[:], in1=C.gateB[:, 1, hf * 512:(hf + 1) * 512], op=ALU.mult), reads=[b_pw, b_pers], writes=[b_tmp])
                    S.op("pool", lambda e, xt=xt, tmp=tmp, hf=hf: e.tensor_tensor(out=xt[:, hf * 512:(hf + 1) * 512], in0=xt[:, hf * 512:(hf + 1) * 512], in1=tmp[:], op=ALU.add), reads=[b_xt, b_tmp], writes=[b_xt])
                S.dma("sp", "xs%d" % (t % 2), lambda e, xt=xt, t=t: e.dma_start(out=C.out[t * 128:(t + 1) * 128, :], in_=xt[:]), reads=[b_xt], writes=[S.buf()])
        S.flush()


def _consts():
    c = np.zeros((128, 6, 128), np.float32)
    c[:, 0, :] = np.eye(128, dtype=np.float32)
    c[:, 1, :] = 1.0
    i = np.arange(128)
    c[:, 2, :] = ((i[:, None] % 64) == (i[None, :] % 64)).astype(np.float32)
    same = (i[:, None] // 64) == (i[None, :] // 64)
    c[:, 3, :] = (same & (i[:, None] <= i[None, :])).astype(np.float32)
    c[:, 4, :] = (same & (i[:, None] > i[None, :])).astype(np.float32)
    c[:, 5, :] = same.astype(np.float32)
    return c


def weight_inputs(inp):
    f = np.float32
    d = {}
    d["w_ada"] = np.ascontiguousarray(inp["w_ada"][0], dtype=f)
    d["b_ada_pk"] = np.ascontiguousarray(inp["b_ada"][0].reshape(48, 128).T)
    d["b_ada"] = np.ascontiguousarray(inp["b_ada"].reshape(1, -1))
    d["norm1_pk"] = np.ascontiguousarray(inp["norm1_w"][0].reshape(8, 128).T)
    d["norm2_pk"] = np.ascontiguousarray(inp["norm2_w"][0].reshape(8, 128).T)
    d["consts"] = _consts()
    w_in = np.ascontiguousarray(inp["w_in"][0], dtype=f)
    d["w_in"] = w_in
    perm = np.concatenate([np.arange(32, 64), np.arange(0, 32)])
    cols = []
    for h in range(NH):
        cols.append(C_MQ + h * 192 + 128 + perm)
    cols.append(C_KR + perm)
    d["w_in_rot"] = np.ascontiguousarray(w_in[:, np.concatenate(cols)])
    qn = inp["mla_q_norm_w"][0]
    kn = inp["mla_k_norm_w"][0]
    kvn = inp["mla_kv_norm_w"][0]
    m = np.zeros((128, 8), f)
    m[:, 0] = qn[:128]
    m[:, 1] = np.concatenate([qn[128:192], qn[128:192][perm]])
    m[:, 2] = kn[:128]
    m[:, 3] = np.concatenate([kn[128:192], kn[128:192][perm]])
    m[:, 4] = np.concatenate([np.ones(64), -np.ones(32), np.ones(32)])
    inv = (10000.0 ** (-np.arange(32, dtype=np.float32) / 32)).astype(f)
    m[:, 5] = inv[(np.arange(128) % 64) % 32]
    m[:, 6] = kvn[:128]
    m[:, 7] = kvn[128:]
    d["mlapk"] = m
    d["w_uk"] = np.ascontiguousarray(inp["mla_w_uk"][0], dtype=f)
    d["w_uv"] = np.ascontiguousarray(inp["mla_w_uv"][0], dtype=f)
    d["conv_pk"] = np.ascontiguousarray(inp["dn_conv_w"][0].T.reshape(24, 128, 4).transpose(1, 0, 2))
    d["ab_row"] = np.ascontiguousarray(np.concatenate([inp["dn_a_log"][0], inp["dn_dt_bias"][0]])[None, :].astype(f))
    d["dnw_pk"] = np.ascontiguousarray(inp["dn_norm_w"][0].reshape(128, 1))
    d["w_out_dn"] = np.ascontiguousarray(inp["w_out_dn"][0], dtype=f)
    d["w_out_mla"] = np.ascontiguousarray(inp["w_out_mla"][0], dtype=f)
    d["w_o"] = np.ascontiguousarray(inp["w_o"][0], dtype=f)
    d["w_up"] = np.ascontiguousarray(inp["ffn_w_up"][0], dtype=f)
    d["w_down"] = np.ascontiguousarray(inp["ffn_w_down"][0], dtype=f)
    d["fconv_pk"] = np.ascontiguousarray(inp["ffn_conv_w"][0].T.reshape(22, 128, 3).transpose(1, 0, 2))
    d["fconvb_pk"] = np.ascontiguousarray(inp["ffn_conv_b"][0].reshape(22, 128).T)
    return d


def host_inputs(inp, b):
    d = {}
    d["x"] = np.ascontiguousarray(inp["x"][b])
    d["cT"] = np.ascontiguousarray(inp["c"][b].reshape(8, 128).T)
    d["pos"] = np.ascontiguousarray(inp["positions"][b][None, :].astype(np.int32))
    return d


_NC_CACHE = {}


def kernel(**inputs):
    inp = {k: np.asarray(v) for k, v in inputs.items()}
    B, S_, _ = inp["x"].shape
    NB = S_ // 512
    if NB not in _NC_CACHE:
        _NC_CACHE[NB] = build(NB=NB)
    nc = _NC_CACHE[NB]
    wi = weight_inputs(inp)
    in_maps = [dict(wi, **host_inputs(inp, b)) for b in range(B)]
    res = run_bass_kernel_spmd(nc, in_maps, core_ids=list(range(B)))
    return np.stack([np.asarray(res.results[b]["out"]) for b in range(B)], axis=0).astype(np.float32)
```
